# Optimizing a Trainium2 kernel written in Bass

```python
import jax, jax.numpy as jnp
from jax import lax
import numpy as np

D_MODEL = 2048
BATCH = 4
SEQ = 4096
DEPTH = 4

CHUNK = 64
N_META = 16
N_MIXERS = 2
N_MLA = (DEPTH + N_MIXERS - 1) // N_MIXERS
N_LRU = DEPTH // N_MIXERS

MLA_HEADS = 16
Q_LORA = 512
KV_LORA = 512
QK_NOPE = 128
QK_ROPE = 64
V_HEAD = 128
ROPE_THETA = 10000.0
Q_BLOCK = 128

D_RNN = D_MODEL
RNN_BLOCKS = 16
RNN_BW = D_RNN // RNN_BLOCKS
CONV_W = 4
LRU_C = 8.0

D_FF = -(-8 * D_MODEL // (3 * 256)) * 256

RMS_EPS = 1e-6
NEG_BIG = -1e30

kernel_name = 'hybrid_mla_rglru_streaming_trunk'


def rms_norm(x, g):
    xf = x.astype(jnp.float32)
    y = xf * lax.rsqrt(jnp.mean(xf * xf, axis=-1, keepdims=True) + RMS_EPS)
    return (y * g.astype(jnp.float32)).astype(x.dtype)


def chunk_ids(n):
    pos = jnp.arange(n)
    return jnp.where(pos < N_META, 0, 1 + (pos - N_META) // CHUNK)


def apply_rope(x, cos, sin):
    xf = x.astype(jnp.float32)
    x1, x2 = jnp.split(xf, 2, axis=-1)
    out = jnp.concatenate([x1 * cos - x2 * sin, x2 * cos + x1 * sin], axis=-1)
    return out.astype(x.dtype)


def mla_mixer(h, w_in, q_norm, kv_norm, w_uq, w_ukv, w_o):
    B, T, _ = h.shape
    proj = h @ w_in
    c_q, c_kv, k_rope = jnp.split(proj, [Q_LORA, Q_LORA + KV_LORA], axis=-1)
    c_q = rms_norm(c_q, q_norm)
    c_kv = rms_norm(c_kv, kv_norm)
    q = (c_q @ w_uq).reshape(B, T, MLA_HEADS, QK_NOPE + QK_ROPE)
    q_nope, q_rope = jnp.split(q, [QK_NOPE], axis=-1)
    kv = (c_kv @ w_ukv).reshape(B, T, MLA_HEADS, QK_NOPE + V_HEAD)
    k_nope, v = jnp.split(kv, [QK_NOPE], axis=-1)

    pos = jnp.arange(T, dtype=jnp.float32)
    inv_freq = ROPE_THETA ** (-jnp.arange(0, QK_ROPE, 2, dtype=jnp.float32) / QK_ROPE)
    ang = pos[:, None] * inv_freq[None, :]
    cos, sin = jnp.cos(ang), jnp.sin(ang)
    q_rope = apply_rope(q_rope, cos[:, None, :], sin[:, None, :])
    k_rope = apply_rope(k_rope, cos, sin)

    scale = (QK_NOPE + QK_ROPE) ** -0.5
    n_blk = -(-T // Q_BLOCK)
    t_pad = n_blk * Q_BLOCK
    pad = ((0, 0), (0, t_pad - T), (0, 0), (0, 0))
    q_nope = jnp.pad(q_nope, pad)
    q_rope = jnp.pad(q_rope, pad)
    k_chunk = chunk_ids(T)
    q_chunk = chunk_ids(t_pad)

    def attend_block(i):
        s = i * Q_BLOCK
        qn = lax.dynamic_slice_in_dim(q_nope, s, Q_BLOCK, axis=1)
        qr = lax.dynamic_slice_in_dim(q_rope, s, Q_BLOCK, axis=1)
        qc = lax.dynamic_slice_in_dim(q_chunk, s, Q_BLOCK)
        scores = (jnp.einsum('bqhd,bkhd->bhqk', qn, k_nope)
                  + jnp.einsum('bqhd,bkd->bhqk', qr, k_rope)).astype(jnp.float32) * scale
        mask = k_chunk[None, :] <= qc[:, None]
        scores = jnp.where(mask[None, None], scores, NEG_BIG)
        p = jax.nn.softmax(scores, axis=-1).astype(v.dtype)
        return jnp.einsum('bhqk,bkhd->bqhd', p, v)

    out = lax.map(attend_block, jnp.arange(n_blk))
    out = jnp.moveaxis(out, 0, 1).reshape(B, t_pad, MLA_HEADS * V_HEAD)[:, :T]
    return out @ w_o


def rglru_mixer(h, w_in, conv_w, conv_b, w_ga, b_ga, w_gx, b_gx, lam, w_o):
    B, T, _ = h.shape
    xb, yb = jnp.split(h @ w_in, 2, axis=-1)
    yb = jax.nn.gelu(yb, approximate=True)
    xb = lax.conv_general_dilated(
        xb, conv_w[:, None, :], window_strides=(1,), padding=[(CONV_W - 1, 0)],
        dimension_numbers=('NWC', 'WIO', 'NWC'), feature_group_count=D_RNN) + conv_b
    xg = xb.reshape(B, T, RNN_BLOCKS, RNN_BW)
    r = jax.nn.sigmoid(jnp.einsum('btnc,ncd->btnd', xg, w_ga) + b_ga).reshape(B, T, D_RNN)
    i = jax.nn.sigmoid(jnp.einsum('btnc,ncd->btnd', xg, w_gx) + b_gx).reshape(B, T, D_RNN)
    log_a = -LRU_C * r.astype(jnp.float32) * jax.nn.softplus(-lam.astype(jnp.float32))
    a = jnp.exp(log_a)
    b = jnp.sqrt(-jnp.expm1(2.0 * log_a)) * (i * xb).astype(jnp.float32)

    def combine(lhs, rhs):
        a1, b1 = lhs
        a2, b2 = rhs
        return a1 * a2, a2 * b1 + b2

    _, hs = lax.associative_scan(combine, (a, b), axis=1)
    return (hs.astype(h.dtype) * yb) @ w_o


def swiglu(h, w_gu, w_down):
    g, u = jnp.split(h @ w_gu, 2, axis=-1)
    return (jax.nn.silu(g) * u) @ w_down


def setup_inputs(seed: int = 0) -> dict:
    key = jax.random.key(seed)
    ks = jax.random.split(key, 24)
    f32 = jnp.float32

    def dense(k, shape, fan_in):
        return jax.random.normal(k, shape, f32) * (fan_in ** -0.5)

    def gain(k, shape):
        return 1.0 + 0.02 * jax.random.normal(k, shape, f32)

    x = jax.random.normal(ks[0], (BATCH, SEQ, D_MODEL), f32)
    meta_tokens = jax.random.normal(ks[1], (N_META, D_MODEL), f32)
    norm_mix = gain(ks[2], (DEPTH, D_MODEL))
    norm_ffn = gain(ks[3], (DEPTH, D_MODEL))
    norm_final = gain(ks[4], (D_MODEL,))

    mla_w_in = dense(ks[5], (N_MLA, D_MODEL, Q_LORA + KV_LORA + QK_ROPE), D_MODEL)
    mla_q_norm = gain(ks[6], (N_MLA, Q_LORA))
    mla_kv_norm = gain(ks[7], (N_MLA, KV_LORA))
    mla_w_uq = dense(ks[8], (N_MLA, Q_LORA, MLA_HEADS * (QK_NOPE + QK_ROPE)), Q_LORA)
    mla_w_ukv = dense(ks[9], (N_MLA, KV_LORA, MLA_HEADS * (QK_NOPE + V_HEAD)), KV_LORA)
    mla_w_o = dense(ks[10], (N_MLA, MLA_HEADS * V_HEAD, D_MODEL), MLA_HEADS * V_HEAD)

    lru_w_in = dense(ks[11], (N_LRU, D_MODEL, 2 * D_RNN), D_MODEL)
    lru_conv_w = dense(ks[12], (N_LRU, CONV_W, D_RNN), CONV_W)
    lru_conv_b = 0.01 * jax.random.normal(ks[13], (N_LRU, D_RNN), f32)
    lru_w_gate_a = dense(ks[14], (N_LRU, RNN_BLOCKS, RNN_BW, RNN_BW), RNN_BW)
    lru_b_gate_a = 0.01 * jax.random.normal(ks[15], (N_LRU, RNN_BLOCKS, RNN_BW), f32)
    lru_w_gate_x = dense(ks[16], (N_LRU, RNN_BLOCKS, RNN_BW, RNN_BW), RNN_BW)
    lru_b_gate_x = 0.01 * jax.random.normal(ks[17], (N_LRU, RNN_BLOCKS, RNN_BW), f32)
    a_c = jax.random.uniform(ks[18], (N_LRU, D_RNN), f32, 0.9, 0.999)
    a0 = a_c ** (1.0 / LRU_C)
    lru_lambda = jnp.log(a0) - jnp.log1p(-a0)
    lru_w_o = dense(ks[19], (N_LRU, D_RNN, D_MODEL), D_RNN)

    ffn_w_gu = dense(ks[20], (DEPTH, D_MODEL, 2 * D_FF), D_MODEL)
    ffn_w_down = dense(ks[21], (DEPTH, D_FF, D_MODEL), D_FF)

    return {
        'x': x, 'meta_tokens': meta_tokens,
        'norm_mix': norm_mix, 'norm_ffn': norm_ffn, 'norm_final': norm_final,
        'mla_w_in': mla_w_in, 'mla_q_norm': mla_q_norm, 'mla_kv_norm': mla_kv_norm,
        'mla_w_uq': mla_w_uq, 'mla_w_ukv': mla_w_ukv, 'mla_w_o': mla_w_o,
        'lru_w_in': lru_w_in, 'lru_conv_w': lru_conv_w, 'lru_conv_b': lru_conv_b,
        'lru_w_gate_a': lru_w_gate_a, 'lru_b_gate_a': lru_b_gate_a,
        'lru_w_gate_x': lru_w_gate_x, 'lru_b_gate_x': lru_b_gate_x,
        'lru_lambda': lru_lambda, 'lru_w_o': lru_w_o,
        'ffn_w_gu': ffn_w_gu, 'ffn_w_down': ffn_w_down,
    }


def reference(x, meta_tokens, norm_mix, norm_ffn, norm_final,
              mla_w_in, mla_q_norm, mla_kv_norm, mla_w_uq, mla_w_ukv, mla_w_o,
              lru_w_in, lru_conv_w, lru_conv_b, lru_w_gate_a, lru_b_gate_a,
              lru_w_gate_x, lru_b_gate_x, lru_lambda, lru_w_o,
              ffn_w_gu, ffn_w_down):
    B = x.shape[0]
    meta = jnp.broadcast_to(meta_tokens.astype(x.dtype)[None], (B, N_META, D_MODEL))
    h = jnp.concatenate([meta, x], axis=1)
    for layer in range(DEPTH):
        j = layer // N_MIXERS
        hn = rms_norm(h, norm_mix[layer])
        if layer % N_MIXERS == 0:
            mix = mla_mixer(hn, mla_w_in[j], mla_q_norm[j], mla_kv_norm[j],
                            mla_w_uq[j], mla_w_ukv[j], mla_w_o[j])
        else:
            mix = rglru_mixer(hn, lru_w_in[j], lru_conv_w[j], lru_conv_b[j],
                              lru_w_gate_a[j], lru_b_gate_a[j],
                              lru_w_gate_x[j], lru_b_gate_x[j],
                              lru_lambda[j], lru_w_o[j])
        h = h + mix
        h = h + swiglu(rms_norm(h, norm_ffn[layer]), ffn_w_gu[layer], ffn_w_down[layer])
    h = rms_norm(h, norm_final)
    return h[:, N_META:]
```

```python
import numpy as np
from contextlib import ExitStack
import concourse.bass as bass
import concourse.mybir as mybir
from concourse.bass_utils import run_bass_kernel_spmd

F32 = mybir.dt.float32
BF16 = mybir.dt.bfloat16
AF = mybir.ActivationFunctionType
ALU = mybir.AluOpType

D = 2048
KC = 16
NMETA = 16
NFR = 2048
NT = NMETA + NFR
TILE = 688
SEG = 344
NTILES = 3
SEGS = [(0, SEG), (SEG, TILE)]
DFF = 5632
NJ = 44
GJ = 11
NG = 4
NH = 16
ASLOT = 691
NASLOT = 26
NS = 4
LA = 3
SCALE = 192.0 ** -0.5
EPS = 1e-6
FUSED = True
DEBUG = {}


class Buf:
    __slots__ = ("name", "writer", "readers")

    def __init__(self, name=""):
        self.name = name
        self.writer = None
        self.readers = []


class V:
    __slots__ = ("ap", "bufs")

    def __init__(self, ap, bufs):
        self.ap = ap
        self.bufs = bufs


class Op:
    __slots__ = ("eng", "fn", "deps", "dma", "coll", "signals", "token", "prev", "short")


def _flat(vs):
    out = []
    for v in vs:
        if isinstance(v, Buf):
            out.append(v)
        elif isinstance(v, V):
            out.extend(v.bufs)
        else:
            out.extend(_flat(v))
    return out


class Sched:
    ENGS = ("pe", "act", "dve", "pool", "sp")

    def __init__(self):
        self.q = {e: [] for e in self.ENGS}

    def add(self, eng, fn, R=(), W=(), dma=False, coll=False, short=False):
        op = Op()
        op.short = short
        op.eng = eng
        op.fn = fn
        op.dma = dma or coll
        op.coll = coll
        op.signals = False
        op.token = None
        op.prev = None
        reads = _flat(R)
        writes = _flat(W)
        deps = {}
        for b in reads:
            if b.writer is not None:
                deps[id(b.writer)] = b.writer
        for b in writes:
            if b.writer is not None:
                deps[id(b.writer)] = b.writer
            for r in b.readers:
                deps[id(r)] = r
        op.deps = [d for d in deps.values() if d is not op and (d.dma or op.dma or d.eng != eng or d.short or op.short)]
        for b in reads:
            if not op.dma:
                b.readers = [r for r in b.readers if r.dma or r.eng != eng or r.short]
            b.readers.append(op)
        for b in writes:
            b.writer = op
            b.readers = []
        self.q[eng].append(op)
        return op

    def emit(self, nc, es):
        CH = 4000
        RR = 8
        for e in self.ENGS:
            for op in self.q[e]:
                for d in op.deps:
                    d.signals = True
        sems = {}

        def getsem(key):
            if key not in sems:
                sems[key] = es.enter_context(nc.semaphore("s%d" % len(sems)))
            return sems[key]

        ncoll = 0
        for e in self.ENGS:
            ccnt = 0
            dcnt = 0
            for op in self.q[e]:
                if op.coll:
                    op.token = (getsem(("coll", ncoll)), 1)
                    ncoll += 1
                elif op.dma:
                    s = getsem(("d" + e, dcnt % RR))
                    op.token = (s, 16 * (dcnt // RR + 1))
                    if dcnt >= RR:
                        op.prev = (s, 16 * (dcnt // RR))
                    dcnt += 1
                elif op.signals:
                    op.token = (getsem((e, ccnt // CH)), ccnt % CH + 1)
                    ccnt += 1
        self.nsems = len(sems)
        q = self.q

        def run(e, eng):
            waited = {}
            for op in q[e]:
                need = [d.token for d in op.deps]
                if op.prev is not None:
                    need.append(op.prev)
                for (s, v) in need:
                    if waited.get(id(s), 0) < v:
                        eng.wait_ge(s, v)
                        waited[id(s)] = v
                ins = op.fn(eng)
                if op.token is not None:
                    if op.coll:
                        ins.then_inc(op.token[0])
                    elif op.dma:
                        ins.then_inc(op.token[0], 16)
                    else:
                        ins.then_inc(op.token[0], 1)
            for op in q[e]:
                if op.dma and waited.get(id(op.token[0]), 0) < op.token[1]:
                    eng.wait_ge(op.token[0], op.token[1])
                    waited[id(op.token[0])] = op.token[1]

        with nc.Block() as block:
            @block.tensor
            def _(t):
                run("pe", t)

            @block.scalar
            def _(s):
                run("act", s)

            @block.vector
            def _(v):
                run("dve", v)

            @block.gpsimd
            def _(g):
                run("pool", g)

            @block.sync
            def _(sy):
                run("sp", sy)


def build(layer_ids, final_norm, ncores=8):
    RGROUPS = [[2 * i, 2 * i + 1] for i in range(ncores // 2)]
    nc = bass.Bass("TRN2", target_bir_lowering=False)
    es = ExitStack()

    def din(name, shape, dt=F32):
        return nc.dram_tensor(name, list(shape), dt, kind="ExternalInput").ap()

    hin = din("hin", [D, NT])
    cosd = din("cosd", [64, NT])
    sind = din("sind", [64, NT])
    pmd = din("pm", [128, 4])
    gaind = din("gains", [128, 9 * 16])
    mgd = din("mg", [128, 16])
    lrpd = din("lrp", [128, 2 * 16 * 8])
    hout = nc.dram_tensor("hout", [D, NT], F32, kind="ExternalOutput").ap()
    wd = {}
    for l in layer_ids:
        j = l // 2
        wd[("wgu", l)] = din("wgu%d" % l, [NJ, 128, 4096])
        wd[("wdn", l)] = din("wdn%d" % l, [32, 128, 2816])
        if l % 2 == 0:
            wd[("win", j)] = din("win%d" % j, [9, 128, 2048])
            wd[("whd", j)] = din("whd%d" % j, [NH, 128, 2048])
            wd[("wo", j)] = din("wo%d" % j, [16, 128, 2048])
        else:
            wd[("lin", j)] = din("lin%d" % j, [16, 128, 4096])
            wd[("lgw", j)] = din("lgw%d" % j, [1, 128, 4096])
            wd[("lwo", j)] = din("lwo%d" % j, [16, 128, 2048])

    RES = nc.dram_tensor("res", [D, NT], F32)
    CQD = nc.dram_tensor("cqd", [512, NT], BF16)
    LATO = nc.dram_tensor("lato", [640, NT], BF16)
    LATS = [[nc.dram_tensor("lats%d_%d" % (i, k), [128, NFR], BF16) for k in range(5)] for i in range(2)]
    LATR = [[nc.dram_tensor("latr%d_%d" % (i, k), [256, NFR], BF16) for k in range(5)] for i in range(2)]
    LATP = [nc.dram_tensor("latp%d" % i, [640, NFR], BF16) for i in range(2)]
    OTD = nc.dram_tensor("otd", [D, NT], BF16)
    G0D = nc.dram_tensor("g0d", [D, NT], F32)
    G1D = nc.dram_tensor("g1d", [D, NT], F32)
    HSD = [nc.dram_tensor("hsd%d" % i, [128, 1024], F32) for i in range(2)]
    HRD = [nc.dram_tensor("hrd%d" % i, [256, 1024], F32) for i in range(2)]
    SSD = [nc.dram_tensor("ssd%d" % i, [128, 16], F32) for i in range(2)]
    SRD = [nc.dram_tensor("srd%d" % i, [256, 16], F32) for i in range(2)]

    def sb(name, shape, dt):
        return es.enter_context(nc.sbuf_tensor(name, list(shape), dt))

    WS = sb("ws", [128, NS, 4096], BF16)
    HT = sb("ht", [128, KC, TILE], F32)
    HNX = sb("hnx", [128, KC, TILE], BF16)
    AT = sb("at", [128, GJ, TILE], BF16)
    SQ = sb("sq", [128, 2, TILE], BF16)
    RS = sb("rs", [128, 2, TILE], F32)
    SIL = sb("sil", [128, 2, TILE], F32)
    ONES = sb("ones", [128, 128], BF16)
    ZEROS = sb("zeros", [128, TILE], F32)
    CONST = sb("const", [128, 4], F32)
    GAINS = sb("gainsb", [128, 9, 16], F32)
    PM = sb("pmb", [128, 4], F32)
    MG = sb("mgb", [128, 2, 8], F32)
    LRP = sb("lrpb", [128, 2, 16, 8], F32)
    LS = sb("lsb", [128, 2, 3, 16], F32)
    XTAIL = sb("xtail", [128, 16, 3], F32)
    CAR = sb("car", [128, 6, 16], F32)
    HXR = sb("hxr", [128, 16, 3], F32)
    HH = sb("hh", [128, 16, 64], F32)
    HHN = sb("hhn", [128, 16, 64], BF16)
    ARENA = sb("arena", [128, NASLOT * ASLOT], F32)
    PS = es.enter_context(nc.psum_tensor("ps", [128, 8, 512], F32))

    def gen(S, Wst):
        B = {}

        def buf(name):
            if name not in B:
                B[name] = Buf(name)
            return B[name]

        PB = [buf("pb%d" % i) for i in range(8)]
        WSB = [buf("ws%d" % i) for i in range(NS)]
        AB = [buf("ar%d" % i) for i in range(NASLOT)]
        bHT, bHNX, bAT = buf("HT"), buf("HNX"), buf("AT")
        bSQ = [buf("sq0"), buf("sq1")]
        bRS = [buf("rs0"), buf("rs1")]
        bSIL = [buf("sil0"), buf("sil1")]
        bCONST = buf("const")
        bSM = buf("small")
        bXTAIL, bCAR, bHXR, bHH, bHHN = buf("xtail"), buf("car"), buf("hxr"), buf("hh"), buf("hhn")
        Wst.bind(S, WS, WSB)

        def af32(slot, n=1):
            return V(ARENA[:, slot * ASLOT:(slot + n) * ASLOT], AB[slot:slot + n])

        def abf(slot, n):
            return V(ARENA[:, slot * ASLOT:(slot + n) * ASLOT].bitcast(BF16), AB[slot:slot + n])

        def dtile(t3d, name, t, c0=None, c1=None):
            if c0 is None:
                c0, c1 = t * TILE, (t + 1) * TILE
            return V(t3d[:, :, c0:c1], [buf("%s_t%d" % (name, t))])

        def pkn(ap):
            return ap.rearrange("(k p) n -> p k n", p=128)

        S.add("dve", lambda e: e.memset(ONES[:, :], 1.0), W=[bCONST])
        S.add("dve", lambda e: e.memset(ZEROS[:, :], 0.0), W=[bCONST])
        S.add("dve", lambda e: e.memset(CONST[:, 0:1], EPS), W=[bCONST])
        S.add("dve", lambda e: e.memset(CONST[:, 1:2], 1.0), W=[bCONST])
        S.add("dve", lambda e: e.memset(CONST[:, 2:3], 0.0), W=[bCONST])
        S.add("sp", lambda e: e.dma_start(out=GAINS[:, :, :].rearrange("p a b -> p (a b)"), in_=gaind[:, :]), W=[bSM], dma=True)
        S.add("sp", lambda e: e.dma_start(out=PM[:, :], in_=pmd[:, :]), W=[bSM], dma=True)
        S.add("sp", lambda e: e.dma_start(out=MG[:, :, :].rearrange("p a b -> p (a b)"), in_=mgd[:, :]), W=[bSM], dma=True)
        S.add("sp", lambda e: e.dma_start(out=LRP[:, :, :, :].rearrange("p a b c -> p (a b c)"), in_=lrpd[:, :]), W=[bSM], dma=True)
        bLS = buf("ls")
        for jj in range(2):
            S.add("act", lambda e, jj=jj: e.activation(out=LS[:, jj, 2, :], in_=LRP[:, jj, :, 7], func=AF.Exp, scale=-1.0), R=[bSM], W=[bLS], short=True)
            S.add("act", lambda e, jj=jj: e.activation(out=LS[:, jj, 2, :], in_=LS[:, jj, 2, :], func=AF.Ln, bias=CONST[:, 1:2], scale=1.0), R=[bCONST], W=[bLS], short=True)
            S.add("dve", lambda e, jj=jj: e.tensor_scalar(LS[:, jj, 0, :], LS[:, jj, 2, :], -8.0, 0.0, ALU.mult, ALU.add), R=[bLS], W=[bLS], short=True)
            S.add("dve", lambda e, jj=jj: e.tensor_scalar(LS[:, jj, 1, :], LS[:, jj, 2, :], 4.0, 0.0, ALU.mult, ALU.add), R=[bLS], W=[bLS], short=True)

        def mm(out_ap, lhsT, rhs, start, stop, R, W):
            S.add("pe", lambda e: e.matmul(out_ap, lhsT, rhs, start=start, stop=stop), R=R, W=W)

        def segsof(w):
            if w <= 512:
                return [(0, w)]
            h = w // 2
            return [(0, h), (h, w)]

        def rmsnorm(src, srcb, w, nk, inv_n, gain_ap, dst, dstb, banks, rsv, rsb, inplace_f32=False):
            sh = w < 128
            sg = segsof(w)
            for k in range(nk):
                sl = k % 2
                S.add("act", lambda e, k=k, sl=sl: e.activation(out=SQ[:, sl, 0:w], in_=src[:, k, 0:w], func=AF.Square), R=[srcb], W=[bSQ[sl]], short=sh)
                for si, (a, b) in enumerate(sg):
                    mm(PS[:, banks[si], 0:b - a], ONES[:, :], SQ[:, sl, a:b], k == 0, k == nk - 1, [bCONST, bSQ[sl]], [PB[banks[si]]])
            for si, (a, b) in enumerate(sg):
                S.add("act", lambda e, si=si, a=a, b=b: e.activation(out=rsv[:, a:b], in_=PS[:, banks[si], 0:b - a], func=AF.Sqrt, bias=CONST[:, 0:1], scale=inv_n), R=[PB[banks[si]], bCONST], W=[rsb], short=sh)
            S.add("dve", lambda e: e.reciprocal(rsv[:, 0:w], rsv[:, 0:w]), R=[rsb], W=[rsb], short=sh)
            for k in range(nk):
                S.add("dve", lambda e, k=k: e.scalar_tensor_tensor(dst[:, k, 0:w], src[:, k, 0:w], gain_ap[:, k:k + 1], rsv[:, 0:w], ALU.mult, ALU.mult), R=[srcb, rsb, bSM], W=[dstb], short=sh)

        def seg3(ap2d):
            return ap2d.rearrange("p (s c) -> p s c", s=2)

        def ffn(l):
            for g in range(NG):
                for jj in range(GJ):
                    j = g * GJ + jj
                    sl = Wst.get(("wgu", l, j), wd[("wgu", l)][j], 4096)
                    wv = WS[:, sl, :].rearrange("p (a k m) -> p a k m", a=2, k=16)
                    for half, b0 in ((0, 0), (1, 2)):
                        for k in range(KC):
                            for si, (a, b) in enumerate(SEGS):
                                mm(PS[:, b0 + si, 0:SEG], wv[:, half, k, :], HNX[:, k, a:b], k == 0, k == KC - 1, [WSB[sl], bHNX], [PB[b0 + si]])
                    ss = jj % 2
                    S.add("act", lambda e, ss=ss: e.activation(out=seg3(SIL[:, ss, :]), in_=PS[:, 0:2, 0:SEG], func=AF.Silu), R=[PB[0], PB[1]], W=[bSIL[ss]])
                    S.add("dve", lambda e, ss=ss, jj=jj: e.tensor_tensor(seg3(AT[:, jj, :]), seg3(SIL[:, ss, :]), PS[:, 2:4, 0:SEG], ALU.mult), R=[bSIL[ss], PB[2], PB[3]], W=[bAT])
                for ocp in range(8):
                    sl = Wst.get(("wdn", l, g, ocp), wd[("wdn", l)][g * 8 + ocp], 2816)
                    wv = WS[:, sl, 0:2816].rearrange("p (a j m) -> p a j m", a=2, j=GJ)
                    for o2 in range(2):
                        oc = ocp * 2 + o2
                        b0 = 4 + 2 * (oc % 2)
                        for jj in range(GJ):
                            for si, (a, b) in enumerate(SEGS):
                                mm(PS[:, b0 + si, 0:SEG], wv[:, o2, jj, :], AT[:, jj, a:b], jj == 0, jj == GJ - 1, [WSB[sl], bAT], [PB[b0 + si]])
                        S.add("dve", lambda e, oc=oc, b0=b0: e.tensor_tensor(seg3(HT[:, oc, :]), seg3(HT[:, oc, :]), PS[:, b0:b0 + 2, 0:SEG], ALU.add), R=[bHT, PB[b0], PB[b0 + 1]], W=[bHT])

        def proj_residual(key, blocks):
            for oc in range(16):
                sl = Wst.get((key, oc), blocks[oc], 2048)
                wv = WS[:, sl, 0:2048].rearrange("p (k m) -> p k m", k=16)
                b0 = 4 + 2 * (oc % 2)
                for k in range(KC):
                    for si, (a, b) in enumerate(SEGS):
                        mm(PS[:, b0 + si, 0:SEG], wv[:, k, :], HNX[:, k, a:b], k == 0, k == KC - 1, [WSB[sl], bHNX], [PB[b0 + si]])
                S.add("dve", lambda e, oc=oc, b0=b0: e.tensor_tensor(seg3(HT[:, oc, :]), seg3(HT[:, oc, :]), PS[:, b0:b0 + 2, 0:SEG], ALU.add), R=[bHT, PB[b0], PB[b0 + 1]], W=[bHT])

        def tail_tile(l, t, hdst3, hdst_name, is_final):
            rmsnorm(HT, bHT, TILE, KC, 1.0 / D, GAINS[:, 4 + l, :], HNX, bHNX, (0, 1), RS[:, 0, :], bRS[0])
            ffn(l)
            if is_final:
                rmsnorm(HT, bHT, TILE, KC, 1.0 / D, GAINS[:, 8, :], HT, bHT, (0, 1), RS[:, 0, :], bRS[0])
            dv = dtile(hdst3, hdst_name, t)
            S.add("sp", lambda e: e.dma_start(out=dv.ap, in_=HT[:, :, :]), R=[bHT], W=[dv], dma=True)

        def mla_layer(l, hsrc3, hsrc_name, hdst3, hdst_name, is_final):
            j = l // 2
            lats, latr, latp = LATS[j], LATR[j], LATP[j]
            cq3 = pkn(CQD[:, :])
            lato3 = pkn(LATO[:, :])
            for t in range(NTILES):
                sv = dtile(hsrc3, hsrc_name, t)
                S.add("sp", lambda e, sv=sv: e.dma_start(out=HT[:, :, :], in_=sv.ap), R=[sv], W=[bHT], dma=True)
                c0, c1 = t * TILE, (t + 1) * TILE
                S.add("sp", lambda e, c0=c0, c1=c1: e.dma_start(out=SIL[0:64, 0, :], in_=cosd[:, c0:c1]), W=[bSIL[0]], dma=True)
                S.add("sp", lambda e, c0=c0, c1=c1: e.dma_start(out=SIL[0:64, 1, :], in_=sind[:, c0:c1]), W=[bSIL[1]], dma=True)
                rmsnorm(HT, bHT, TILE, KC, 1.0 / D, GAINS[:, l, :], HNX, bHNX, (0, 1), RS[:, 0, :], bRS[0])
                CQ = [af32(i) for i in range(4)]
                CKV = [af32(4 + i) for i in range(4)]
                CQN = abf(8, 2)
                LATT = abf(10, 3)
                TM1, TM2 = af32(13), af32(14)
                RSQ = af32(15)
                cqn3 = CQN.ap[:, 0:4 * TILE].rearrange("p (k n) -> p k n", k=4)
                latt3 = LATT.ap[:, 0:5 * TILE].rearrange("p (k n) -> p k n", k=5)
                for blk in range(8):
                    sl = Wst.get(("win", j, blk), wd[("win", j)][blk], 2048)
                    wv = WS[:, sl, 0:2048].rearrange("p (k m) -> p k m", k=16)
                    b0 = 4 + 2 * (blk % 2)
                    for k in range(KC):
                        for si, (a, b) in enumerate(SEGS):
                            mm(PS[:, b0 + si, 0:SEG], wv[:, k, :], HNX[:, k, a:b], k == 0, k == KC - 1, [WSB[sl], bHNX], [PB[b0 + si]])
                    dstv = CQ[blk] if blk < 4 else CKV[blk - 4]
                    S.add("act", lambda e, dstv=dstv, b0=b0: e.activation(out=seg3(dstv.ap[:, 0:TILE]), in_=PS[:, b0:b0 + 2, 0:SEG], func=AF.Copy), R=[PB[b0], PB[b0 + 1]], W=[dstv])
                sl = Wst.get(("win", j, 8), wd[("win", j)][8], 2048)
                wv = WS[:, sl, 0:2048].rearrange("p (k m) -> p k m", k=16)
                for which, b0 in ((0, 0), (1, 2)):
                    for k in range(KC):
                        for si, (a, b) in enumerate(SEGS):
                            mm(PS[0:64, b0 + si, 0:SEG], wv[:, k, which * 64:(which + 1) * 64], HNX[:, k, a:b], k == 0, k == KC - 1, [WSB[sl], bHNX], [PB[b0 + si]])
                S.add("dve", lambda e, TM1=TM1: e.tensor_tensor(seg3(TM1.ap[0:64, 0:TILE]), PS[0:64, 0:2, 0:SEG], seg3(SIL[0:64, 0, :]), ALU.mult), R=[PB[0], PB[1], bSIL[0]], W=[TM1])
                S.add("dve", lambda e, TM2=TM2: e.tensor_tensor(seg3(TM2.ap[0:64, 0:TILE]), PS[0:64, 2:4, 0:SEG], seg3(SIL[0:64, 1, :]), ALU.mult), R=[PB[2], PB[3], bSIL[1]], W=[TM2])
                S.add("dve", lambda e, TM1=TM1, TM2=TM2, latt3=latt3: e.tensor_tensor(latt3[0:64, 4, :], TM1.ap[0:64, 0:TILE], TM2.ap[0:64, 0:TILE], ALU.add), R=[TM1, TM2], W=[LATT])
                for which, (srcs, dst3, goff) in enumerate(((CQ, cqn3, 0), (CKV, latt3, 4))):
                    for k in range(4):
                        sl2 = k % 2
                        S.add("act", lambda e, k=k, sl2=sl2, srcs=srcs: e.activation(out=SQ[:, sl2, :], in_=srcs[k].ap[:, 0:TILE], func=AF.Square), R=[srcs[k]], W=[bSQ[sl2]])
                        for si, (a, b) in enumerate(SEGS):
                            mm(PS[:, si, 0:SEG], ONES[:, :], SQ[:, sl2, a:b], k == 0, k == 3, [bCONST, bSQ[sl2]], [PB[si]])
                    S.add("act", lambda e, RSQ=RSQ: e.activation(out=seg3(RSQ.ap[:, 0:TILE]), in_=PS[:, 0:2, 0:SEG], func=AF.Sqrt, bias=CONST[:, 0:1], scale=1.0 / 512), R=[PB[0], PB[1], bCONST], W=[RSQ])
                    S.add("dve", lambda e, RSQ=RSQ: e.reciprocal(RSQ.ap[:, 0:TILE], RSQ.ap[:, 0:TILE]), R=[RSQ], W=[RSQ])
                    dstV = CQN if which == 0 else LATT
                    for k in range(4):
                        S.add("dve", lambda e, k=k, srcs=srcs, dst3=dst3, goff=goff, RSQ=RSQ: e.scalar_tensor_tensor(dst3[:, k, :], srcs[k].ap[:, 0:TILE], MG[:, j, goff + k:goff + k + 1], RSQ.ap[:, 0:TILE], ALU.mult, ALU.mult), R=[srcs[k], RSQ, bSM], W=[dstV])
                cqv = dtile(cq3, "cqd", t)
                S.add("sp", lambda e, cqv=cqv, cqn3=cqn3: e.dma_start(out=cqv.ap, in_=cqn3), R=[CQN], W=[cqv], dma=True)
                lov = dtile(lato3, "lato", t)
                S.add("sp", lambda e, lov=lov, latt3=latt3: e.dma_start(out=lov.ap, in_=latt3), R=[LATT], W=[lov], dma=True)
                f0 = max(c0, NMETA)
                for k in range(5):
                    lsv = V(lats[k][:, f0 - NMETA:c1 - NMETA], [buf("lats%d_%d_t%d" % (j, k, t))])
                    S.add("sp", lambda e, lsv=lsv, latt3=latt3, f0=f0, c0=c0, k=k: e.dma_start(out=lsv.ap, in_=latt3[:, k, f0 - c0:TILE]), R=[LATT], W=[lsv], dma=True)
            if DEBUG.get('stop') == 'p1':
                return
            blr = buf("latp%d" % j)
            for k in range(5):
                brk = buf("latr%d_%d" % (j, k))
                S.add("pool", lambda e, k=k: e.collective_compute("AllGather", ALU.bypass, replica_groups=RGROUPS,
                                                             ins=[lats[k].ap().opt()], outs=[latr[k].ap().opt()]),
                      R=[buf("lats%d_%d_t%d" % (j, k, t)) for t in range(NTILES)], W=[brk], coll=True)
                bpk = buf("latp%d_%d" % (j, k))
                S.add("sp", lambda e, k=k: e.dma_start(out=latp[k * 128:(k + 1) * 128, :], in_=latr[k][0:128, :]), R=[brk], W=[bpk], dma=True)
            allp = [buf("latp%d_%d" % (j, k)) for k in range(5)]
            latr3 = pkn(latp[:, :])
            if DEBUG.get('stop') == 'xch':
                return
            CQNin = abf(0, 2)
            LATin = abf(2, 2)
            QN, QR = abf(4, 2), abf(6, 2)
            KNo, KNp = abf(8, 2), abf(10, 2)
            Vo, Vp = abf(12, 2), abf(14, 2)
            KRo, KRp = abf(16, 2), abf(18, 2)
            PTs = [V(ARENA[:, 20 * ASLOT:22 * ASLOT].bitcast(BF16)[:, i * 512:(i + 1) * 512], [buf("pt%d" % i), AB[20], AB[21]]) for i in range(4)]
            for p_ in PTs:
                p_.bufs = [p_.bufs[0]]
            OO = abf(22, 1)
            RD = af32(23)
            PTS = af32(24)
            PTSB = abf(25, 1)
            HNXf = HNX[:, :, :].rearrange("p k n -> p (k n)")
            ATf = AT[:, :, :].rearrange("p j n -> p (j n)").bitcast(F32)
            CQI = [V(HNXf[:, b_ * 2048:(b_ + 1) * 2048].rearrange("p (k n) -> p k n", k=4), [buf("cqin%d" % b_)]) for b_ in range(2)]
            LTI = [V(HNXf[:, 4096 + b_ * 2048:4096 + (b_ + 1) * 2048].rearrange("p (k n) -> p k n", k=4), [buf("latin%d" % b_)]) for b_ in range(2)]
            COSB = [V(ATf[0:64, b_ * 512:(b_ + 1) * 512], [buf("cosb%d" % b_)]) for b_ in range(2)]
            SINB = [V(ATf[0:64, 1024 + b_ * 512:1024 + (b_ + 1) * 512], [buf("sinb%d" % b_)]) for b_ in range(2)]
            S.add("dve", lambda e: e.memset(HNXf[0:1, 0:2], 0.0), W=[bHNX] + CQI + LTI)
            S.add("dve", lambda e: e.memset(ATf[0:1, 0:1], 0.0), W=[bAT] + COSB + SINB)
            gctr = [0]
            vo3 = Vo.ap[:, 0:17 * 128].rearrange("p (b d) -> p b d", b=17)
            vp3 = Vp.ap[:, 0:16 * 128].rearrange("p (b d) -> p b d", b=16)
            allat = [buf("lato_t%d" % t) for t in range(NTILES)]
            allcq = [buf("cqd_t%d" % t) for t in range(NTILES)]
            S.add("sp", lambda e: e.dma_start(out=KRo.ap[0:64, 0:NT], in_=LATO[512:576, :]), R=allat, W=[KRo, AB[20], AB[21]], dma=True)
            S.add("sp", lambda e: e.dma_start(out=KRp.ap[0:64, 0:NFR], in_=latp[512:576, :]), R=allp, W=[KRp], dma=True)
            QG = [(0, 128)] + [(NMETA + 512 * g, NMETA + 512 * (g + 1)) for g in range(4)]
            otd3 = OTD[:, :]
            for h in range(DEBUG.get('nh', NH)):
                sl = Wst.get(("whd", j, h), wd[("whd", j)][h], 2048)
                wq = WS[:, sl, 0:1024].rearrange("p (k m) -> p k m", k=4)
                wkv = WS[:, sl, 1024:2048].rearrange("p (k m) -> p k m", k=4)
                wb = WSB[sl]
                for gi, (c0, c1) in enumerate(QG):
                    w = c1 - c0
                    pb_ = gctr[0] % 2
                    gctr[0] += 1
                    CQNin, LATin, COSv, SINv = CQI[pb_], LTI[pb_], COSB[pb_], SINB[pb_]
                    cqin3, latin3 = CQNin.ap, LATin.ap
                    S.add("sp", lambda e, c0=c0, c1=c1, w=w, cqin3=cqin3: e.dma_start(out=cqin3[:, :, 0:w], in_=cq3[:, :, c0:c1]), R=allcq, W=[CQNin], dma=True)
                    S.add("sp", lambda e, c0=c0, c1=c1, w=w, COSv=COSv: e.dma_start(out=COSv.ap[:, 0:w], in_=cosd[:, c0:c1]), W=[COSv], dma=True)
                    S.add("sp", lambda e, c0=c0, c1=c1, w=w, SINv=SINv: e.dma_start(out=SINv.ap[:, 0:w], in_=sind[:, c0:c1]), W=[SINv], dma=True)
                    wl = 128 if gi == 0 else w
                    S.add("sp", lambda e, c0=c0, wl=wl, latin3=latin3: e.dma_start(out=latin3[:, 0:4, 0:wl], in_=lato3[:, 0:4, c0:c0 + wl]), R=allat, W=[LATin], dma=True)
                    for k in range(4):
                        mm(PS[:, 5, 0:w], wq[:, k, 0:128], cqin3[:, k, 0:w], k == 0, k == 3, [wb, CQNin], [PB[5]])
                    S.add("act", lambda e, c0=c0, c1=c1, w=w: e.activation(out=QN.ap[:, c0:c1], in_=PS[:, 5, 0:w], func=AF.Copy), R=[PB[5]], W=[QN])
                    for k in range(4):
                        mm(PS[0:64, 6, 0:w], wq[:, k, 128:192], cqin3[:, k, 0:w], k == 0, k == 3, [wb, CQNin], [PB[6]])
                    for k in range(4):
                        mm(PS[0:64, 7, 0:w], wq[:, k, 192:256], cqin3[:, k, 0:w], k == 0, k == 3, [wb, CQNin], [PB[7]])
                    S.add("dve", lambda e, w=w, COSv=COSv: e.tensor_tensor(SIL[0:64, 1, 0:w], PS[0:64, 6, 0:w], COSv.ap[:, 0:w], ALU.mult), R=[PB[6], COSv], W=[bSIL[1]])
                    S.add("dve", lambda e, w=w, SINv=SINv: e.tensor_tensor(RS[0:64, 0, 0:w], PS[0:64, 7, 0:w], SINv.ap[:, 0:w], ALU.mult), R=[PB[7], SINv], W=[bRS[0]])
                    S.add("dve", lambda e, c0=c0, c1=c1, w=w: e.tensor_tensor(QR.ap[0:64, c0:c1], SIL[0:64, 1, 0:w], RS[0:64, 0, 0:w], ALU.add), R=[bSIL[1], bRS[0]], W=[QR])
                    for k in range(4):
                        mm(PS[:, 5, 0:w], wkv[:, k, 0:128], latin3[:, k, 0:w], k == 0, k == 3, [wb, LATin], [PB[5]])
                    S.add("act", lambda e, c0=c0, c1=c1, w=w: e.activation(out=KNo.ap[:, c0:c1], in_=PS[:, 5, 0:w], func=AF.Copy), R=[PB[5]], W=[KNo])
                    if gi == 0:
                        for k in range(4):
                            mm(PS[:, 6, 0:128], latin3[:, k, 0:128], wkv[:, k, 128:256], k == 0, k == 3, [wb, LATin], [PB[6]])
                        S.add("dve", lambda e: e.tensor_copy(vo3[:, 0, :], PS[:, 6, 0:128]), R=[PB[6]], W=[Vo])
                    else:
                        for bb in range(4):
                            for k in range(4):
                                mm(PS[:, 6, bb * 128:(bb + 1) * 128], latin3[:, k, bb * 128:(bb + 1) * 128], wkv[:, k, 128:256], k == 0, k == 3, [wb, LATin], [PB[6]])
                        vb0 = 1 + 4 * (gi - 1)
                        S.add("dve", lambda e, vb0=vb0: e.tensor_copy(vo3[:, vb0:vb0 + 4, :].rearrange("p b d -> p (b d)"), PS[:, 6, :]), R=[PB[6]], W=[Vo])
                for g in range(4):
                    c0, c1 = 512 * g, 512 * (g + 1)
                    pb_ = gctr[0] % 2
                    gctr[0] += 1
                    LATin = LTI[pb_]
                    latin3 = LATin.ap
                    S.add("sp", lambda e, c0=c0, c1=c1, latin3=latin3: e.dma_start(out=latin3[:, 0:4, :], in_=latr3[:, 0:4, c0:c1]), R=allp, W=[LATin], dma=True)
                    for k in range(4):
                        mm(PS[:, 5, :], wkv[:, k, 0:128], latin3[:, k, :], k == 0, k == 3, [wb, LATin], [PB[5]])
                    S.add("act", lambda e, c0=c0, c1=c1: e.activation(out=KNp.ap[:, c0:c1], in_=PS[:, 5, :], func=AF.Copy), R=[PB[5]], W=[KNp])
                    for bb in range(4):
                        for k in range(4):
                            mm(PS[:, 6, bb * 128:(bb + 1) * 128], latin3[:, k, bb * 128:(bb + 1) * 128], wkv[:, k, 128:256], k == 0, k == 3, [wb, LATin], [PB[6]])
                    S.add("dve", lambda e, g=g: e.tensor_copy(vp3[:, 4 * g:4 * g + 4, :].rearrange("p b d -> p (b d)"), PS[:, 6, :]), R=[PB[6]], W=[Vp])
                for gi, (c0, c1) in enumerate(QG):
                    w = c1 - c0
                    if gi not in DEBUG.get('groups', range(5)):
                        continue
                    kbl = []
                    if gi == 0:
                        kbl.append((KNo, KRo, vo3, Vo, 0, 128, 0, 0, False, 'meta'))
                    else:
                        g = gi - 1
                        for pb in range(16):
                            kbl.append((KNp, KRp, vp3, Vp, pb * 128, 128, pb, 0, False, 'prev'))
                        kbl.append((KNo, KRo, vo3, Vo, 0, 128, 0, 0, False, 'meta'))
                        for i in range(4 * g + 4):
                            kbl.append((KNo, KRo, vo3, Vo, NMETA + 128 * i, 128, 1 + i, max(0, 128 * (i - 4 * g)), i >= 4 * g, None))
                    if DEBUG.get('kfilter'):
                        kf = DEBUG['kfilter']
                        kbl = [kb for kb in kbl if (('p' in kf and kb[9] == 'prev') or ('m' in kf and kb[9] == 'meta') or ('d' in kf and kb[8]) or ('o' in kf and (not kb[9]) and kb[5] == 128 and not kb[8]))]
                    if DEBUG.get('fullcols'):
                        kbl = [kb[:7] + (0,) + kb[8:] for kb in kbl]
                    if DEBUG.get('nodiag'):
                        kbl = [kb[:8] + (False,) + kb[9:] for kb in kbl]
                    nb = len(kbl)

                    def qk(bi):
                        KNv, KRv, v3, Vv, k0, kw, vb, cs, diag, prevb = kbl[bi]
                        sb_ = bi % 3
                        mm(PS[0:kw, sb_, cs:w], KNv.ap[:, k0:k0 + kw], QN.ap[:, c0 + cs:c1], True, False, [KNv, QN], [PB[sb_]])
                        mm(PS[0:kw, sb_, cs:w], KRv.ap[0:64, k0:k0 + kw], QR.ap[0:64, c0 + cs:c1], False, True, [KRv, QR], [PB[sb_]])
                        pt = PTs[bi % 4]
                        o_ = pt.ap[0:kw, cs:w]
                        i_ = PS[0:kw, sb_, cs:w]
                        if prevb:
                            bi_ = PM[0:kw, 2:3] if prevb == 'prev' else PM[0:kw, 3:4]
                            S.add("act", lambda e, o_=o_, i_=i_, bi_=bi_: e.activation(out=o_, in_=i_, func=AF.Exp, bias=bi_, scale=SCALE), R=[PB[sb_], bSM], W=[pt])
                        else:
                            bi_ = CONST[0:kw, 2:3]
                            S.add("act", lambda e, o_=o_, i_=i_, bi_=bi_: e.activation(out=o_, in_=i_, func=AF.Exp, bias=bi_, scale=SCALE), R=[PB[sb_], bCONST], W=[pt])
                            if diag:
                                z_ = pt.ap[64:128, cs:cs + 64]
                                S.add("act", lambda e, z_=z_: e.memzero(z_), W=[pt])

                    def pv(bi):
                        KNv, KRv, v3, Vv, k0, kw, vb, cs, diag, prevb = kbl[bi]
                        pt = PTs[bi % 4]
                        first = bi == 0
                        last = bi == nb - 1
                        if True:
                            mm(PS[:, 3, cs:w], v3[0:kw, vb, :], pt.ap[0:kw, cs:w], first, last, [Vv, pt], [PB[3]])
                            po_, pi_ = PTS.ap[:, cs:w], pt.ap[:, cs:w]
                            if first:
                                S.add("pool", lambda e, po_=po_, pi_=pi_: e.tensor_copy(po_, pi_), R=[pt], W=[PTS])
                            else:
                                S.add("pool", lambda e, po_=po_, pi_=pi_: e.tensor_tensor(po_, po_, pi_, ALU.add), R=[pt, PTS], W=[PTS])
                        else:
                            mm(PS[:, 3, cs:w], v3[0:64, vb, :], pt.ap[0:64, cs:w], first, False, [Vv, pt], [PB[3]])
                            mm(PS[:, 4, cs:w], ONES[0:64, :], pt.ap[0:64, cs:w], first, False, [bCONST, pt], [PB[4]])
                            mm(PS[:, 3, cs + 64:w], v3[64:128, vb, :], pt.ap[64:128, cs + 64:w], False, last, [Vv, pt], [PB[3]])
                            mm(PS[:, 4, cs + 64:w], ONES[64:128, :], pt.ap[64:128, cs + 64:w], False, last, [bCONST, pt], [PB[4]])

                    qk(0)
                    if nb > 1:
                        qk(1)
                    for bi in range(nb):
                        if bi + 2 < nb:
                            qk(bi + 2)
                        pv(bi)
                    S.add("pool", lambda e, w=w: e.tensor_copy(PTSB.ap[:, 0:w], PTS.ap[:, 0:w]), R=[PTS], W=[PTSB])
                    mm(PS[:, 4, 0:w], ONES[:, :], PTSB.ap[:, 0:w], True, True, [bCONST, PTSB], [PB[4]])
                    S.add("dve", lambda e, w=w: e.reciprocal(RD.ap[:, 0:w], PS[:, 4, 0:w]), R=[PB[4]], W=[RD])
                    S.add("dve", lambda e, w=w: e.tensor_tensor(OO.ap[:, 0:w], PS[:, 3, 0:w], RD.ap[:, 0:w], ALU.mult), R=[PB[3], RD], W=[OO])
                    ws_ = NMETA if gi == 0 else w
                    ov = V(otd3[h * 128:(h + 1) * 128, c0:c0 + ws_], [buf("otd_h%d_g%d" % (h, gi))])
                    S.add("sp", lambda e, ov=ov, ws_=ws_: e.dma_start(out=ov.ap, in_=OO.ap[:, 0:ws_]), R=[OO], W=[ov], dma=True)
            if DEBUG.get('stop') == 'p2':
                return
            for p_ in PTs:
                S.add("dve", lambda e, p_=p_: e.memset(p_.ap[0:1, 0:1], 0.0), W=[p_, AB[20], AB[21]])
            S.add("dve", lambda e: e.memset(HNXf[0:1, 0:2], 0.0), W=[bHNX] + CQI + LTI)
            S.add("dve", lambda e: e.memset(ATf[0:1, 0:1], 0.0), W=[bAT] + COSB + SINB)
            allot = [buf("otd_h%d_g%d" % (h, gi)) for h in range(NH) for gi in range(5)]
            ot3 = pkn(OTD[:, :])
            for t in range(NTILES):
                sv = dtile(hsrc3, hsrc_name, t)
                S.add("sp", lambda e, sv=sv: e.dma_start(out=HT[:, :, :], in_=sv.ap), R=[sv], W=[bHT], dma=True)
                c0, c1 = t * TILE, (t + 1) * TILE
                S.add("sp", lambda e, c0=c0, c1=c1: e.dma_start(out=HNX[:, :, :], in_=ot3[:, :, c0:c1]), R=allot, W=[bHNX], dma=True)
                proj_residual(("wo", j), wd[("wo", j)])
                tail_tile(l, t, hdst3, hdst_name, is_final)

        def lru_layer(l, hsrc3, hsrc_name, hsrc2d, hdst3, hdst_name, is_final):
            j = l // 2
            hsd, hrd, ssd, srd = HSD[j], HRD[j], SSD[j], SRD[j]
            bhs, bhr = buf("hsd%d" % j), buf("hrd%d" % j)
            lastsrc = buf("%s_t%d" % (hsrc_name, NTILES - 1))
            hsd2 = hsd[:, :].rearrange("a (b c) -> (a b) c", c=64)
            hrd2 = hrd[0:128, :].rearrange("a (b c) -> (a b) c", c=64)
            S.add("sp", lambda e: e.dma_start(out=hsd2[:, :], in_=hsrc2d[:, NT - 64:NT]), R=[lastsrc], W=[bhs], dma=True)
            S.add("pool", lambda e: e.collective_compute("AllGather", ALU.bypass, replica_groups=RGROUPS,
                                                         ins=[hsd.ap().opt()], outs=[hrd.ap().opt()]), R=[bhs], W=[bhr], coll=True)
            S.add("sp", lambda e: e.dma_start(out=HH[:, :, :], in_=pkn(hrd2)), R=[bhr], W=[bHH], dma=True)
            rmsnorm(HH, bHH, 64, KC, 1.0 / D, GAINS[:, l, :], HHN, bHHN, (0,), RS[:, 1, :], bRS[1])
            if DEBUG.get('lstop') == 'halo':
                return
            GWv = abf(22, 3)
            S.add("pool", lambda e: e.dma_start(out=GWv.ap[:, 0:4096], in_=wd[("lgw", j)][0]), W=[GWv], dma=True)
            gw4 = GWv.ap[:, 0:4096].rearrange("p (a c m) -> p a c m", a=2, c=16)
            g03 = pkn(G0D[:, :])
            g13 = pkn(G1D[:, :])
            CWv = LRP[:, j, :, :]
            S1 = LS[:, j, 0, :]
            H1 = LS[:, j, 1, :]
            for t in range(DEBUG.get('ltiles', NTILES)):
                sv = dtile(hsrc3, hsrc_name, t)
                S.add("sp", lambda e, sv=sv: e.dma_start(out=HT[:, :, :], in_=sv.ap), R=[sv], W=[bHT], dma=True)
                rmsnorm(HT, bHT, TILE, KC, 1.0 / D, GAINS[:, l, :], HNX, bHNX, (4, 5), RS[:, 0, :], bRS[0])
                c0 = t * TILE
                def lru_views(c):
                        st = (c % 2) * 10
                        XB, Y, XC, RG, IG, AA, TH, H0, AC, OM = [af32(st + i) for i in range(10)]
                        XCB = abf(20 + (c % 2), 1)
                        return XB, Y, XC, RG, IG, AA, TH, H0, AC, OM, XCB

                def stageA(c, t=t, c0=c0):
                        XB, Y, XC, RG, IG, AA, TH, H0, AC, OM, XCB = lru_views(c)
                        sl = Wst.get(("lin", j, c), wd[("lin", j)][c], 4096)
                        wv = WS[:, sl, :].rearrange("p (a k m) -> p a k m", a=2, k=16)
                        for half, b0 in ((0, 0), (1, 2)):
                            for k in range(KC):
                                for si, (a, b) in enumerate(SEGS):
                                    mm(PS[:, b0 + si, 0:SEG], wv[:, half, k, :], HNX[:, k, a:b], k == 0, k == KC - 1, [WSB[sl], bHNX], [PB[b0 + si]])
                        if t == 0 and not DEBUG.get('nohalo'):
                            for k in range(KC):
                                mm(PS[:, 3, 384:448], wv[:, 0, k, :], HHN[:, k, 0:64], k == 0, k == KC - 1, [WSB[sl], bHHN], [PB[3]])
                            S.add("act", lambda e, c=c: e.activation(out=HXR[:, c, :], in_=PS[:, 3, 445:448], func=AF.Copy, scale=PM[:, 1:2]), R=[PB[3], bSM], W=[bHXR], short=True)
                        S.add("act", lambda e, Y=Y: e.activation(out=seg3(Y.ap[:, 0:TILE]), in_=PS[:, 2:4, 0:SEG], func=AF.Gelu_apprx_tanh), R=[PB[2], PB[3]], W=[Y])
                        S.add("act", lambda e, XB=XB: e.activation(out=seg3(XB.ap[:, 3:3 + TILE]), in_=PS[:, 0:2, 0:SEG], func=AF.Copy), R=[PB[0], PB[1]], W=[XB])

                        def conv(a, b, XB=XB, XC=XC, c=c, sh=False):
                            S.add("dve", lambda e: e.tensor_scalar(XC.ap[:, a:b], XB.ap[:, a + 3:b + 3], CWv[:, c, 3:4], CWv[:, c, 4:5], ALU.mult, ALU.add), R=[XB, bSM], W=[XC], short=sh)
                            for tap in range(3):
                                S.add("dve", lambda e, tap=tap: e.scalar_tensor_tensor(XC.ap[:, a:b], XB.ap[:, a + tap:b + tap], CWv[:, c, tap:tap + 1], XC.ap[:, a:b], ALU.mult, ALU.add), R=[XB, XC, bSM], W=[XC], short=sh)
                        if t == 0:
                            S.add("dve", lambda e, XB=XB: e.memset(XB.ap[:, 0:3], 0.0), W=[XB], short=True)
                            conv(0, NMETA, sh=True)
                            S.add("dve", lambda e, XB=XB, c=c: e.scalar_tensor_tensor(XB.ap[:, NMETA:NMETA + 3], XB.ap[:, NMETA:NMETA + 3], PM[:, 0:1], HXR[:, c, :], ALU.mult, ALU.add), R=[XB, bHXR, bSM], W=[XB], short=True)
                            conv(NMETA, TILE)
                        else:
                            S.add("dve", lambda e, XB=XB, c=c: e.tensor_copy(XB.ap[:, 0:3], XTAIL[:, c, :]), R=[bXTAIL], W=[XB], short=True)
                            conv(0, TILE)
                        S.add("dve", lambda e, XB=XB, c=c: e.tensor_copy(XTAIL[:, c, :], XB.ap[:, TILE:TILE + 3]), R=[XB], W=[bXTAIL], short=True)
                        S.add("act", lambda e, XC=XC, XCB=XCB: e.activation(out=XCB.ap[:, 0:TILE], in_=XC.ap[:, 0:TILE], func=AF.Copy), R=[XC], W=[XCB])

                def stageB(c, t=t, c0=c0):
                        XB, Y, XC, RG, IG, AA, TH, H0, AC, OM, XCB = lru_views(c)
                        for gate, b0 in ((0, 4), (1, 6)):
                            for si, (a, b) in enumerate(SEGS):
                                mm(PS[:, b0 + si, 0:SEG], gw4[:, gate, c, :], XCB.ap[:, a:b], True, True, [GWv, XCB], [PB[b0 + si]])
                        S.add("act", lambda e, RG=RG, c=c: e.activation(out=seg3(RG.ap[:, 0:TILE]), in_=PS[:, 4:6, 0:SEG], func=AF.Sigmoid, bias=CWv[:, c, 5:6], scale=1.0), R=[PB[4], PB[5], bSM], W=[RG])
                        S.add("act", lambda e, IG=IG, c=c: e.activation(out=seg3(IG.ap[:, 0:TILE]), in_=PS[:, 6:8, 0:SEG], func=AF.Sigmoid, bias=CWv[:, c, 6:7], scale=1.0), R=[PB[6], PB[7], bSM], W=[IG])
                        S.add("act", lambda e, RG=RG, AA=AA, c=c: e.activation(out=AA.ap[:, 0:TILE], in_=RG.ap[:, 0:TILE], func=AF.Exp, scale=S1[:, c:c + 1]), R=[RG, bLS], W=[AA])
                        S.add("act", lambda e, RG=RG, TH=TH, c=c: e.activation(out=TH.ap[:, 0:TILE], in_=RG.ap[:, 0:TILE], func=AF.Tanh, scale=H1[:, c:c + 1]), R=[RG, bLS], W=[TH])
                        S.add("dve", lambda e, AA=AA, TH=TH, OM=OM: e.scalar_tensor_tensor(OM.ap[:, 0:TILE], AA.ap[:, 0:TILE], 1.0, TH.ap[:, 0:TILE], ALU.add, ALU.mult), R=[AA, TH], W=[OM])
                        S.add("dve", lambda e, AA=AA, OM=OM: e.tensor_scalar(AA.ap[:, 0:TILE], OM.ap[:, 0:TILE], -1.0, 1.0, ALU.mult, ALU.add), R=[OM], W=[AA])
                        S.add("dve", lambda e, AA=AA, TH=TH, OM=OM: e.scalar_tensor_tensor(TH.ap[:, 0:TILE], AA.ap[:, 0:TILE], 1.0, OM.ap[:, 0:TILE], ALU.add, ALU.mult), R=[AA, OM], W=[TH])
                        S.add("act", lambda e, TH=TH: e.activation(out=TH.ap[:, 0:TILE], in_=TH.ap[:, 0:TILE], func=AF.Sqrt), R=[TH], W=[TH])
                        S.add("dve", lambda e, IG=IG, XC=XC: e.tensor_tensor(IG.ap[:, 0:TILE], IG.ap[:, 0:TILE], XC.ap[:, 0:TILE], ALU.mult), R=[IG, XC], W=[IG])
                        S.add("dve", lambda e, IG=IG, TH=TH: e.tensor_tensor(IG.ap[:, 0:TILE], IG.ap[:, 0:TILE], TH.ap[:, 0:TILE], ALU.mult), R=[IG, TH], W=[IG])
                        if t == 0:
                            S.add("dve", lambda e, AA=AA, IG=IG, H0=H0: e.tensor_tensor_scan(H0.ap[:, 0:NMETA], AA.ap[:, 0:NMETA], IG.ap[:, 0:NMETA], 0.0, ALU.mult, ALU.add), R=[AA, IG], W=[H0], short=True)
                            S.add("dve", lambda e, AC=AC: e.memset(AC.ap[:, 0:NMETA], 0.0), W=[AC], short=True)
                            S.add("dve", lambda e, H0=H0, c=c: e.tensor_copy(CAR[:, 2, c:c + 1], H0.ap[:, NMETA - 1:NMETA]), R=[H0], W=[bCAR], short=True)
                            S.add("dve", lambda e, AA=AA, IG=IG, H0=H0: e.tensor_tensor_scan(H0.ap[:, NMETA:TILE], AA.ap[:, NMETA:TILE], IG.ap[:, NMETA:TILE], 0.0, ALU.mult, ALU.add), R=[AA, IG], W=[H0])
                            S.add("dve", lambda e, AA=AA, AC=AC: e.tensor_tensor_scan(AC.ap[:, NMETA:TILE], AA.ap[:, NMETA:TILE], ZEROS[:, NMETA:TILE], 1.0, ALU.mult, ALU.add), R=[AA, bCONST], W=[AC])
                        else:
                            S.add("dve", lambda e, AA=AA, IG=IG, H0=H0, c=c: e.tensor_tensor_scan(H0.ap[:, 0:TILE], AA.ap[:, 0:TILE], IG.ap[:, 0:TILE], CAR[:, 0, c:c + 1], ALU.mult, ALU.add), R=[AA, IG, bCAR], W=[H0])
                            S.add("dve", lambda e, AA=AA, AC=AC, c=c: e.tensor_tensor_scan(AC.ap[:, 0:TILE], AA.ap[:, 0:TILE], ZEROS[:, 0:TILE], CAR[:, 1, c:c + 1], ALU.mult, ALU.add), R=[AA, bCONST, bCAR], W=[AC])
                        S.add("dve", lambda e, H0=H0, c=c: e.tensor_copy(CAR[:, 0, c:c + 1], H0.ap[:, TILE - 1:TILE]), R=[H0], W=[bCAR], short=True)
                        S.add("dve", lambda e, AC=AC, c=c: e.tensor_copy(CAR[:, 1, c:c + 1], AC.ap[:, TILE - 1:TILE]), R=[AC], W=[bCAR], short=True)
                        S.add("pool", lambda e, H0=H0, Y=Y: e.tensor_tensor(H0.ap[:, 0:TILE], H0.ap[:, 0:TILE], Y.ap[:, 0:TILE], ALU.mult), R=[H0, Y], W=[H0])
                        S.add("pool", lambda e, AC=AC, Y=Y: e.tensor_tensor(AC.ap[:, 0:TILE], AC.ap[:, 0:TILE], Y.ap[:, 0:TILE], ALU.mult), R=[AC, Y], W=[AC])
                        g0v = V(g03[:, c, c0:c0 + TILE], [buf("g0_%d_%d" % (t, c))])
                        g1v = V(g13[:, c, c0:c0 + TILE], [buf("g1_%d_%d" % (t, c))])
                        S.add("sp", lambda e, g0v=g0v, H0=H0: e.dma_start(out=g0v.ap, in_=H0.ap[:, 0:TILE]), R=[H0], W=[g0v], dma=True)
                        S.add("sp", lambda e, g1v=g1v, AC=AC: e.dma_start(out=g1v.ap, in_=AC.ap[:, 0:TILE]), R=[AC], W=[g1v], dma=True)

                ncx = DEBUG.get('nchunks', 16)
                stageA(0)
                for c in range(ncx):
                    if c + 1 < ncx:
                        stageA(c + 1)
                    stageB(c)

            if DEBUG.get('lstop') == 'p1':
                return
            S.add("dve", lambda e: e.tensor_tensor(CAR[:, 4, :], CAR[:, 1, :], CAR[:, 2, :], ALU.mult), R=[bCAR], W=[bCAR], short=True)
            S.add("dve", lambda e: e.tensor_tensor(CAR[:, 4, :], CAR[:, 4, :], CAR[:, 0, :], ALU.add), R=[bCAR], W=[bCAR], short=True)
            bss, bsr = buf("ssd%d" % j), buf("srd%d" % j)
            S.add("sp", lambda e: e.dma_start(out=ssd[:, :], in_=CAR[:, 4, :]), R=[bCAR], W=[bss], dma=True)
            S.add("pool", lambda e: e.collective_compute("AllGather", ALU.bypass, replica_groups=RGROUPS,
                                                         ins=[ssd.ap().opt()], outs=[srd.ap().opt()]), R=[bss], W=[bsr], coll=True)
            S.add("sp", lambda e: e.dma_start(out=CAR[:, 3, :], in_=srd[0:128, :]), R=[bsr], W=[bCAR], dma=True)
            S.add("dve", lambda e: e.tensor_scalar(CAR[:, 5, :], CAR[:, 2, :], PM[:, 0:1], 0.0, ALU.mult, ALU.add), R=[bCAR, bSM], W=[bCAR], short=True)
            S.add("dve", lambda e: e.scalar_tensor_tensor(CAR[:, 5, :], CAR[:, 3, :], PM[:, 1:2], CAR[:, 5, :], ALU.mult, ALU.add), R=[bCAR, bSM], W=[bCAR], short=True)
            if DEBUG.get('lstop') == 'x2':
                S.add("sp", lambda e: e.dma_start(out=hout[0:128, 0:96], in_=CAR[:, :, :].rearrange("p a b -> p (a b)")), R=[bCAR], W=[buf("hout_dbg")], dma=True)
                return
            for t in range(NTILES):
                sv = dtile(hsrc3, hsrc_name, t)
                S.add("sp", lambda e, sv=sv: e.dma_start(out=HT[:, :, :], in_=sv.ap), R=[sv], W=[bHT], dma=True)
                c0 = t * TILE
                for c in range(16):
                    A0, A1 = af32(2 * (c % 4)), af32(2 * (c % 4) + 1)
                    S.add("sp", lambda e, A0=A0, c=c, c0=c0: e.dma_start(out=A0.ap[:, 0:TILE], in_=g03[:, c, c0:c0 + TILE]), R=[buf("g0_%d_%d" % (t, c))], W=[A0], dma=True)
                    S.add("sp", lambda e, A1=A1, c=c, c0=c0: e.dma_start(out=A1.ap[:, 0:TILE], in_=g13[:, c, c0:c0 + TILE]), R=[buf("g1_%d_%d" % (t, c))], W=[A1], dma=True)
                    S.add("dve", lambda e, A0=A0, A1=A1, c=c: e.scalar_tensor_tensor(HNX[:, c, :], A1.ap[:, 0:TILE], CAR[:, 5, c:c + 1], A0.ap[:, 0:TILE], ALU.mult, ALU.add), R=[A0, A1, bCAR], W=[bHNX])
                proj_residual(("lwo", j), wd[("lwo", j)])
                tail_tile(l, t, hdst3, hdst_name, is_final)

        n = len(layer_ids)
        for i, l in enumerate(layer_ids):
            src2d = hin if i == 0 else RES[:, :]
            src_name = "hin" if i == 0 else "res"
            dst2d = hout if i == n - 1 else RES[:, :]
            dst_name = "hout" if i == n - 1 else "res"
            fin = final_norm and i == n - 1
            if l % 2 == 0:
                mla_layer(l, pkn(src2d), src_name, pkn(dst2d), dst_name, fin)
            else:
                lru_layer(l, pkn(src2d), src_name, src2d, pkn(dst2d), dst_name, fin)

    w0 = WStream(None)
    gen(Sched(), w0)
    S = Sched()
    w1 = WStream(w0.reqs)
    gen(S, w1)
    S.emit(nc, es)
    es.close()
    return nc


class WStream:
    def __init__(self, plan):
        self.plan = plan
        self.reqs = []
        self.i = 0
        self.issued = 0

    def bind(self, S, WS, WSB):
        self.S, self.WS, self.WSB = S, WS, WSB

    def get(self, key, src, n):
        i = self.i
        self.i += 1
        if self.plan is None:
            self.reqs.append((key, src, n))
            return i % NS
        plan = self.plan
        assert plan[i][0] == key, (plan[i][0], key)
        while self.issued < min(len(plan), i + 1 + LA):
            k, s_ap, nn = plan[self.issued]
            slot = self.issued % NS
            WS = self.WS
            self.S.add("pool", lambda e, slot=slot, s_ap=s_ap, nn=nn: e.dma_start(out=WS[:, slot, 0:nn], in_=s_ap), W=[self.WSB[slot]], dma=True)
            self.issued += 1
        return i % NS


def _blk2(W, half, nblk):
    a = W[:, :half].reshape(16, 128, nblk, 128).transpose(2, 1, 0, 3)
    b = W[:, half:].reshape(16, 128, nblk, 128).transpose(2, 1, 0, 3)
    return np.ascontiguousarray(np.stack([a, b], axis=2).reshape(nblk, 128, 4096))


def _blk_sq(W):
    return np.ascontiguousarray(W.reshape(16, 128, 16, 128).transpose(2, 1, 0, 3).reshape(16, 128, 2048))


def _prep_weights(inp, layer_ids):
    out = {}
    for l in layer_ids:
        j = l // 2
        out["wgu%d" % l] = _blk2(inp["ffn_w_gu"][l], DFF, NJ)
        wdn = inp["ffn_w_down"][l].reshape(NG, GJ, 128, 8, 2, 128).transpose(0, 3, 2, 4, 1, 5)
        out["wdn%d" % l] = np.ascontiguousarray(wdn.reshape(32, 128, 2816))
        if l % 2 == 0:
            w_in = inp["mla_w_in"][j]
            w3 = w_in.reshape(16, 128, 1088)
            blks = [w3[:, :, b * 128:(b + 1) * 128] for b in range(8)]
            rope = np.concatenate([w3[:, :, 1024:1088], w3[:, :, 1056:1088], w3[:, :, 1024:1056]], axis=2)
            blks.append(rope)
            out["win%d" % j] = np.ascontiguousarray(np.stack(blks, 0).transpose(0, 2, 1, 3).reshape(9, 128, 2048))
            wq = inp["mla_w_uq"][j].reshape(4, 128, NH, 192)
            wqh = np.concatenate([wq[..., 0:192], wq[..., 160:192], wq[..., 128:160]], axis=3)
            wkv = inp["mla_w_ukv"][j].reshape(4, 128, NH, 256)
            hd = np.concatenate([wqh.transpose(2, 1, 0, 3).reshape(NH, 128, 1024), wkv.transpose(2, 1, 0, 3).reshape(NH, 128, 1024)], axis=2)
            out["whd%d" % j] = np.ascontiguousarray(hd)
            out["wo%d" % j] = _blk_sq(inp["mla_w_o"][j])
        else:
            out["lin%d" % j] = _blk2(inp["lru_w_in"][j], 2048, 16)
            ga = inp["lru_w_gate_a"][j].transpose(1, 0, 2)
            gx = inp["lru_w_gate_x"][j].transpose(1, 0, 2)
            out["lgw%d" % j] = np.ascontiguousarray(np.stack([ga, gx], axis=1).reshape(1, 128, 4096))
            out["lwo%d" % j] = _blk_sq(inp["lru_w_o"][j])
    return out


def _prep_small(inp):
    def pk(v):
        return v.reshape(16, 128).T
    gains = np.stack([pk(inp["norm_mix"][l]) for l in range(4)] + [pk(inp["norm_ffn"][l]) for l in range(4)] + [pk(inp["norm_final"])], axis=1)
    mg = np.zeros((128, 2, 8), np.float32)
    for j in range(2):
        mg[:, j, 0:4] = inp["mla_q_norm"][j].reshape(4, 128).T
        mg[:, j, 4:8] = inp["mla_kv_norm"][j].reshape(4, 128).T
    lrp = np.zeros((128, 2, 16, 8), np.float32)
    for j in range(2):
        lrp[:, j, :, 0:4] = inp["lru_conv_w"][j].reshape(4, 16, 128).transpose(2, 1, 0)
        lrp[:, j, :, 4] = pk(inp["lru_conv_b"][j])
        lrp[:, j, :, 5] = inp["lru_b_gate_a"][j].T
        lrp[:, j, :, 6] = inp["lru_b_gate_x"][j].T
        lrp[:, j, :, 7] = pk(inp["lru_lambda"][j])
    return {"gains": np.ascontiguousarray(gains.reshape(128, 144), np.float32),
            "mg": np.ascontiguousarray(mg.reshape(128, 16)),
            "lrp": np.ascontiguousarray(lrp.reshape(128, 256))}


def _rope_tables(pos):
    inv_freq = (np.float32(10000.0) ** (-np.arange(0, 64, 2, dtype=np.float32) / np.float32(64))).astype(np.float32)
    ang = (pos.astype(np.float32)[None, :] * inv_freq[:, None]).astype(np.float32)
    c, s = np.cos(ang).astype(np.float32), np.sin(ang).astype(np.float32)
    return np.ascontiguousarray(np.concatenate([c, c], 0)), np.ascontiguousarray(np.concatenate([-s, s], 0))


_NC_CACHE = {}


def _get_nc(layer_ids, final, ncores=8):
    key = (tuple(layer_ids), final, ncores)
    if key not in _NC_CACHE:
        _NC_CACHE[key] = build(list(layer_ids), final, ncores)
    return _NC_CACHE[key]


def _core_consts(ncores):
    per = []
    for core in range(ncores):
        half = core % 2
        pos = np.concatenate([np.arange(NMETA), NMETA + half * NFR + np.arange(NFR)])
        cosd, sind = _rope_tables(pos)
        pm = np.zeros((128, 4), np.float32)
        pm[:, 0] = 1.0 if half == 0 else 0.0
        pm[:, 1] = 0.0 if half == 0 else 1.0
        pm[:, 2] = -30000.0 if half == 0 else 0.0
        pm[NMETA:, 3] = -30000.0
        per.append({"cosd": cosd, "sind": sind, "pm": pm})
    return per


def run_layers(inp, hT_list, layer_ids, final, ncores=8):
    nc = _get_nc(layer_ids, final, ncores)
    small = _prep_small(inp)
    wts = _prep_weights(inp, layer_ids)
    consts = _core_consts(ncores)
    in_maps = []
    for core in range(ncores):
        m = {"hin": hT_list[core]}
        m.update(consts[core])
        m.update(small)
        m.update(wts)
        in_maps.append(m)
    res = run_bass_kernel_spmd(nc, in_maps, core_ids=list(range(ncores)))
    return [r["hout"] for r in res.results]


def make_hT(x, meta_tokens, nb):
    hT = []
    metaT = np.ascontiguousarray(meta_tokens.T)
    for b in range(nb):
        xT = x[b].T
        for half in range(2):
            hT.append(np.ascontiguousarray(np.concatenate([metaT, xT[:, half * NFR:(half + 1) * NFR]], axis=1), dtype=np.float32))
    return hT


def kernel(**inp):
    inp = {k: np.asarray(v) for k, v in inp.items()}
    x = inp["x"]
    nb = x.shape[0]
    hT = make_hT(x, inp["meta_tokens"], nb)
    if FUSED:
        outs = run_layers(inp, hT, [0, 1, 2, 3], True, 2 * nb)
    else:
        outs = hT
        for l in range(4):
            outs = run_layers(inp, outs, [l], l == 3, 2 * nb)
    y = np.empty((nb, 2 * NFR, D), np.float32)
    for b in range(nb):
        for half in range(2):
            y[b, half * NFR:(half + 1) * NFR, :] = outs[2 * b + half][:, NMETA:].T
    return y
```

```python
import numpy as np
from contextlib import ExitStack
import concourse.bass as bass
import concourse.mybir as mybir
from concourse.bass_utils import run_bass_kernel_spmd

F32 = mybir.dt.float32
BF16 = mybir.dt.bfloat16
AF = mybir.ActivationFunctionType
ALU = mybir.AluOpType

D = 2048
KC = 16
NMETA = 16
NFR = 2048
NT = NMETA + NFR
TILE = 688
SEG = 344
NTILES = 3
SEGS = [(0, SEG), (SEG, TILE)]
DFF = 5632
NJ = 44
GJ = 11
NG = 4
NH = 16
ASLOT = 691
NASLOT = 26
NS = 4
LA = 3
SCALE = 192.0 ** -0.5
EPS = 1e-6
FUSED = True
DEBUG = {}


class Buf:
    __slots__ = ("name", "writer", "readers")

    def __init__(self, name=""):
        self.name = name
        self.writer = None
        self.readers = []


class V:
    __slots__ = ("ap", "bufs")

    def __init__(self, ap, bufs):
        self.ap = ap
        self.bufs = bufs


class Op:
    __slots__ = ("eng", "fn", "deps", "dma", "coll", "signals", "token", "prev", "short")


def _flat(vs):
    out = []
    for v in vs:
        if isinstance(v, Buf):
            out.append(v)
        elif isinstance(v, V):
            out.extend(v.bufs)
        else:
            out.extend(_flat(v))
    return out


class Sched:
    ENGS = ("pe", "act", "dve", "pool", "sp")

    def __init__(self):
        self.q = {e: [] for e in self.ENGS}

    def add(self, eng, fn, R=(), W=(), dma=False, coll=False, short=False):
        op = Op()
        op.short = short
        op.eng = eng
        op.fn = fn
        op.dma = dma or coll
        op.coll = coll
        op.signals = False
        op.token = None
        op.prev = None
        reads = _flat(R)
        writes = _flat(W)
        deps = {}
        for b in reads:
            if b.writer is not None:
                deps[id(b.writer)] = b.writer
        for b in writes:
            if b.writer is not None:
                deps[id(b.writer)] = b.writer
            for r in b.readers:
                deps[id(r)] = r
        op.deps = [d for d in deps.values() if d is not op and (d.dma or op.dma or d.eng != eng or d.short or op.short)]
        for b in reads:
            if not op.dma:
                b.readers = [r for r in b.readers if r.dma or r.eng != eng or r.short]
            b.readers.append(op)
        for b in writes:
            b.writer = op
            b.readers = []
        self.q[eng].append(op)
        return op

    def emit(self, nc, es):
        CH = 4000
        RR = 8
        for e in self.ENGS:
            for op in self.q[e]:
                for d in op.deps:
                    d.signals = True
        sems = {}

        def getsem(key):
            if key not in sems:
                sems[key] = es.enter_context(nc.semaphore("s%d" % len(sems)))
            return sems[key]

        ncoll = 0
        for e in self.ENGS:
            ccnt = 0
            dcnt = 0
            for op in self.q[e]:
                if op.coll:
                    op.token = (getsem(("coll", ncoll)), 1)
                    ncoll += 1
                elif op.dma:
                    s = getsem(("d" + e, dcnt % RR))
                    op.token = (s, 16 * (dcnt // RR + 1))
                    if dcnt >= RR:
                        op.prev = (s, 16 * (dcnt // RR))
                    dcnt += 1
                elif op.signals:
                    op.token = (getsem((e, ccnt // CH)), ccnt % CH + 1)
                    ccnt += 1
        self.nsems = len(sems)
        q = self.q

        def run(e, eng):
            waited = {}
            for op in q[e]:
                need = [d.token for d in op.deps]
                if op.prev is not None:
                    need.append(op.prev)
                for (s, v) in need:
                    if waited.get(id(s), 0) < v:
                        eng.wait_ge(s, v)
                        waited[id(s)] = v
                ins = op.fn(eng)
                if op.token is not None:
                    if op.coll:
                        ins.then_inc(op.token[0])
                    elif op.dma:
                        ins.then_inc(op.token[0], 16)
                    else:
                        ins.then_inc(op.token[0], 1)
            for op in q[e]:
                if op.dma and waited.get(id(op.token[0]), 0) < op.token[1]:
                    eng.wait_ge(op.token[0], op.token[1])
                    waited[id(op.token[0])] = op.token[1]

        with nc.Block() as block:
            @block.tensor
            def _(t):
                run("pe", t)

            @block.scalar
            def _(s):
                run("act", s)

            @block.vector
            def _(v):
                run("dve", v)

            @block.gpsimd
            def _(g):
                run("pool", g)

            @block.sync
            def _(sy):
                run("sp", sy)


def build(layer_ids, final_norm, ncores=8):
    RGROUPS = [[2 * i, 2 * i + 1] for i in range(ncores // 2)]
    nc = bass.Bass("TRN2", target_bir_lowering=False)
    es = ExitStack()

    def din(name, shape, dt=F32):
        return nc.dram_tensor(name, list(shape), dt, kind="ExternalInput").ap()

    hin = din("hin", [D, NT])
    cosd = din("cosd", [64, NT])
    sind = din("sind", [64, NT])
    pmd = din("pm", [128, 4])
    gaind = din("gains", [128, 9 * 16])
    mgd = din("mg", [128, 16])
    lrpd = din("lrp", [128, 2 * 16 * 8])
    hout = nc.dram_tensor("hout", [D, NT], F32, kind="ExternalOutput").ap()
    wd = {}
    for l in layer_ids:
        j = l // 2
        wd[("wgu", l)] = din("wgu%d" % l, [NJ, 128, 4096])
        wd[("wdn", l)] = din("wdn%d" % l, [32, 128, 2816])
        if l % 2 == 0:
            wd[("win", j)] = din("win%d" % j, [9, 128, 2048])
            wd[("whd", j)] = din("whd%d" % j, [NH, 128, 2048])
            wd[("wo", j)] = din("wo%d" % j, [16, 128, 2048])
        else:
            wd[("lin", j)] = din("lin%d" % j, [16, 128, 4096])
            wd[("lgw", j)] = din("lgw%d" % j, [1, 128, 4096])
            wd[("lwo", j)] = din("lwo%d" % j, [16, 128, 2048])

    RES = nc.dram_tensor("res", [D, NT], F32)
    CQD = nc.dram_tensor("cqd", [512, NT], BF16)
    LATO = nc.dram_tensor("lato", [640, NT], BF16)
    LATS = [[nc.dram_tensor("lats%d_%d" % (i, k), [128, NFR], BF16) for k in range(5)] for i in range(2)]
    LATR = [[nc.dram_tensor("latr%d_%d" % (i, k), [256, NFR], BF16) for k in range(5)] for i in range(2)]
    LATP = [nc.dram_tensor("latp%d" % i, [640, NFR], BF16) for i in range(2)]
    OTD = nc.dram_tensor("otd", [D, NT], BF16)
    G0D = nc.dram_tensor("g0d", [D, NT], F32)
    G1D = nc.dram_tensor("g1d", [D, NT], F32)
    HSD = [nc.dram_tensor("hsd%d" % i, [128, 1024], F32) for i in range(2)]
    HRD = [nc.dram_tensor("hrd%d" % i, [256, 1024], F32) for i in range(2)]
    SSD = [nc.dram_tensor("ssd%d" % i, [128, 16], F32) for i in range(2)]
    SRD = [nc.dram_tensor("srd%d" % i, [256, 16], F32) for i in range(2)]

    def sb(name, shape, dt):
        return es.enter_context(nc.sbuf_tensor(name, list(shape), dt))

    WS = sb("ws", [128, NS, 4096], BF16)
    HT = sb("ht", [128, KC, TILE], F32)
    HNX = sb("hnx", [128, KC, TILE], BF16)
    AT = sb("at", [128, GJ, TILE], BF16)
    SQ = sb("sq", [128, 2, TILE], BF16)
    RS = sb("rs", [128, 2, TILE], F32)
    SIL = sb("sil", [128, 2, TILE], F32)
    ONES = sb("ones", [128, 128], BF16)
    ZEROS = sb("zeros", [128, TILE], F32)
    CONST = sb("const", [128, 4], F32)
    GAINS = sb("gainsb", [128, 9, 16], F32)
    PM = sb("pmb", [128, 4], F32)
    MG = sb("mgb", [128, 2, 8], F32)
    LRP = sb("lrpb", [128, 2, 16, 8], F32)
    LS = sb("lsb", [128, 2, 3, 16], F32)
    XTAIL = sb("xtail", [128, 16, 3], F32)
    CAR = sb("car", [128, 6, 16], F32)
    HXR = sb("hxr", [128, 16, 3], F32)
    HH = sb("hh", [128, 16, 64], F32)
    HHN = sb("hhn", [128, 16, 64], BF16)
    ARENA = sb("arena", [128, NASLOT * ASLOT], F32)
    PS = es.enter_context(nc.psum_tensor("ps", [128, 8, 512], F32))

    def gen(S, Wst):
        B = {}

        def buf(name):
            if name not in B:
                B[name] = Buf(name)
            return B[name]

        PB = [buf("pb%d" % i) for i in range(8)]
        WSB = [buf("ws%d" % i) for i in range(NS)]
        AB = [buf("ar%d" % i) for i in range(NASLOT)]
        bHT, bHNX, bAT = buf("HT"), buf("HNX"), buf("AT")
        bSQ = [buf("sq0"), buf("sq1")]
        bRS = [buf("rs0"), buf("rs1")]
        bSIL = [buf("sil0"), buf("sil1")]
        bCONST = buf("const")
        bSM = buf("small")
        bXTAIL, bCAR, bHXR, bHH, bHHN = buf("xtail"), buf("car"), buf("hxr"), buf("hh"), buf("hhn")
        Wst.bind(S, WS, WSB)

        def af32(slot, n=1):
            return V(ARENA[:, slot * ASLOT:(slot + n) * ASLOT], AB[slot:slot + n])

        def abf(slot, n):
            return V(ARENA[:, slot * ASLOT:(slot + n) * ASLOT].bitcast(BF16), AB[slot:slot + n])

        def dtile(t3d, name, t, c0=None, c1=None):
            if c0 is None:
                c0, c1 = t * TILE, (t + 1) * TILE
            return V(t3d[:, :, c0:c1], [buf("%s_t%d" % (name, t))])

        def pkn(ap):
            return ap.rearrange("(k p) n -> p k n", p=128)

        S.add("dve", lambda e: e.memset(ONES[:, :], 1.0), W=[bCONST])
        S.add("dve", lambda e: e.memset(ZEROS[:, :], 0.0), W=[bCONST])
        S.add("dve", lambda e: e.memset(CONST[:, 0:1], EPS), W=[bCONST])
        S.add("dve", lambda e: e.memset(CONST[:, 1:2], 1.0), W=[bCONST])
        S.add("dve", lambda e: e.memset(CONST[:, 2:3], 0.0), W=[bCONST])
        S.add("sp", lambda e: e.dma_start(out=GAINS[:, :, :].rearrange("p a b -> p (a b)"), in_=gaind[:, :]), W=[bSM], dma=True)
        S.add("sp", lambda e: e.dma_start(out=PM[:, :], in_=pmd[:, :]), W=[bSM], dma=True)
        S.add("sp", lambda e: e.dma_start(out=MG[:, :, :].rearrange("p a b -> p (a b)"), in_=mgd[:, :]), W=[bSM], dma=True)
        S.add("sp", lambda e: e.dma_start(out=LRP[:, :, :, :].rearrange("p a b c -> p (a b c)"), in_=lrpd[:, :]), W=[bSM], dma=True)
        bLS = buf("ls")
        for jj in range(2):
            S.add("act", lambda e, jj=jj: e.activation(out=LS[:, jj, 2, :], in_=LRP[:, jj, :, 7], func=AF.Exp, scale=-1.0), R=[bSM], W=[bLS], short=True)
            S.add("act", lambda e, jj=jj: e.activation(out=LS[:, jj, 2, :], in_=LS[:, jj, 2, :], func=AF.Ln, bias=CONST[:, 1:2], scale=1.0), R=[bCONST], W=[bLS], short=True)
            S.add("dve", lambda e, jj=jj: e.tensor_scalar(LS[:, jj, 0, :], LS[:, jj, 2, :], -8.0, 0.0, ALU.mult, ALU.add), R=[bLS], W=[bLS], short=True)
            S.add("dve", lambda e, jj=jj: e.tensor_scalar(LS[:, jj, 1, :], LS[:, jj, 2, :], 4.0, 0.0, ALU.mult, ALU.add), R=[bLS], W=[bLS], short=True)

        def mm(out_ap, lhsT, rhs, start, stop, R, W):
            S.add("pe", lambda e: e.matmul(out_ap, lhsT, rhs, start=start, stop=stop), R=R, W=W)

        def segsof(w):
            if w <= 512:
                return [(0, w)]
            h = w // 2
            return [(0, h), (h, w)]

        def rmsnorm(src, srcb, w, nk, inv_n, gain_ap, dst, dstb, banks, rsv, rsb, inplace_f32=False):
            sh = w < 128
            sg = segsof(w)
            for k in range(nk):
                sl = k % 2
                S.add("act", lambda e, k=k, sl=sl: e.activation(out=SQ[:, sl, 0:w], in_=src[:, k, 0:w], func=AF.Square), R=[srcb], W=[bSQ[sl]], short=sh)
                for si, (a, b) in enumerate(sg):
                    mm(PS[:, banks[si], 0:b - a], ONES[:, :], SQ[:, sl, a:b], k == 0, k == nk - 1, [bCONST, bSQ[sl]], [PB[banks[si]]])
            for si, (a, b) in enumerate(sg):
                S.add("act", lambda e, si=si, a=a, b=b: e.activation(out=rsv[:, a:b], in_=PS[:, banks[si], 0:b - a], func=AF.Sqrt, bias=CONST[:, 0:1], scale=inv_n), R=[PB[banks[si]], bCONST], W=[rsb], short=sh)
            S.add("dve", lambda e: e.reciprocal(rsv[:, 0:w], rsv[:, 0:w]), R=[rsb], W=[rsb], short=sh)
            for k in range(nk):
                S.add("dve", lambda e, k=k: e.scalar_tensor_tensor(dst[:, k, 0:w], src[:, k, 0:w], gain_ap[:, k:k + 1], rsv[:, 0:w], ALU.mult, ALU.mult), R=[srcb, rsb, bSM], W=[dstb], short=sh)

        def seg3(ap2d):
            return ap2d.rearrange("p (s c) -> p s c", s=2)

        def ffn(l):
            for g in range(NG):
                for jj in range(GJ):
                    j = g * GJ + jj
                    sl = Wst.get(("wgu", l, j), wd[("wgu", l)][j], 4096)
                    wv = WS[:, sl, :].rearrange("p (a k m) -> p a k m", a=2, k=16)
                    for half, b0 in ((0, 0), (1, 2)):
                        for k in range(KC):
                            for si, (a, b) in enumerate(SEGS):
                                mm(PS[:, b0 + si, 0:SEG], wv[:, half, k, :], HNX[:, k, a:b], k == 0, k == KC - 1, [WSB[sl], bHNX], [PB[b0 + si]])
                    ss = jj % 2
                    S.add("act", lambda e, ss=ss: e.activation(out=seg3(SIL[:, ss, :]), in_=PS[:, 0:2, 0:SEG], func=AF.Silu), R=[PB[0], PB[1]], W=[bSIL[ss]])
                    S.add("dve", lambda e, ss=ss, jj=jj: e.tensor_tensor(seg3(AT[:, jj, :]), seg3(SIL[:, ss, :]), PS[:, 2:4, 0:SEG], ALU.mult), R=[bSIL[ss], PB[2], PB[3]], W=[bAT])
                for ocp in range(8):
                    sl = Wst.get(("wdn", l, g, ocp), wd[("wdn", l)][g * 8 + ocp], 2816)
                    wv = WS[:, sl, 0:2816].rearrange("p (a j m) -> p a j m", a=2, j=GJ)
                    for o2 in range(2):
                        oc = ocp * 2 + o2
                        b0 = 4 + 2 * (oc % 2)
                        for jj in range(GJ):
                            for si, (a, b) in enumerate(SEGS):
                                mm(PS[:, b0 + si, 0:SEG], wv[:, o2, jj, :], AT[:, jj, a:b], jj == 0, jj == GJ - 1, [WSB[sl], bAT], [PB[b0 + si]])
                        S.add("dve", lambda e, oc=oc, b0=b0: e.tensor_tensor(seg3(HT[:, oc, :]), seg3(HT[:, oc, :]), PS[:, b0:b0 + 2, 0:SEG], ALU.add), R=[bHT, PB[b0], PB[b0 + 1]], W=[bHT])

        def proj_residual(key, blocks):
            for oc in range(16):
                sl = Wst.get((key, oc), blocks[oc], 2048)
                wv = WS[:, sl, 0:2048].rearrange("p (k m) -> p k m", k=16)
                b0 = 4 + 2 * (oc % 2)
                for k in range(KC):
                    for si, (a, b) in enumerate(SEGS):
                        mm(PS[:, b0 + si, 0:SEG], wv[:, k, :], HNX[:, k, a:b], k == 0, k == KC - 1, [WSB[sl], bHNX], [PB[b0 + si]])
                S.add("dve", lambda e, oc=oc, b0=b0: e.tensor_tensor(seg3(HT[:, oc, :]), seg3(HT[:, oc, :]), PS[:, b0:b0 + 2, 0:SEG], ALU.add), R=[bHT, PB[b0], PB[b0 + 1]], W=[bHT])

        def tail_tile(l, t, hdst3, hdst_name, is_final):
            rmsnorm(HT, bHT, TILE, KC, 1.0 / D, GAINS[:, 4 + l, :], HNX, bHNX, (0, 1), RS[:, 0, :], bRS[0])
            ffn(l)
            if is_final:
                rmsnorm(HT, bHT, TILE, KC, 1.0 / D, GAINS[:, 8, :], HT, bHT, (0, 1), RS[:, 0, :], bRS[0])
            dv = dtile(hdst3, hdst_name, t)
            S.add("sp", lambda e: e.dma_start(out=dv.ap, in_=HT[:, :, :]), R=[bHT], W=[dv], dma=True)

        def mla_layer(l, hsrc3, hsrc_name, hdst3, hdst_name, is_final):
            j = l // 2
            lats, latr, latp = LATS[j], LATR[j], LATP[j]
            cq3 = pkn(CQD[:, :])
            lato3 = pkn(LATO[:, :])
            for t in range(NTILES):
                sv = dtile(hsrc3, hsrc_name, t)
                S.add("sp", lambda e, sv=sv: e.dma_start(out=HT[:, :, :], in_=sv.ap), R=[sv], W=[bHT], dma=True)
                c0, c1 = t * TILE, (t + 1) * TILE
                S.add("sp", lambda e, c0=c0, c1=c1: e.dma_start(out=SIL[0:64, 0, :], in_=cosd[:, c0:c1]), W=[bSIL[0]], dma=True)
                S.add("sp", lambda e, c0=c0, c1=c1: e.dma_start(out=SIL[0:64, 1, :], in_=sind[:, c0:c1]), W=[bSIL[1]], dma=True)
                rmsnorm(HT, bHT, TILE, KC, 1.0 / D, GAINS[:, l, :], HNX, bHNX, (0, 1), RS[:, 0, :], bRS[0])
                CQ = [af32(i) for i in range(4)]
                CKV = [af32(4 + i) for i in range(4)]
                CQN = abf(8, 2)
                LATT = abf(10, 3)
                TM1, TM2 = af32(13), af32(14)
                RSQ = af32(15)
                cqn3 = CQN.ap[:, 0:4 * TILE].rearrange("p (k n) -> p k n", k=4)
                latt3 = LATT.ap[:, 0:5 * TILE].rearrange("p (k n) -> p k n", k=5)
                for blk in range(8):
                    sl = Wst.get(("win", j, blk), wd[("win", j)][blk], 2048)
                    wv = WS[:, sl, 0:2048].rearrange("p (k m) -> p k m", k=16)
                    b0 = 4 + 2 * (blk % 2)
                    for k in range(KC):
                        for si, (a, b) in enumerate(SEGS):
                            mm(PS[:, b0 + si, 0:SEG], wv[:, k, :], HNX[:, k, a:b], k == 0, k == KC - 1, [WSB[sl], bHNX], [PB[b0 + si]])
                    dstv = CQ[blk] if blk < 4 else CKV[blk - 4]
                    S.add("act", lambda e, dstv=dstv, b0=b0: e.activation(out=seg3(dstv.ap[:, 0:TILE]), in_=PS[:, b0:b0 + 2, 0:SEG], func=AF.Copy), R=[PB[b0], PB[b0 + 1]], W=[dstv])
                sl = Wst.get(("win", j, 8), wd[("win", j)][8], 2048)
                wv = WS[:, sl, 0:2048].rearrange("p (k m) -> p k m", k=16)
                for which, b0 in ((0, 0), (1, 2)):
                    for k in range(KC):
                        for si, (a, b) in enumerate(SEGS):
                            mm(PS[0:64, b0 + si, 0:SEG], wv[:, k, which * 64:(which + 1) * 64], HNX[:, k, a:b], k == 0, k == KC - 1, [WSB[sl], bHNX], [PB[b0 + si]])
                S.add("dve", lambda e, TM1=TM1: e.tensor_tensor(seg3(TM1.ap[0:64, 0:TILE]), PS[0:64, 0:2, 0:SEG], seg3(SIL[0:64, 0, :]), ALU.mult), R=[PB[0], PB[1], bSIL[0]], W=[TM1])
                S.add("dve", lambda e, TM2=TM2: e.tensor_tensor(seg3(TM2.ap[0:64, 0:TILE]), PS[0:64, 2:4, 0:SEG], seg3(SIL[0:64, 1, :]), ALU.mult), R=[PB[2], PB[3], bSIL[1]], W=[TM2])
                S.add("dve", lambda e, TM1=TM1, TM2=TM2, latt3=latt3: e.tensor_tensor(latt3[0:64, 4, :], TM1.ap[0:64, 0:TILE], TM2.ap[0:64, 0:TILE], ALU.add), R=[TM1, TM2], W=[LATT])
                for which, (srcs, dst3, goff) in enumerate(((CQ, cqn3, 0), (CKV, latt3, 4))):
                    for k in range(4):
                        sl2 = k % 2
                        S.add("act", lambda e, k=k, sl2=sl2, srcs=srcs: e.activation(out=SQ[:, sl2, :], in_=srcs[k].ap[:, 0:TILE], func=AF.Square), R=[srcs[k]], W=[bSQ[sl2]])
                        for si, (a, b) in enumerate(SEGS):
                            mm(PS[:, si, 0:SEG], ONES[:, :], SQ[:, sl2, a:b], k == 0, k == 3, [bCONST, bSQ[sl2]], [PB[si]])
                    S.add("act", lambda e, RSQ=RSQ: e.activation(out=seg3(RSQ.ap[:, 0:TILE]), in_=PS[:, 0:2, 0:SEG], func=AF.Sqrt, bias=CONST[:, 0:1], scale=1.0 / 512), R=[PB[0], PB[1], bCONST], W=[RSQ])
                    S.add("dve", lambda e, RSQ=RSQ: e.reciprocal(RSQ.ap[:, 0:TILE], RSQ.ap[:, 0:TILE]), R=[RSQ], W=[RSQ])
                    dstV = CQN if which == 0 else LATT
                    for k in range(4):
                        S.add("dve", lambda e, k=k, srcs=srcs, dst3=dst3, goff=goff, RSQ=RSQ: e.scalar_tensor_tensor(dst3[:, k, :], srcs[k].ap[:, 0:TILE], MG[:, j, goff + k:goff + k + 1], RSQ.ap[:, 0:TILE], ALU.mult, ALU.mult), R=[srcs[k], RSQ, bSM], W=[dstV])
                cqv = dtile(cq3, "cqd", t)
                S.add("sp", lambda e, cqv=cqv, cqn3=cqn3: e.dma_start(out=cqv.ap, in_=cqn3), R=[CQN], W=[cqv], dma=True)
                lov = dtile(lato3, "lato", t)
                S.add("sp", lambda e, lov=lov, latt3=latt3: e.dma_start(out=lov.ap, in_=latt3), R=[LATT], W=[lov], dma=True)
                f0 = max(c0, NMETA)
                for k in range(5):
                    lsv = V(lats[k][:, f0 - NMETA:c1 - NMETA], [buf("lats%d_%d_t%d" % (j, k, t))])
                    S.add("sp", lambda e, lsv=lsv, latt3=latt3, f0=f0, c0=c0, k=k: e.dma_start(out=lsv.ap, in_=latt3[:, k, f0 - c0:TILE]), R=[LATT], W=[lsv], dma=True)
            if DEBUG.get('stop') == 'p1':
                return
            blr = buf("latp%d" % j)
            for k in range(5):
                brk = buf("latr%d_%d" % (j, k))
                S.add("pool", lambda e, k=k: e.collective_compute("AllGather", ALU.bypass, replica_groups=RGROUPS,
                                                             ins=[lats[k].ap().opt()], outs=[latr[k].ap().opt()]),
                      R=[buf("lats%d_%d_t%d" % (j, k, t)) for t in range(NTILES)], W=[brk], coll=True)
                bpk = buf("latp%d_%d" % (j, k))
                S.add("sp", lambda e, k=k: e.dma_start(out=latp[k * 128:(k + 1) * 128, :], in_=latr[k][0:128, :]), R=[brk], W=[bpk], dma=True)
            allp = [buf("latp%d_%d" % (j, k)) for k in range(5)]
            latr3 = pkn(latp[:, :])
            if DEBUG.get('stop') == 'xch':
                return
            CQNin = abf(0, 2)
            LATin = abf(2, 2)
            QN, QR = abf(4, 2), abf(6, 2)
            KNo, KNp = abf(8, 2), abf(10, 2)
            Vo, Vp = abf(12, 2), abf(14, 2)
            KRo, KRp = abf(16, 2), abf(18, 2)
            PTs = [V(ARENA[:, 20 * ASLOT:22 * ASLOT].bitcast(BF16)[:, i * 512:(i + 1) * 512], [buf("pt%d" % i), AB[20], AB[21]]) for i in range(4)]
            for p_ in PTs:
                p_.bufs = [p_.bufs[0]]
            OO = abf(22, 1)
            RD = af32(23)
            PTS = af32(24)
            PTS2 = af32(0)
            PTSB = abf(25, 1)
            HNXf = HNX[:, :, :].rearrange("p k n -> p (k n)")
            ATf = AT[:, :, :].rearrange("p j n -> p (j n)").bitcast(F32)
            CQI = [V(HNXf[:, b_ * 2048:(b_ + 1) * 2048].rearrange("p (k n) -> p k n", k=4), [buf("cqin%d" % b_)]) for b_ in range(2)]
            LTI = [V(HNXf[:, 4096 + b_ * 2048:4096 + (b_ + 1) * 2048].rearrange("p (k n) -> p k n", k=4), [buf("latin%d" % b_)]) for b_ in range(2)]
            COSB = [V(ATf[0:64, b_ * 512:(b_ + 1) * 512], [buf("cosb%d" % b_)]) for b_ in range(2)]
            SINB = [V(ATf[0:64, 1024 + b_ * 512:1024 + (b_ + 1) * 512], [buf("sinb%d" % b_)]) for b_ in range(2)]
            S.add("dve", lambda e: e.memset(HNXf[0:1, 0:2], 0.0), W=[bHNX] + CQI + LTI)
            S.add("dve", lambda e: e.memset(ATf[0:1, 0:1], 0.0), W=[bAT] + COSB + SINB)
            gctr = [0]
            vo3 = Vo.ap[:, 0:17 * 128].rearrange("p (b d) -> p b d", b=17)
            vp3 = Vp.ap[:, 0:16 * 128].rearrange("p (b d) -> p b d", b=16)
            allat = [buf("lato_t%d" % t) for t in range(NTILES)]
            allcq = [buf("cqd_t%d" % t) for t in range(NTILES)]
            S.add("sp", lambda e: e.dma_start(out=KRo.ap[0:64, 0:NT], in_=LATO[512:576, :]), R=allat, W=[KRo, AB[20], AB[21]], dma=True)
            S.add("sp", lambda e: e.dma_start(out=KRp.ap[0:64, 0:NFR], in_=latp[512:576, :]), R=allp, W=[KRp], dma=True)
            QG = [(0, 128)] + [(NMETA + 512 * g, NMETA + 512 * (g + 1)) for g in range(4)]
            otd3 = OTD[:, :]
            for h in range(DEBUG.get('nh', NH)):
                sl = Wst.get(("whd", j, h), wd[("whd", j)][h], 2048)
                wq = WS[:, sl, 0:1024].rearrange("p (k m) -> p k m", k=4)
                wkv = WS[:, sl, 1024:2048].rearrange("p (k m) -> p k m", k=4)
                wb = WSB[sl]
                for gi, (c0, c1) in enumerate(QG):
                    w = c1 - c0
                    pb_ = gctr[0] % 2
                    gctr[0] += 1
                    CQNin, LATin, COSv, SINv = CQI[pb_], LTI[pb_], COSB[pb_], SINB[pb_]
                    cqin3, latin3 = CQNin.ap, LATin.ap
                    S.add("sp", lambda e, c0=c0, c1=c1, w=w, cqin3=cqin3: e.dma_start(out=cqin3[:, :, 0:w], in_=cq3[:, :, c0:c1]), R=allcq, W=[CQNin], dma=True)
                    S.add("sp", lambda e, c0=c0, c1=c1, w=w, COSv=COSv: e.dma_start(out=COSv.ap[:, 0:w], in_=cosd[:, c0:c1]), W=[COSv], dma=True)
                    S.add("sp", lambda e, c0=c0, c1=c1, w=w, SINv=SINv: e.dma_start(out=SINv.ap[:, 0:w], in_=sind[:, c0:c1]), W=[SINv], dma=True)
                    wl = 128 if gi == 0 else w
                    S.add("sp", lambda e, c0=c0, wl=wl, latin3=latin3: e.dma_start(out=latin3[:, 0:4, 0:wl], in_=lato3[:, 0:4, c0:c0 + wl]), R=allat, W=[LATin], dma=True)
                    for k in range(4):
                        mm(PS[:, 5, 0:w], wq[:, k, 0:128], cqin3[:, k, 0:w], k == 0, k == 3, [wb, CQNin], [PB[5]])
                    S.add("act", lambda e, c0=c0, c1=c1, w=w: e.activation(out=QN.ap[:, c0:c1], in_=PS[:, 5, 0:w], func=AF.Copy), R=[PB[5]], W=[QN])
                    for k in range(4):
                        mm(PS[0:64, 6, 0:w], wq[:, k, 128:192], cqin3[:, k, 0:w], k == 0, k == 3, [wb, CQNin], [PB[6]])
                    for k in range(4):
                        mm(PS[0:64, 7, 0:w], wq[:, k, 192:256], cqin3[:, k, 0:w], k == 0, k == 3, [wb, CQNin], [PB[7]])
                    S.add("dve", lambda e, w=w, COSv=COSv: e.tensor_tensor(SIL[0:64, 1, 0:w], PS[0:64, 6, 0:w], COSv.ap[:, 0:w], ALU.mult), R=[PB[6], COSv], W=[bSIL[1]])
                    S.add("dve", lambda e, w=w, SINv=SINv: e.tensor_tensor(RS[0:64, 0, 0:w], PS[0:64, 7, 0:w], SINv.ap[:, 0:w], ALU.mult), R=[PB[7], SINv], W=[bRS[0]])
                    S.add("dve", lambda e, c0=c0, c1=c1, w=w: e.tensor_tensor(QR.ap[0:64, c0:c1], SIL[0:64, 1, 0:w], RS[0:64, 0, 0:w], ALU.add), R=[bSIL[1], bRS[0]], W=[QR])
                    for k in range(4):
                        mm(PS[:, 5, 0:w], wkv[:, k, 0:128], latin3[:, k, 0:w], k == 0, k == 3, [wb, LATin], [PB[5]])
                    S.add("act", lambda e, c0=c0, c1=c1, w=w: e.activation(out=KNo.ap[:, c0:c1], in_=PS[:, 5, 0:w], func=AF.Copy), R=[PB[5]], W=[KNo])
                    if gi == 0:
                        for k in range(4):
                            mm(PS[:, 6, 0:128], latin3[:, k, 0:128], wkv[:, k, 128:256], k == 0, k == 3, [wb, LATin], [PB[6]])
                        S.add("dve", lambda e: e.tensor_copy(vo3[:, 0, :], PS[:, 6, 0:128]), R=[PB[6]], W=[Vo])
                    else:
                        for bb in range(4):
                            for k in range(4):
                                mm(PS[:, 6, bb * 128:(bb + 1) * 128], latin3[:, k, bb * 128:(bb + 1) * 128], wkv[:, k, 128:256], k == 0, k == 3, [wb, LATin], [PB[6]])
                        vb0 = 1 + 4 * (gi - 1)
                        S.add("dve", lambda e, vb0=vb0: e.tensor_copy(vo3[:, vb0:vb0 + 4, :].rearrange("p b d -> p (b d)"), PS[:, 6, :]), R=[PB[6]], W=[Vo])
                for g in range(4):
                    c0, c1 = 512 * g, 512 * (g + 1)
                    pb_ = gctr[0] % 2
                    gctr[0] += 1
                    LATin = LTI[pb_]
                    latin3 = LATin.ap
                    S.add("sp", lambda e, c0=c0, c1=c1, latin3=latin3: e.dma_start(out=latin3[:, 0:4, :], in_=latr3[:, 0:4, c0:c1]), R=allp, W=[LATin], dma=True)
                    for k in range(4):
                        mm(PS[:, 5, :], wkv[:, k, 0:128], latin3[:, k, :], k == 0, k == 3, [wb, LATin], [PB[5]])
                    S.add("act", lambda e, c0=c0, c1=c1: e.activation(out=KNp.ap[:, c0:c1], in_=PS[:, 5, :], func=AF.Copy), R=[PB[5]], W=[KNp])
                    for bb in range(4):
                        for k in range(4):
                            mm(PS[:, 6, bb * 128:(bb + 1) * 128], latin3[:, k, bb * 128:(bb + 1) * 128], wkv[:, k, 128:256], k == 0, k == 3, [wb, LATin], [PB[6]])
                    S.add("dve", lambda e, g=g: e.tensor_copy(vp3[:, 4 * g:4 * g + 4, :].rearrange("p b d -> p (b d)"), PS[:, 6, :]), R=[PB[6]], W=[Vp])
                for gi, (c0, c1) in enumerate(QG):
                    w = c1 - c0
                    if gi not in DEBUG.get('groups', range(5)):
                        continue
                    kbl = []
                    if gi == 0:
                        kbl.append((KNo, KRo, vo3, Vo, 0, 128, 0, 0, False, 'meta'))
                    else:
                        g = gi - 1
                        for pb in range(16):
                            kbl.append((KNp, KRp, vp3, Vp, pb * 128, 128, pb, 0, False, 'prev'))
                        kbl.append((KNo, KRo, vo3, Vo, 0, 128, 0, 0, False, 'meta'))
                        for i in range(4 * g + 4):
                            kbl.append((KNo, KRo, vo3, Vo, NMETA + 128 * i, 128, 1 + i, max(0, 128 * (i - 4 * g)), i >= 4 * g, None))
                    if DEBUG.get('kfilter'):
                        kf = DEBUG['kfilter']
                        kbl = [kb for kb in kbl if (('p' in kf and kb[9] == 'prev') or ('m' in kf and kb[9] == 'meta') or ('d' in kf and kb[8]) or ('o' in kf and (not kb[9]) and kb[5] == 128 and not kb[8]))]
                    if DEBUG.get('fullcols'):
                        kbl = [kb[:7] + (0,) + kb[8:] for kb in kbl]
                    if DEBUG.get('nodiag'):
                        kbl = [kb[:8] + (False,) + kb[9:] for kb in kbl]
                    nb = len(kbl)
                    ob, db = (3, 4) if gi % 2 == 0 else (5, 6)

                    def qk(bi):
                        KNv, KRv, v3, Vv, k0, kw, vb, cs, diag, prevb = kbl[bi]
                        sb_ = bi % 3
                        mm(PS[0:kw, sb_, cs:w], KNv.ap[:, k0:k0 + kw], QN.ap[:, c0 + cs:c1], True, False, [KNv, QN], [PB[sb_]])
                        mm(PS[0:kw, sb_, cs:w], KRv.ap[0:64, k0:k0 + kw], QR.ap[0:64, c0 + cs:c1], False, True, [KRv, QR], [PB[sb_]])
                        pt = PTs[bi % 4]
                        o_ = pt.ap[0:kw, cs:w]
                        i_ = PS[0:kw, sb_, cs:w]
                        if prevb:
                            bi_ = PM[0:kw, 2:3] if prevb == 'prev' else PM[0:kw, 3:4]
                            S.add("act", lambda e, o_=o_, i_=i_, bi_=bi_: e.activation(out=o_, in_=i_, func=AF.Exp, bias=bi_, scale=SCALE), R=[PB[sb_], bSM], W=[pt])
                        else:
                            bi_ = CONST[0:kw, 2:3]
                            S.add("act", lambda e, o_=o_, i_=i_, bi_=bi_: e.activation(out=o_, in_=i_, func=AF.Exp, bias=bi_, scale=SCALE), R=[PB[sb_], bCONST], W=[pt])
                            if diag:
                                z_ = pt.ap[64:128, cs:cs + 64]
                                S.add("act", lambda e, z_=z_: e.memzero(z_), W=[pt])

                    def pv(bi):
                        KNv, KRv, v3, Vv, k0, kw, vb, cs, diag, prevb = kbl[bi]
                        pt = PTs[bi % 4]
                        first = bi == 0
                        last = bi == nb - 1
                        if True:
                            mm(PS[:, ob, cs:w], v3[0:kw, vb, :], pt.ap[0:kw, cs:w], first, last, [Vv, pt], [PB[ob]])
                            acc, aeng = (PTS, "pool") if bi % 2 == 0 else (PTS2, "dve")
                            po_, pi_ = acc.ap[:, cs:w], pt.ap[:, cs:w]
                            if bi < 2:
                                S.add(aeng, lambda e, po_=po_, pi_=pi_: e.tensor_copy(po_, pi_), R=[pt], W=[acc])
                            else:
                                S.add(aeng, lambda e, po_=po_, pi_=pi_: e.tensor_tensor(po_, po_, pi_, ALU.add), R=[pt, acc], W=[acc])
                        else:
                            mm(PS[:, 3, cs:w], v3[0:64, vb, :], pt.ap[0:64, cs:w], first, False, [Vv, pt], [PB[3]])
                            mm(PS[:, 4, cs:w], ONES[0:64, :], pt.ap[0:64, cs:w], first, False, [bCONST, pt], [PB[4]])
                            mm(PS[:, 3, cs + 64:w], v3[64:128, vb, :], pt.ap[64:128, cs + 64:w], False, last, [Vv, pt], [PB[3]])
                            mm(PS[:, 4, cs + 64:w], ONES[64:128, :], pt.ap[64:128, cs + 64:w], False, last, [bCONST, pt], [PB[4]])

                    qk(0)
                    if nb > 1:
                        qk(1)
                    for bi in range(nb):
                        if bi + 2 < nb:
                            qk(bi + 2)
                        pv(bi)
                    if nb > 1:
                        S.add("dve", lambda e, w=w: e.tensor_tensor(PTSB.ap[:, 0:w], PTS.ap[:, 0:w], PTS2.ap[:, 0:w], ALU.add), R=[PTS, PTS2], W=[PTSB])
                    else:
                        S.add("dve", lambda e, w=w: e.tensor_copy(PTSB.ap[:, 0:w], PTS.ap[:, 0:w]), R=[PTS], W=[PTSB])
                    mm(PS[:, db, 0:w], ONES[:, :], PTSB.ap[:, 0:w], True, True, [bCONST, PTSB], [PB[db]])
                    S.add("dve", lambda e, w=w, db=db: e.reciprocal(RD.ap[:, 0:w], PS[:, db, 0:w]), R=[PB[db]], W=[RD])
                    S.add("dve", lambda e, w=w, ob=ob: e.tensor_tensor(OO.ap[:, 0:w], PS[:, ob, 0:w], RD.ap[:, 0:w], ALU.mult), R=[PB[ob], RD], W=[OO])
                    ws_ = NMETA if gi == 0 else w
                    ov = V(otd3[h * 128:(h + 1) * 128, c0:c0 + ws_], [buf("otd_h%d_g%d" % (h, gi))])
                    S.add("sp", lambda e, ov=ov, ws_=ws_: e.dma_start(out=ov.ap, in_=OO.ap[:, 0:ws_]), R=[OO], W=[ov], dma=True)
            if DEBUG.get('stop') == 'p2':
                return
            for p_ in PTs:
                S.add("dve", lambda e, p_=p_: e.memset(p_.ap[0:1, 0:1], 0.0), W=[p_, AB[20], AB[21]])
            S.add("dve", lambda e: e.memset(HNXf[0:1, 0:2], 0.0), W=[bHNX] + CQI + LTI)
            S.add("dve", lambda e: e.memset(ATf[0:1, 0:1], 0.0), W=[bAT] + COSB + SINB)
            allot = [buf("otd_h%d_g%d" % (h, gi)) for h in range(NH) for gi in range(5)]
            ot3 = pkn(OTD[:, :])
            for t in range(NTILES):
                sv = dtile(hsrc3, hsrc_name, t)
                S.add("sp", lambda e, sv=sv: e.dma_start(out=HT[:, :, :], in_=sv.ap), R=[sv], W=[bHT], dma=True)
                c0, c1 = t * TILE, (t + 1) * TILE
                S.add("sp", lambda e, c0=c0, c1=c1: e.dma_start(out=HNX[:, :, :], in_=ot3[:, :, c0:c1]), R=allot, W=[bHNX], dma=True)
                proj_residual(("wo", j), wd[("wo", j)])
                tail_tile(l, t, hdst3, hdst_name, is_final)

        def lru_layer(l, hsrc3, hsrc_name, hsrc2d, hdst3, hdst_name, is_final):
            j = l // 2
            hsd, hrd, ssd, srd = HSD[j], HRD[j], SSD[j], SRD[j]
            bhs, bhr = buf("hsd%d" % j), buf("hrd%d" % j)
            lastsrc = buf("%s_t%d" % (hsrc_name, NTILES - 1))
            hsd2 = hsd[:, :].rearrange("a (b c) -> (a b) c", c=64)
            hrd2 = hrd[0:128, :].rearrange("a (b c) -> (a b) c", c=64)
            S.add("sp", lambda e: e.dma_start(out=hsd2[:, :], in_=hsrc2d[:, NT - 64:NT]), R=[lastsrc], W=[bhs], dma=True)
            S.add("pool", lambda e: e.collective_compute("AllGather", ALU.bypass, replica_groups=RGROUPS,
                                                         ins=[hsd.ap().opt()], outs=[hrd.ap().opt()]), R=[bhs], W=[bhr], coll=True)
            S.add("sp", lambda e: e.dma_start(out=HH[:, :, :], in_=pkn(hrd2)), R=[bhr], W=[bHH], dma=True)
            rmsnorm(HH, bHH, 64, KC, 1.0 / D, GAINS[:, l, :], HHN, bHHN, (0,), RS[:, 1, :], bRS[1])
            if DEBUG.get('lstop') == 'halo':
                return
            GWv = abf(22, 3)
            S.add("pool", lambda e: e.dma_start(out=GWv.ap[:, 0:4096], in_=wd[("lgw", j)][0]), W=[GWv], dma=True)
            gw4 = GWv.ap[:, 0:4096].rearrange("p (a c m) -> p a c m", a=2, c=16)
            g03 = pkn(G0D[:, :])
            g13 = pkn(G1D[:, :])
            CWv = LRP[:, j, :, :]
            S1 = LS[:, j, 0, :]
            H1 = LS[:, j, 1, :]
            for t in range(DEBUG.get('ltiles', NTILES)):
                sv = dtile(hsrc3, hsrc_name, t)
                S.add("sp", lambda e, sv=sv: e.dma_start(out=HT[:, :, :], in_=sv.ap), R=[sv], W=[bHT], dma=True)
                rmsnorm(HT, bHT, TILE, KC, 1.0 / D, GAINS[:, l, :], HNX, bHNX, (4, 5), RS[:, 0, :], bRS[0])
                c0 = t * TILE
                def lru_views(c):
                        st = (c % 2) * 10
                        XB, Y, XC, RG, IG, AA, TH, H0, AC, OM = [af32(st + i) for i in range(10)]
                        XCB = abf(20 + (c % 2), 1)
                        return XB, Y, XC, RG, IG, AA, TH, H0, AC, OM, XCB

                def stageA(c, t=t, c0=c0):
                        XB, Y, XC, RG, IG, AA, TH, H0, AC, OM, XCB = lru_views(c)
                        sl = Wst.get(("lin", j, c), wd[("lin", j)][c], 4096)
                        wv = WS[:, sl, :].rearrange("p (a k m) -> p a k m", a=2, k=16)
                        for half, b0 in ((0, 0), (1, 2)):
                            for k in range(KC):
                                for si, (a, b) in enumerate(SEGS):
                                    mm(PS[:, b0 + si, 0:SEG], wv[:, half, k, :], HNX[:, k, a:b], k == 0, k == KC - 1, [WSB[sl], bHNX], [PB[b0 + si]])
                        if t == 0 and not DEBUG.get('nohalo'):
                            for k in range(KC):
                                mm(PS[:, 3, 384:448], wv[:, 0, k, :], HHN[:, k, 0:64], k == 0, k == KC - 1, [WSB[sl], bHHN], [PB[3]])
                            S.add("act", lambda e, c=c: e.activation(out=HXR[:, c, :], in_=PS[:, 3, 445:448], func=AF.Copy, scale=PM[:, 1:2]), R=[PB[3], bSM], W=[bHXR], short=True)
                        S.add("act", lambda e, Y=Y: e.activation(out=seg3(Y.ap[:, 0:TILE]), in_=PS[:, 2:4, 0:SEG], func=AF.Gelu_apprx_tanh), R=[PB[2], PB[3]], W=[Y])
                        S.add("act", lambda e, XB=XB: e.activation(out=seg3(XB.ap[:, 3:3 + TILE]), in_=PS[:, 0:2, 0:SEG], func=AF.Copy), R=[PB[0], PB[1]], W=[XB])

                        def conv(a, b, XB=XB, XC=XC, c=c, sh=False):
                            S.add("dve", lambda e: e.tensor_scalar(XC.ap[:, a:b], XB.ap[:, a + 3:b + 3], CWv[:, c, 3:4], CWv[:, c, 4:5], ALU.mult, ALU.add), R=[XB, bSM], W=[XC], short=sh)
                            for tap in range(3):
                                S.add("dve", lambda e, tap=tap: e.scalar_tensor_tensor(XC.ap[:, a:b], XB.ap[:, a + tap:b + tap], CWv[:, c, tap:tap + 1], XC.ap[:, a:b], ALU.mult, ALU.add), R=[XB, XC, bSM], W=[XC], short=sh)
                        if t == 0:
                            S.add("dve", lambda e, XB=XB: e.memset(XB.ap[:, 0:3], 0.0), W=[XB], short=True)
                            conv(0, NMETA, sh=True)
                            S.add("dve", lambda e, XB=XB, c=c: e.scalar_tensor_tensor(XB.ap[:, NMETA:NMETA + 3], XB.ap[:, NMETA:NMETA + 3], PM[:, 0:1], HXR[:, c, :], ALU.mult, ALU.add), R=[XB, bHXR, bSM], W=[XB], short=True)
                            conv(NMETA, TILE)
                        else:
                            S.add("dve", lambda e, XB=XB, c=c: e.tensor_copy(XB.ap[:, 0:3], XTAIL[:, c, :]), R=[bXTAIL], W=[XB], short=True)
                            conv(0, TILE)
                        S.add("dve", lambda e, XB=XB, c=c: e.tensor_copy(XTAIL[:, c, :], XB.ap[:, TILE:TILE + 3]), R=[XB], W=[bXTAIL], short=True)
                        S.add("act", lambda e, XC=XC, XCB=XCB: e.activation(out=XCB.ap[:, 0:TILE], in_=XC.ap[:, 0:TILE], func=AF.Copy), R=[XC], W=[XCB])

                def stageB(c, t=t, c0=c0):
                        XB, Y, XC, RG, IG, AA, TH, H0, AC, OM, XCB = lru_views(c)
                        for gate, b0 in ((0, 4), (1, 6)):
                            for si, (a, b) in enumerate(SEGS):
                                mm(PS[:, b0 + si, 0:SEG], gw4[:, gate, c, :], XCB.ap[:, a:b], True, True, [GWv, XCB], [PB[b0 + si]])
                        S.add("act", lambda e, RG=RG, c=c: e.activation(out=seg3(RG.ap[:, 0:TILE]), in_=PS[:, 4:6, 0:SEG], func=AF.Sigmoid, bias=CWv[:, c, 5:6], scale=1.0), R=[PB[4], PB[5], bSM], W=[RG])
                        S.add("act", lambda e, IG=IG, c=c: e.activation(out=seg3(IG.ap[:, 0:TILE]), in_=PS[:, 6:8, 0:SEG], func=AF.Sigmoid, bias=CWv[:, c, 6:7], scale=1.0), R=[PB[6], PB[7], bSM], W=[IG])
                        S.add("act", lambda e, RG=RG, AA=AA, c=c: e.activation(out=AA.ap[:, 0:TILE], in_=RG.ap[:, 0:TILE], func=AF.Exp, scale=S1[:, c:c + 1]), R=[RG, bLS], W=[AA])
                        S.add("act", lambda e, RG=RG, TH=TH, c=c: e.activation(out=TH.ap[:, 0:TILE], in_=RG.ap[:, 0:TILE], func=AF.Tanh, scale=H1[:, c:c + 1]), R=[RG, bLS], W=[TH])
                        S.add("dve", lambda e, AA=AA, TH=TH, OM=OM: e.scalar_tensor_tensor(OM.ap[:, 0:TILE], AA.ap[:, 0:TILE], 1.0, TH.ap[:, 0:TILE], ALU.add, ALU.mult), R=[AA, TH], W=[OM])
                        S.add("dve", lambda e, AA=AA, OM=OM: e.tensor_scalar(AA.ap[:, 0:TILE], OM.ap[:, 0:TILE], -1.0, 1.0, ALU.mult, ALU.add), R=[OM], W=[AA])
                        S.add("dve", lambda e, AA=AA, TH=TH, OM=OM: e.scalar_tensor_tensor(TH.ap[:, 0:TILE], AA.ap[:, 0:TILE], 1.0, OM.ap[:, 0:TILE], ALU.add, ALU.mult), R=[AA, OM], W=[TH])
                        S.add("act", lambda e, TH=TH: e.activation(out=TH.ap[:, 0:TILE], in_=TH.ap[:, 0:TILE], func=AF.Sqrt), R=[TH], W=[TH])
                        S.add("dve", lambda e, IG=IG, XC=XC: e.tensor_tensor(IG.ap[:, 0:TILE], IG.ap[:, 0:TILE], XC.ap[:, 0:TILE], ALU.mult), R=[IG, XC], W=[IG])
                        S.add("dve", lambda e, IG=IG, TH=TH: e.tensor_tensor(IG.ap[:, 0:TILE], IG.ap[:, 0:TILE], TH.ap[:, 0:TILE], ALU.mult), R=[IG, TH], W=[IG])
                        if t == 0:
                            S.add("dve", lambda e, AA=AA, IG=IG, H0=H0: e.tensor_tensor_scan(H0.ap[:, 0:NMETA], AA.ap[:, 0:NMETA], IG.ap[:, 0:NMETA], 0.0, ALU.mult, ALU.add), R=[AA, IG], W=[H0], short=True)
                            S.add("dve", lambda e, AC=AC: e.memset(AC.ap[:, 0:NMETA], 0.0), W=[AC], short=True)
                            S.add("dve", lambda e, H0=H0, c=c: e.tensor_copy(CAR[:, 2, c:c + 1], H0.ap[:, NMETA - 1:NMETA]), R=[H0], W=[bCAR], short=True)
                            S.add("dve", lambda e, AA=AA, IG=IG, H0=H0: e.tensor_tensor_scan(H0.ap[:, NMETA:TILE], AA.ap[:, NMETA:TILE], IG.ap[:, NMETA:TILE], 0.0, ALU.mult, ALU.add), R=[AA, IG], W=[H0])
                            S.add("dve", lambda e, AA=AA, AC=AC: e.tensor_tensor_scan(AC.ap[:, NMETA:TILE], AA.ap[:, NMETA:TILE], ZEROS[:, NMETA:TILE], 1.0, ALU.mult, ALU.add), R=[AA, bCONST], W=[AC])
                        else:
                            S.add("dve", lambda e, AA=AA, IG=IG, H0=H0, c=c: e.tensor_tensor_scan(H0.ap[:, 0:TILE], AA.ap[:, 0:TILE], IG.ap[:, 0:TILE], CAR[:, 0, c:c + 1], ALU.mult, ALU.add), R=[AA, IG, bCAR], W=[H0])
                            S.add("dve", lambda e, AA=AA, AC=AC, c=c: e.tensor_tensor_scan(AC.ap[:, 0:TILE], AA.ap[:, 0:TILE], ZEROS[:, 0:TILE], CAR[:, 1, c:c + 1], ALU.mult, ALU.add), R=[AA, bCONST, bCAR], W=[AC])
                        S.add("dve", lambda e, H0=H0, c=c: e.tensor_copy(CAR[:, 0, c:c + 1], H0.ap[:, TILE - 1:TILE]), R=[H0], W=[bCAR], short=True)
                        S.add("dve", lambda e, AC=AC, c=c: e.tensor_copy(CAR[:, 1, c:c + 1], AC.ap[:, TILE - 1:TILE]), R=[AC], W=[bCAR], short=True)
                        S.add("pool", lambda e, H0=H0, Y=Y: e.tensor_tensor(H0.ap[:, 0:TILE], H0.ap[:, 0:TILE], Y.ap[:, 0:TILE], ALU.mult), R=[H0, Y], W=[H0])
                        S.add("pool", lambda e, AC=AC, Y=Y: e.tensor_tensor(AC.ap[:, 0:TILE], AC.ap[:, 0:TILE], Y.ap[:, 0:TILE], ALU.mult), R=[AC, Y], W=[AC])
                        g0v = V(g03[:, c, c0:c0 + TILE], [buf("g0_%d_%d" % (t, c))])
                        g1v = V(g13[:, c, c0:c0 + TILE], [buf("g1_%d_%d" % (t, c))])
                        S.add("sp", lambda e, g0v=g0v, H0=H0: e.dma_start(out=g0v.ap, in_=H0.ap[:, 0:TILE]), R=[H0], W=[g0v], dma=True)
                        S.add("sp", lambda e, g1v=g1v, AC=AC: e.dma_start(out=g1v.ap, in_=AC.ap[:, 0:TILE]), R=[AC], W=[g1v], dma=True)

                ncx = DEBUG.get('nchunks', 16)
                stageA(0)
                for c in range(ncx):
                    if c + 1 < ncx:
                        stageA(c + 1)
                    stageB(c)

            if DEBUG.get('lstop') == 'p1':
                return
            S.add("dve", lambda e: e.tensor_tensor(CAR[:, 4, :], CAR[:, 1, :], CAR[:, 2, :], ALU.mult), R=[bCAR], W=[bCAR], short=True)
            S.add("dve", lambda e: e.tensor_tensor(CAR[:, 4, :], CAR[:, 4, :], CAR[:, 0, :], ALU.add), R=[bCAR], W=[bCAR], short=True)
            bss, bsr = buf("ssd%d" % j), buf("srd%d" % j)
            S.add("sp", lambda e: e.dma_start(out=ssd[:, :], in_=CAR[:, 4, :]), R=[bCAR], W=[bss], dma=True)
            S.add("pool", lambda e: e.collective_compute("AllGather", ALU.bypass, replica_groups=RGROUPS,
                                                         ins=[ssd.ap().opt()], outs=[srd.ap().opt()]), R=[bss], W=[bsr], coll=True)
            S.add("sp", lambda e: e.dma_start(out=CAR[:, 3, :], in_=srd[0:128, :]), R=[bsr], W=[bCAR], dma=True)
            S.add("dve", lambda e: e.tensor_scalar(CAR[:, 5, :], CAR[:, 2, :], PM[:, 0:1], 0.0, ALU.mult, ALU.add), R=[bCAR, bSM], W=[bCAR], short=True)
            S.add("dve", lambda e: e.scalar_tensor_tensor(CAR[:, 5, :], CAR[:, 3, :], PM[:, 1:2], CAR[:, 5, :], ALU.mult, ALU.add), R=[bCAR, bSM], W=[bCAR], short=True)
            if DEBUG.get('lstop') == 'x2':
                S.add("sp", lambda e: e.dma_start(out=hout[0:128, 0:96], in_=CAR[:, :, :].rearrange("p a b -> p (a b)")), R=[bCAR], W=[buf("hout_dbg")], dma=True)
                return
            for t in range(NTILES):
                sv = dtile(hsrc3, hsrc_name, t)
                S.add("sp", lambda e, sv=sv: e.dma_start(out=HT[:, :, :], in_=sv.ap), R=[sv], W=[bHT], dma=True)
                c0 = t * TILE
                for c in range(16):
                    A0, A1 = af32(2 * (c % 4)), af32(2 * (c % 4) + 1)
                    S.add("sp", lambda e, A0=A0, c=c, c0=c0: e.dma_start(out=A0.ap[:, 0:TILE], in_=g03[:, c, c0:c0 + TILE]), R=[buf("g0_%d_%d" % (t, c))], W=[A0], dma=True)
                    S.add("sp", lambda e, A1=A1, c=c, c0=c0: e.dma_start(out=A1.ap[:, 0:TILE], in_=g13[:, c, c0:c0 + TILE]), R=[buf("g1_%d_%d" % (t, c))], W=[A1], dma=True)
                    S.add("dve", lambda e, A0=A0, A1=A1, c=c: e.scalar_tensor_tensor(HNX[:, c, :], A1.ap[:, 0:TILE], CAR[:, 5, c:c + 1], A0.ap[:, 0:TILE], ALU.mult, ALU.add), R=[A0, A1, bCAR], W=[bHNX])
                proj_residual(("lwo", j), wd[("lwo", j)])
                tail_tile(l, t, hdst3, hdst_name, is_final)

        n = len(layer_ids)
        for i, l in enumerate(layer_ids):
            src2d = hin if i == 0 else RES[:, :]
            src_name = "hin" if i == 0 else "res"
            dst2d = hout if i == n - 1 else RES[:, :]
            dst_name = "hout" if i == n - 1 else "res"
            fin = final_norm and i == n - 1
            if l % 2 == 0:
                mla_layer(l, pkn(src2d), src_name, pkn(dst2d), dst_name, fin)
            else:
                lru_layer(l, pkn(src2d), src_name, src2d, pkn(dst2d), dst_name, fin)

    w0 = WStream(None)
    gen(Sched(), w0)
    S = Sched()
    w1 = WStream(w0.reqs)
    gen(S, w1)
    S.emit(nc, es)
    es.close()
    return nc


class WStream:
    def __init__(self, plan):
        self.plan = plan
        self.reqs = []
        self.i = 0
        self.issued = 0

    def bind(self, S, WS, WSB):
        self.S, self.WS, self.WSB = S, WS, WSB

    def get(self, key, src, n):
        i = self.i
        self.i += 1
        if self.plan is None:
            self.reqs.append((key, src, n))
            return i % NS
        plan = self.plan
        assert plan[i][0] == key, (plan[i][0], key)
        while self.issued < min(len(plan), i + 1 + LA):
            k, s_ap, nn = plan[self.issued]
            slot = self.issued % NS
            WS = self.WS
            self.S.add("pool", lambda e, slot=slot, s_ap=s_ap, nn=nn: e.dma_start(out=WS[:, slot, 0:nn], in_=s_ap), W=[self.WSB[slot]], dma=True)
            self.issued += 1
        return i % NS


def _blk2(W, half, nblk):
    a = W[:, :half].reshape(16, 128, nblk, 128).transpose(2, 1, 0, 3)
    b = W[:, half:].reshape(16, 128, nblk, 128).transpose(2, 1, 0, 3)
    return np.ascontiguousarray(np.stack([a, b], axis=2).reshape(nblk, 128, 4096))


def _blk_sq(W):
    return np.ascontiguousarray(W.reshape(16, 128, 16, 128).transpose(2, 1, 0, 3).reshape(16, 128, 2048))


def _prep_weights(inp, layer_ids):
    out = {}
    for l in layer_ids:
        j = l // 2
        out["wgu%d" % l] = _blk2(inp["ffn_w_gu"][l], DFF, NJ)
        wdn = inp["ffn_w_down"][l].reshape(NG, GJ, 128, 8, 2, 128).transpose(0, 3, 2, 4, 1, 5)
        out["wdn%d" % l] = np.ascontiguousarray(wdn.reshape(32, 128, 2816))
        if l % 2 == 0:
            w_in = inp["mla_w_in"][j]
            w3 = w_in.reshape(16, 128, 1088)
            blks = [w3[:, :, b * 128:(b + 1) * 128] for b in range(8)]
            rope = np.concatenate([w3[:, :, 1024:1088], w3[:, :, 1056:1088], w3[:, :, 1024:1056]], axis=2)
            blks.append(rope)
            out["win%d" % j] = np.ascontiguousarray(np.stack(blks, 0).transpose(0, 2, 1, 3).reshape(9, 128, 2048))
            wq = inp["mla_w_uq"][j].reshape(4, 128, NH, 192)
            wqh = np.concatenate([wq[..., 0:192], wq[..., 160:192], wq[..., 128:160]], axis=3)
            wkv = inp["mla_w_ukv"][j].reshape(4, 128, NH, 256)
            hd = np.concatenate([wqh.transpose(2, 1, 0, 3).reshape(NH, 128, 1024), wkv.transpose(2, 1, 0, 3).reshape(NH, 128, 1024)], axis=2)
            out["whd%d" % j] = np.ascontiguousarray(hd)
            out["wo%d" % j] = _blk_sq(inp["mla_w_o"][j])
        else:
            out["lin%d" % j] = _blk2(inp["lru_w_in"][j], 2048, 16)
            ga = inp["lru_w_gate_a"][j].transpose(1, 0, 2)
            gx = inp["lru_w_gate_x"][j].transpose(1, 0, 2)
            out["lgw%d" % j] = np.ascontiguousarray(np.stack([ga, gx], axis=1).reshape(1, 128, 4096))
            out["lwo%d" % j] = _blk_sq(inp["lru_w_o"][j])
    return out


def _prep_small(inp):
    def pk(v):
        return v.reshape(16, 128).T
    gains = np.stack([pk(inp["norm_mix"][l]) for l in range(4)] + [pk(inp["norm_ffn"][l]) for l in range(4)] + [pk(inp["norm_final"])], axis=1)
    mg = np.zeros((128, 2, 8), np.float32)
    for j in range(2):
        mg[:, j, 0:4] = inp["mla_q_norm"][j].reshape(4, 128).T
        mg[:, j, 4:8] = inp["mla_kv_norm"][j].reshape(4, 128).T
    lrp = np.zeros((128, 2, 16, 8), np.float32)
    for j in range(2):
        lrp[:, j, :, 0:4] = inp["lru_conv_w"][j].reshape(4, 16, 128).transpose(2, 1, 0)
        lrp[:, j, :, 4] = pk(inp["lru_conv_b"][j])
        lrp[:, j, :, 5] = inp["lru_b_gate_a"][j].T
        lrp[:, j, :, 6] = inp["lru_b_gate_x"][j].T
        lrp[:, j, :, 7] = pk(inp["lru_lambda"][j])
    return {"gains": np.ascontiguousarray(gains.reshape(128, 144), np.float32),
            "mg": np.ascontiguousarray(mg.reshape(128, 16)),
            "lrp": np.ascontiguousarray(lrp.reshape(128, 256))}


def _rope_tables(pos):
    inv_freq = (np.float32(10000.0) ** (-np.arange(0, 64, 2, dtype=np.float32) / np.float32(64))).astype(np.float32)
    ang = (pos.astype(np.float32)[None, :] * inv_freq[:, None]).astype(np.float32)
    c, s = np.cos(ang).astype(np.float32), np.sin(ang).astype(np.float32)
    return np.ascontiguousarray(np.concatenate([c, c], 0)), np.ascontiguousarray(np.concatenate([-s, s], 0))


_NC_CACHE = {}


def _get_nc(layer_ids, final, ncores=8):
    key = (tuple(layer_ids), final, ncores)
    if key not in _NC_CACHE:
        _NC_CACHE[key] = build(list(layer_ids), final, ncores)
    return _NC_CACHE[key]


def _core_consts(ncores):
    per = []
    for core in range(ncores):
        half = core % 2
        pos = np.concatenate([np.arange(NMETA), NMETA + half * NFR + np.arange(NFR)])
        cosd, sind = _rope_tables(pos)
        pm = np.zeros((128, 4), np.float32)
        pm[:, 0] = 1.0 if half == 0 else 0.0
        pm[:, 1] = 0.0 if half == 0 else 1.0
        pm[:, 2] = -30000.0 if half == 0 else 0.0
        pm[NMETA:, 3] = -30000.0
        per.append({"cosd": cosd, "sind": sind, "pm": pm})
    return per


def run_layers(inp, hT_list, layer_ids, final, ncores=8):
    nc = _get_nc(layer_ids, final, ncores)
    small = _prep_small(inp)
    wts = _prep_weights(inp, layer_ids)
    consts = _core_consts(ncores)
    in_maps = []
    for core in range(ncores):
        m = {"hin": hT_list[core]}
        m.update(consts[core])
        m.update(small)
        m.update(wts)
        in_maps.append(m)
    res = run_bass_kernel_spmd(nc, in_maps, core_ids=list(range(ncores)))
    return [r["hout"] for r in res.results]


def make_hT(x, meta_tokens, nb):
    hT = []
    metaT = np.ascontiguousarray(meta_tokens.T)
    for b in range(nb):
        xT = x[b].T
        for half in range(2):
            hT.append(np.ascontiguousarray(np.concatenate([metaT, xT[:, half * NFR:(half + 1) * NFR]], axis=1), dtype=np.float32))
    return hT


def kernel(**inp):
    inp = {k: np.asarray(v) for k, v in inp.items()}
    x = inp["x"]
    nb = x.shape[0]
    hT = make_hT(x, inp["meta_tokens"], nb)
    if FUSED:
        outs = run_layers(inp, hT, [0, 1, 2, 3], True, 2 * nb)
    else:
        outs = hT
        for l in range(4):
            outs = run_layers(inp, outs, [l], l == 3, 2 * nb)
    y = np.empty((nb, 2 * NFR, D), np.float32)
    for b in range(nb):
        for half in range(2):
            y[b, half * NFR:(half + 1) * NFR, :] = outs[2 * b + half][:, NMETA:].T
    return y
```

```python
import numpy as np
from contextlib import ExitStack
import concourse.bass as bass
import concourse.mybir as mybir
from concourse.bass_utils import run_bass_kernel_spmd

F32 = mybir.dt.float32
BF16 = mybir.dt.bfloat16
AF = mybir.ActivationFunctionType
ALU = mybir.AluOpType

D = 2048
KC = 16
NMETA = 16
NFR = 2048
NT = NMETA + NFR
TILE = 688
SEG = 344
NTILES = 3
SEGS = [(0, SEG), (SEG, TILE)]
DFF = 5632
NJ = 44
GJ = 11
NG = 4
NH = 16
ASLOT = 691
NASLOT = 26
NS = 4
LA = 3
SCALE = 192.0 ** -0.5
EPS = 1e-6
FUSED = True
DEBUG = {}


class Buf:
    __slots__ = ("name", "writer", "readers")

    def __init__(self, name=""):
        self.name = name
        self.writer = None
        self.readers = []


class V:
    __slots__ = ("ap", "bufs")

    def __init__(self, ap, bufs):
        self.ap = ap
        self.bufs = bufs


class Op:
    __slots__ = ("eng", "fn", "deps", "dma", "coll", "signals", "token", "prev", "short")


def _flat(vs):
    out = []
    for v in vs:
        if isinstance(v, Buf):
            out.append(v)
        elif isinstance(v, V):
            out.extend(v.bufs)
        else:
            out.extend(_flat(v))
    return out


class Sched:
    ENGS = ("pe", "act", "dve", "pool", "sp")

    def __init__(self):
        self.q = {e: [] for e in self.ENGS}

    def add(self, eng, fn, R=(), W=(), dma=False, coll=False, short=False):
        op = Op()
        op.short = short
        op.eng = eng
        op.fn = fn
        op.dma = dma or coll
        op.coll = coll
        op.signals = False
        op.token = None
        op.prev = None
        reads = _flat(R)
        writes = _flat(W)
        deps = {}
        for b in reads:
            if b.writer is not None:
                deps[id(b.writer)] = b.writer
        for b in writes:
            if b.writer is not None:
                deps[id(b.writer)] = b.writer
            for r in b.readers:
                deps[id(r)] = r
        op.deps = [d for d in deps.values() if d is not op and (d.dma or op.dma or d.eng != eng or d.short or op.short)]
        for b in reads:
            if not op.dma:
                b.readers = [r for r in b.readers if r.dma or r.eng != eng or r.short]
            b.readers.append(op)
        for b in writes:
            b.writer = op
            b.readers = []
        self.q[eng].append(op)
        return op

    def emit(self, nc, es):
        CH = 4000
        RR = 8
        for e in self.ENGS:
            for op in self.q[e]:
                for d in op.deps:
                    d.signals = True
        sems = {}

        def getsem(key):
            if key not in sems:
                sems[key] = es.enter_context(nc.semaphore("s%d" % len(sems)))
            return sems[key]

        ncoll = 0
        for e in self.ENGS:
            ccnt = 0
            dcnt = 0
            for op in self.q[e]:
                if op.coll:
                    op.token = (getsem(("coll", ncoll)), 1)
                    ncoll += 1
                elif op.dma:
                    s = getsem(("d" + e, dcnt % RR))
                    op.token = (s, 16 * (dcnt // RR + 1))
                    if dcnt >= RR:
                        op.prev = (s, 16 * (dcnt // RR))
                    dcnt += 1
                elif op.signals:
                    op.token = (getsem((e, ccnt // CH)), ccnt % CH + 1)
                    ccnt += 1
        self.nsems = len(sems)
        q = self.q

        def run(e, eng):
            waited = {}
            for op in q[e]:
                need = [d.token for d in op.deps]
                if op.prev is not None:
                    need.append(op.prev)
                for (s, v) in need:
                    if waited.get(id(s), 0) < v:
                        eng.wait_ge(s, v)
                        waited[id(s)] = v
                ins = op.fn(eng)
                if op.token is not None:
                    if op.coll:
                        ins.then_inc(op.token[0])
                    elif op.dma:
                        ins.then_inc(op.token[0], 16)
                    else:
                        ins.then_inc(op.token[0], 1)
            for op in q[e]:
                if op.dma and waited.get(id(op.token[0]), 0) < op.token[1]:
                    eng.wait_ge(op.token[0], op.token[1])
                    waited[id(op.token[0])] = op.token[1]

        with nc.Block() as block:
            @block.tensor
            def _(t):
                run("pe", t)

            @block.scalar
            def _(s):
                run("act", s)

            @block.vector
            def _(v):
                run("dve", v)

            @block.gpsimd
            def _(g):
                run("pool", g)

            @block.sync
            def _(sy):
                run("sp", sy)


def build(layer_ids, final_norm, ncores=8):
    RGROUPS = [[2 * i, 2 * i + 1] for i in range(ncores // 2)]
    nc = bass.Bass("TRN2", target_bir_lowering=False)
    es = ExitStack()

    def din(name, shape, dt=F32):
        return nc.dram_tensor(name, list(shape), dt, kind="ExternalInput").ap()

    hin = din("hin", [D, NT])
    cosd = din("cosd", [64, NT])
    sind = din("sind", [64, NT])
    pmd = din("pm", [128, 4])
    gaind = din("gains", [128, 9 * 16])
    mgd = din("mg", [128, 16])
    lrpd = din("lrp", [128, 2 * 16 * 8])
    hout = nc.dram_tensor("hout", [D, NT], F32, kind="ExternalOutput").ap()
    wd = {}
    for l in layer_ids:
        j = l // 2
        wd[("wgu", l)] = din("wgu%d" % l, [NJ, 128, 4096])
        wd[("wdn", l)] = din("wdn%d" % l, [32, 128, 2816])
        if l % 2 == 0:
            wd[("win", j)] = din("win%d" % j, [9, 128, 2048])
            wd[("whd", j)] = din("whd%d" % j, [NH, 128, 2048])
            wd[("wo", j)] = din("wo%d" % j, [16, 128, 2048])
        else:
            wd[("lin", j)] = din("lin%d" % j, [16, 128, 4096])
            wd[("lgw", j)] = din("lgw%d" % j, [1, 128, 4096])
            wd[("lwo", j)] = din("lwo%d" % j, [16, 128, 2048])

    RES = nc.dram_tensor("res", [D, NT], F32)
    CQD = nc.dram_tensor("cqd", [512, NT], BF16)
    LATO = nc.dram_tensor("lato", [640, NT], BF16)
    LATS = [[nc.dram_tensor("lats%d_%d" % (i, k), [128, NFR], BF16) for k in range(5)] for i in range(2)]
    LATR = [[nc.dram_tensor("latr%d_%d" % (i, k), [256, NFR], BF16) for k in range(5)] for i in range(2)]
    LATP = [nc.dram_tensor("latp%d" % i, [640, NFR], BF16) for i in range(2)]
    OTD = nc.dram_tensor("otd", [D, NT], BF16)
    G0D = nc.dram_tensor("g0d", [D, NT], F32)
    G1D = nc.dram_tensor("g1d", [D, NT], F32)
    HSD = [nc.dram_tensor("hsd%d" % i, [128, 1024], F32) for i in range(2)]
    HRD = [nc.dram_tensor("hrd%d" % i, [256, 1024], F32) for i in range(2)]
    SSD = [nc.dram_tensor("ssd%d" % i, [128, 16], F32) for i in range(2)]
    SRD = [nc.dram_tensor("srd%d" % i, [256, 16], F32) for i in range(2)]

    def sb(name, shape, dt):
        return es.enter_context(nc.sbuf_tensor(name, list(shape), dt))

    WS = sb("ws", [128, NS, 4096], BF16)
    HT = sb("ht", [128, KC, TILE], F32)
    HNX = sb("hnx", [128, KC, TILE], BF16)
    AT = sb("at", [128, GJ, TILE], BF16)
    SQ = sb("sq", [128, 2, TILE], BF16)
    RS = sb("rs", [128, 2, TILE], F32)
    SIL = sb("sil", [128, 2, TILE], F32)
    ONES = sb("ones", [128, 128], BF16)
    ZEROS = sb("zeros", [128, TILE], F32)
    CONST = sb("const", [128, 4], F32)
    GAINS = sb("gainsb", [128, 9, 16], F32)
    PM = sb("pmb", [128, 4], F32)
    MG = sb("mgb", [128, 2, 8], F32)
    LRP = sb("lrpb", [128, 2, 16, 8], F32)
    LS = sb("lsb", [128, 2, 3, 16], F32)
    XTAIL = sb("xtail", [128, 16, 3], F32)
    CAR = sb("car", [128, 6, 16], F32)
    HXR = sb("hxr", [128, 16, 3], F32)
    HH = sb("hh", [128, 16, 64], F32)
    HHN = sb("hhn", [128, 16, 64], BF16)
    ARENA = sb("arena", [128, NASLOT * ASLOT], F32)
    PS = es.enter_context(nc.psum_tensor("ps", [128, 8, 512], F32))

    def gen(S, Wst):
        B = {}

        def buf(name):
            if name not in B:
                B[name] = Buf(name)
            return B[name]

        PB = [buf("pb%d" % i) for i in range(8)]
        WSB = [buf("ws%d" % i) for i in range(NS)]
        AB = [buf("ar%d" % i) for i in range(NASLOT)]
        bHT, bHNX, bAT = buf("HT"), buf("HNX"), buf("AT")
        bSQ = [buf("sq0"), buf("sq1")]
        bRS = [buf("rs0"), buf("rs1")]
        bSIL = [buf("sil0"), buf("sil1")]
        bCONST = buf("const")
        bSM = buf("small")
        bXTAIL, bCAR, bHXR, bHH, bHHN = buf("xtail"), buf("car"), buf("hxr"), buf("hh"), buf("hhn")
        Wst.bind(S, WS, WSB)

        def af32(slot, n=1):
            return V(ARENA[:, slot * ASLOT:(slot + n) * ASLOT], AB[slot:slot + n])

        def abf(slot, n):
            return V(ARENA[:, slot * ASLOT:(slot + n) * ASLOT].bitcast(BF16), AB[slot:slot + n])

        def dtile(t3d, name, t, c0=None, c1=None):
            if c0 is None:
                c0, c1 = t * TILE, (t + 1) * TILE
            return V(t3d[:, :, c0:c1], [buf("%s_t%d" % (name, t))])

        def pkn(ap):
            return ap.rearrange("(k p) n -> p k n", p=128)

        S.add("dve", lambda e: e.memset(ONES[:, :], 1.0), W=[bCONST])
        S.add("dve", lambda e: e.memset(ZEROS[:, :], 0.0), W=[bCONST])
        S.add("dve", lambda e: e.memset(CONST[:, 0:1], EPS), W=[bCONST])
        S.add("dve", lambda e: e.memset(CONST[:, 1:2], 1.0), W=[bCONST])
        S.add("dve", lambda e: e.memset(CONST[:, 2:3], 0.0), W=[bCONST])
        S.add("sp", lambda e: e.dma_start(out=GAINS[:, :, :].rearrange("p a b -> p (a b)"), in_=gaind[:, :]), W=[bSM], dma=True)
        S.add("sp", lambda e: e.dma_start(out=PM[:, :], in_=pmd[:, :]), W=[bSM], dma=True)
        S.add("sp", lambda e: e.dma_start(out=MG[:, :, :].rearrange("p a b -> p (a b)"), in_=mgd[:, :]), W=[bSM], dma=True)
        S.add("sp", lambda e: e.dma_start(out=LRP[:, :, :, :].rearrange("p a b c -> p (a b c)"), in_=lrpd[:, :]), W=[bSM], dma=True)
        bLS = buf("ls")
        for jj in range(2):
            S.add("act", lambda e, jj=jj: e.activation(out=LS[:, jj, 2, :], in_=LRP[:, jj, :, 7], func=AF.Exp, scale=-1.0), R=[bSM], W=[bLS], short=True)
            S.add("act", lambda e, jj=jj: e.activation(out=LS[:, jj, 2, :], in_=LS[:, jj, 2, :], func=AF.Ln, bias=CONST[:, 1:2], scale=1.0), R=[bCONST], W=[bLS], short=True)
            S.add("dve", lambda e, jj=jj: e.tensor_scalar(LS[:, jj, 0, :], LS[:, jj, 2, :], -8.0, 0.0, ALU.mult, ALU.add), R=[bLS], W=[bLS], short=True)
            S.add("dve", lambda e, jj=jj: e.tensor_scalar(LS[:, jj, 1, :], LS[:, jj, 2, :], 4.0, 0.0, ALU.mult, ALU.add), R=[bLS], W=[bLS], short=True)

        def mm(out_ap, lhsT, rhs, start, stop, R, W):
            S.add("pe", lambda e: e.matmul(out_ap, lhsT, rhs, start=start, stop=stop), R=R, W=W)

        def segsof(w):
            if w <= 512:
                return [(0, w)]
            h = w // 2
            return [(0, h), (h, w)]

        def rmsnorm(src, srcb, w, nk, inv_n, gain_ap, dst, dstb, banks, rsv, rsb, inplace_f32=False):
            sh = w < 128
            sg = segsof(w)
            for k in range(nk):
                sl = k % 2
                S.add("act", lambda e, k=k, sl=sl: e.activation(out=SQ[:, sl, 0:w], in_=src[:, k, 0:w], func=AF.Square), R=[srcb], W=[bSQ[sl]], short=sh)
                for si, (a, b) in enumerate(sg):
                    mm(PS[:, banks[si], 0:b - a], ONES[:, :], SQ[:, sl, a:b], k == 0, k == nk - 1, [bCONST, bSQ[sl]], [PB[banks[si]]])
            for si, (a, b) in enumerate(sg):
                S.add("act", lambda e, si=si, a=a, b=b: e.activation(out=rsv[:, a:b], in_=PS[:, banks[si], 0:b - a], func=AF.Sqrt, bias=CONST[:, 0:1], scale=inv_n), R=[PB[banks[si]], bCONST], W=[rsb], short=sh)
            S.add("dve", lambda e: e.reciprocal(rsv[:, 0:w], rsv[:, 0:w]), R=[rsb], W=[rsb], short=sh)
            for k in range(nk):
                S.add("dve", lambda e, k=k: e.scalar_tensor_tensor(dst[:, k, 0:w], src[:, k, 0:w], gain_ap[:, k:k + 1], rsv[:, 0:w], ALU.mult, ALU.mult), R=[srcb, rsb, bSM], W=[dstb], short=sh)

        def seg3(ap2d):
            return ap2d.rearrange("p (s c) -> p s c", s=2)

        def ffn(l):
            for g in range(NG):
                for jj in range(GJ):
                    j = g * GJ + jj
                    sl = Wst.get(("wgu", l, j), wd[("wgu", l)][j], 4096)
                    wv = WS[:, sl, :].rearrange("p (a k m) -> p a k m", a=2, k=16)
                    for half, b0 in ((0, 0), (1, 2)):
                        for k in range(KC):
                            for si, (a, b) in enumerate(SEGS):
                                mm(PS[:, b0 + si, 0:SEG], wv[:, half, k, :], HNX[:, k, a:b], k == 0, k == KC - 1, [WSB[sl], bHNX], [PB[b0 + si]])
                    ss = jj % 2
                    S.add("act", lambda e, ss=ss: e.activation(out=seg3(SIL[:, ss, :]), in_=PS[:, 0:2, 0:SEG], func=AF.Silu), R=[PB[0], PB[1]], W=[bSIL[ss]])
                    S.add("dve", lambda e, ss=ss, jj=jj: e.tensor_tensor(seg3(AT[:, jj, :]), seg3(SIL[:, ss, :]), PS[:, 2:4, 0:SEG], ALU.mult), R=[bSIL[ss], PB[2], PB[3]], W=[bAT])
                for ocp in range(8):
                    sl = Wst.get(("wdn", l, g, ocp), wd[("wdn", l)][g * 8 + ocp], 2816)
                    wv = WS[:, sl, 0:2816].rearrange("p (a j m) -> p a j m", a=2, j=GJ)
                    for o2 in range(2):
                        oc = ocp * 2 + o2
                        b0 = 4 + 2 * (oc % 2)
                        for jj in range(GJ):
                            for si, (a, b) in enumerate(SEGS):
                                mm(PS[:, b0 + si, 0:SEG], wv[:, o2, jj, :], AT[:, jj, a:b], jj == 0, jj == GJ - 1, [WSB[sl], bAT], [PB[b0 + si]])
                        S.add("dve", lambda e, oc=oc, b0=b0: e.tensor_tensor(seg3(HT[:, oc, :]), seg3(HT[:, oc, :]), PS[:, b0:b0 + 2, 0:SEG], ALU.add), R=[bHT, PB[b0], PB[b0 + 1]], W=[bHT])

        def proj_residual(key, blocks):
            for oc in range(16):
                sl = Wst.get((key, oc), blocks[oc], 2048)
                wv = WS[:, sl, 0:2048].rearrange("p (k m) -> p k m", k=16)
                b0 = 4 + 2 * (oc % 2)
                for k in range(KC):
                    for si, (a, b) in enumerate(SEGS):
                        mm(PS[:, b0 + si, 0:SEG], wv[:, k, :], HNX[:, k, a:b], k == 0, k == KC - 1, [WSB[sl], bHNX], [PB[b0 + si]])
                S.add("dve", lambda e, oc=oc, b0=b0: e.tensor_tensor(seg3(HT[:, oc, :]), seg3(HT[:, oc, :]), PS[:, b0:b0 + 2, 0:SEG], ALU.add), R=[bHT, PB[b0], PB[b0 + 1]], W=[bHT])

        def tail_tile(l, t, hdst3, hdst_name, is_final):
            rmsnorm(HT, bHT, TILE, KC, 1.0 / D, GAINS[:, 4 + l, :], HNX, bHNX, (0, 1), RS[:, 0, :], bRS[0])
            ffn(l)
            if is_final:
                rmsnorm(HT, bHT, TILE, KC, 1.0 / D, GAINS[:, 8, :], HT, bHT, (0, 1), RS[:, 0, :], bRS[0])
            dv = dtile(hdst3, hdst_name, t)
            S.add("sp", lambda e: e.dma_start(out=dv.ap, in_=HT[:, :, :]), R=[bHT], W=[dv], dma=True)

        def mla_layer(l, hsrc3, hsrc_name, hdst3, hdst_name, is_final):
            j = l // 2
            lats, latr, latp = LATS[j], LATR[j], LATP[j]
            cq3 = pkn(CQD[:, :])
            lato3 = pkn(LATO[:, :])
            for t in range(NTILES):
                sv = dtile(hsrc3, hsrc_name, t)
                S.add("sp", lambda e, sv=sv: e.dma_start(out=HT[:, :, :], in_=sv.ap), R=[sv], W=[bHT], dma=True)
                c0, c1 = t * TILE, (t + 1) * TILE
                S.add("sp", lambda e, c0=c0, c1=c1: e.dma_start(out=SIL[0:64, 0, :], in_=cosd[:, c0:c1]), W=[bSIL[0]], dma=True)
                S.add("sp", lambda e, c0=c0, c1=c1: e.dma_start(out=SIL[0:64, 1, :], in_=sind[:, c0:c1]), W=[bSIL[1]], dma=True)
                rmsnorm(HT, bHT, TILE, KC, 1.0 / D, GAINS[:, l, :], HNX, bHNX, (0, 1), RS[:, 0, :], bRS[0])
                CQ = [af32(i) for i in range(4)]
                CKV = [af32(4 + i) for i in range(4)]
                CQN = abf(8, 2)
                LATT = abf(10, 3)
                TM1, TM2 = af32(13), af32(14)
                RSQ = af32(15)
                cqn3 = CQN.ap[:, 0:4 * TILE].rearrange("p (k n) -> p k n", k=4)
                latt3 = LATT.ap[:, 0:5 * TILE].rearrange("p (k n) -> p k n", k=5)
                for blk in range(8):
                    sl = Wst.get(("win", j, blk), wd[("win", j)][blk], 2048)
                    wv = WS[:, sl, 0:2048].rearrange("p (k m) -> p k m", k=16)
                    b0 = 4 + 2 * (blk % 2)
                    for k in range(KC):
                        for si, (a, b) in enumerate(SEGS):
                            mm(PS[:, b0 + si, 0:SEG], wv[:, k, :], HNX[:, k, a:b], k == 0, k == KC - 1, [WSB[sl], bHNX], [PB[b0 + si]])
                    dstv = CQ[blk] if blk < 4 else CKV[blk - 4]
                    S.add("act", lambda e, dstv=dstv, b0=b0: e.activation(out=seg3(dstv.ap[:, 0:TILE]), in_=PS[:, b0:b0 + 2, 0:SEG], func=AF.Copy), R=[PB[b0], PB[b0 + 1]], W=[dstv])
                sl = Wst.get(("win", j, 8), wd[("win", j)][8], 2048)
                wv = WS[:, sl, 0:2048].rearrange("p (k m) -> p k m", k=16)
                for which, b0 in ((0, 0), (1, 2)):
                    for k in range(KC):
                        for si, (a, b) in enumerate(SEGS):
                            mm(PS[0:64, b0 + si, 0:SEG], wv[:, k, which * 64:(which + 1) * 64], HNX[:, k, a:b], k == 0, k == KC - 1, [WSB[sl], bHNX], [PB[b0 + si]])
                S.add("dve", lambda e, TM1=TM1: e.tensor_tensor(seg3(TM1.ap[0:64, 0:TILE]), PS[0:64, 0:2, 0:SEG], seg3(SIL[0:64, 0, :]), ALU.mult), R=[PB[0], PB[1], bSIL[0]], W=[TM1])
                S.add("dve", lambda e, TM2=TM2: e.tensor_tensor(seg3(TM2.ap[0:64, 0:TILE]), PS[0:64, 2:4, 0:SEG], seg3(SIL[0:64, 1, :]), ALU.mult), R=[PB[2], PB[3], bSIL[1]], W=[TM2])
                S.add("dve", lambda e, TM1=TM1, TM2=TM2, latt3=latt3: e.tensor_tensor(latt3[0:64, 4, :], TM1.ap[0:64, 0:TILE], TM2.ap[0:64, 0:TILE], ALU.add), R=[TM1, TM2], W=[LATT])
                for which, (srcs, dst3, goff) in enumerate(((CQ, cqn3, 0), (CKV, latt3, 4))):
                    for k in range(4):
                        sl2 = k % 2
                        S.add("act", lambda e, k=k, sl2=sl2, srcs=srcs: e.activation(out=SQ[:, sl2, :], in_=srcs[k].ap[:, 0:TILE], func=AF.Square), R=[srcs[k]], W=[bSQ[sl2]])
                        for si, (a, b) in enumerate(SEGS):
                            mm(PS[:, si, 0:SEG], ONES[:, :], SQ[:, sl2, a:b], k == 0, k == 3, [bCONST, bSQ[sl2]], [PB[si]])
                    S.add("act", lambda e, RSQ=RSQ: e.activation(out=seg3(RSQ.ap[:, 0:TILE]), in_=PS[:, 0:2, 0:SEG], func=AF.Sqrt, bias=CONST[:, 0:1], scale=1.0 / 512), R=[PB[0], PB[1], bCONST], W=[RSQ])
                    S.add("dve", lambda e, RSQ=RSQ: e.reciprocal(RSQ.ap[:, 0:TILE], RSQ.ap[:, 0:TILE]), R=[RSQ], W=[RSQ])
                    dstV = CQN if which == 0 else LATT
                    for k in range(4):
                        S.add("dve", lambda e, k=k, srcs=srcs, dst3=dst3, goff=goff, RSQ=RSQ: e.scalar_tensor_tensor(dst3[:, k, :], srcs[k].ap[:, 0:TILE], MG[:, j, goff + k:goff + k + 1], RSQ.ap[:, 0:TILE], ALU.mult, ALU.mult), R=[srcs[k], RSQ, bSM], W=[dstV])
                cqv = dtile(cq3, "cqd", t)
                S.add("sp", lambda e, cqv=cqv, cqn3=cqn3: e.dma_start(out=cqv.ap, in_=cqn3), R=[CQN], W=[cqv], dma=True)
                lov = dtile(lato3, "lato", t)
                S.add("sp", lambda e, lov=lov, latt3=latt3: e.dma_start(out=lov.ap, in_=latt3), R=[LATT], W=[lov], dma=True)
                f0 = max(c0, NMETA)
                for k in range(5):
                    lsv = V(lats[k][:, f0 - NMETA:c1 - NMETA], [buf("lats%d_%d_t%d" % (j, k, t))])
                    S.add("sp", lambda e, lsv=lsv, latt3=latt3, f0=f0, c0=c0, k=k: e.dma_start(out=lsv.ap, in_=latt3[:, k, f0 - c0:TILE]), R=[LATT], W=[lsv], dma=True)
            if DEBUG.get('stop') == 'p1':
                return
            blr = buf("latp%d" % j)
            for k in range(5):
                brk = buf("latr%d_%d" % (j, k))
                S.add("pool", lambda e, k=k: e.collective_compute("AllGather", ALU.bypass, replica_groups=RGROUPS,
                                                             ins=[lats[k].ap().opt()], outs=[latr[k].ap().opt()]),
                      R=[buf("lats%d_%d_t%d" % (j, k, t)) for t in range(NTILES)], W=[brk], coll=True)
                bpk = buf("latp%d_%d" % (j, k))
                S.add("sp", lambda e, k=k: e.dma_start(out=latp[k * 128:(k + 1) * 128, :], in_=latr[k][0:128, :]), R=[brk], W=[bpk], dma=True)
            allp = [buf("latp%d_%d" % (j, k)) for k in range(5)]
            latr3 = pkn(latp[:, :])
            if DEBUG.get('stop') == 'xch':
                return
            CQNin = abf(0, 2)
            LATin = abf(2, 2)
            QN, QR = abf(4, 2), abf(6, 2)
            KNo, KNp = abf(8, 2), abf(10, 2)
            Vo, Vp = abf(12, 2), abf(14, 2)
            KRo, KRp = abf(16, 2), abf(18, 2)
            PTs = [V(ARENA[:, 20 * ASLOT:22 * ASLOT].bitcast(BF16)[:, i * 512:(i + 1) * 512], [buf("pt%d" % i), AB[20], AB[21]]) for i in range(4)]
            for p_ in PTs:
                p_.bufs = [p_.bufs[0]]
            OO = abf(22, 1)
            RD = af32(23)
            HNXf = HNX[:, :, :].rearrange("p k n -> p (k n)")
            ATf = AT[:, :, :].rearrange("p j n -> p (j n)").bitcast(F32)
            CQI = [V(HNXf[:, b_ * 2048:(b_ + 1) * 2048].rearrange("p (k n) -> p k n", k=4), [buf("cqin%d" % b_)]) for b_ in range(2)]
            LTI = [V(HNXf[:, 4096 + b_ * 2048:4096 + (b_ + 1) * 2048].rearrange("p (k n) -> p k n", k=4), [buf("latin%d" % b_)]) for b_ in range(2)]
            COSB = [V(ATf[0:64, b_ * 512:(b_ + 1) * 512], [buf("cosb%d" % b_)]) for b_ in range(2)]
            SINB = [V(ATf[0:64, 1024 + b_ * 512:1024 + (b_ + 1) * 512], [buf("sinb%d" % b_)]) for b_ in range(2)]
            S.add("dve", lambda e: e.memset(HNXf[0:1, 0:2], 0.0), W=[bHNX] + CQI + LTI)
            S.add("dve", lambda e: e.memset(ATf[0:1, 0:1], 0.0), W=[bAT] + COSB + SINB)
            gctr = [0]
            vo3 = Vo.ap[:, 0:17 * 128].rearrange("p (b d) -> p b d", b=17)
            vp3 = Vp.ap[:, 0:16 * 128].rearrange("p (b d) -> p b d", b=16)
            allat = [buf("lato_t%d" % t) for t in range(NTILES)]
            allcq = [buf("cqd_t%d" % t) for t in range(NTILES)]
            S.add("sp", lambda e: e.dma_start(out=KRo.ap[0:64, 0:NT], in_=LATO[512:576, :]), R=allat, W=[KRo, AB[20], AB[21]], dma=True)
            S.add("sp", lambda e: e.dma_start(out=KRp.ap[0:64, 0:NFR], in_=latp[512:576, :]), R=allp, W=[KRp], dma=True)
            QG = [(0, 128)] + [(NMETA + 512 * g, NMETA + 512 * (g + 1)) for g in range(4)]
            otd3 = OTD[:, :]
            for h in range(DEBUG.get('nh', NH)):
                sl = Wst.get(("whd", j, h), wd[("whd", j)][h], 2048)
                wq = WS[:, sl, 0:1024].rearrange("p (k m) -> p k m", k=4)
                wkv = WS[:, sl, 1024:2048].rearrange("p (k m) -> p k m", k=4)
                wb = WSB[sl]
                for gi, (c0, c1) in enumerate(QG):
                    w = c1 - c0
                    pb_ = gctr[0] % 2
                    gctr[0] += 1
                    CQNin, LATin, COSv, SINv = CQI[pb_], LTI[pb_], COSB[pb_], SINB[pb_]
                    cqin3, latin3 = CQNin.ap, LATin.ap
                    S.add("sp", lambda e, c0=c0, c1=c1, w=w, cqin3=cqin3: e.dma_start(out=cqin3[:, :, 0:w], in_=cq3[:, :, c0:c1]), R=allcq, W=[CQNin], dma=True)
                    S.add("sp", lambda e, c0=c0, c1=c1, w=w, COSv=COSv: e.dma_start(out=COSv.ap[:, 0:w], in_=cosd[:, c0:c1]), W=[COSv], dma=True)
                    S.add("sp", lambda e, c0=c0, c1=c1, w=w, SINv=SINv: e.dma_start(out=SINv.ap[:, 0:w], in_=sind[:, c0:c1]), W=[SINv], dma=True)
                    wl = 128 if gi == 0 else w
                    S.add("sp", lambda e, c0=c0, wl=wl, latin3=latin3: e.dma_start(out=latin3[:, 0:4, 0:wl], in_=lato3[:, 0:4, c0:c0 + wl]), R=allat, W=[LATin], dma=True)
                    for k in range(4):
                        mm(PS[:, 5, 0:w], wq[:, k, 0:128], cqin3[:, k, 0:w], k == 0, k == 3, [wb, CQNin], [PB[5]])
                    S.add("act", lambda e, c0=c0, c1=c1, w=w: e.activation(out=QN.ap[:, c0:c1], in_=PS[:, 5, 0:w], func=AF.Copy), R=[PB[5]], W=[QN])
                    for k in range(4):
                        mm(PS[0:64, 6, 0:w], wq[:, k, 128:192], cqin3[:, k, 0:w], k == 0, k == 3, [wb, CQNin], [PB[6]])
                    for k in range(4):
                        mm(PS[0:64, 7, 0:w], wq[:, k, 192:256], cqin3[:, k, 0:w], k == 0, k == 3, [wb, CQNin], [PB[7]])
                    S.add("dve", lambda e, w=w, COSv=COSv: e.tensor_tensor(SIL[0:64, 1, 0:w], PS[0:64, 6, 0:w], COSv.ap[:, 0:w], ALU.mult), R=[PB[6], COSv], W=[bSIL[1]])
                    S.add("dve", lambda e, w=w, SINv=SINv: e.tensor_tensor(RS[0:64, 0, 0:w], PS[0:64, 7, 0:w], SINv.ap[:, 0:w], ALU.mult), R=[PB[7], SINv], W=[bRS[0]])
                    S.add("dve", lambda e, c0=c0, c1=c1, w=w: e.tensor_tensor(QR.ap[0:64, c0:c1], SIL[0:64, 1, 0:w], RS[0:64, 0, 0:w], ALU.add), R=[bSIL[1], bRS[0]], W=[QR])
                    for k in range(4):
                        mm(PS[:, 5, 0:w], wkv[:, k, 0:128], latin3[:, k, 0:w], k == 0, k == 3, [wb, LATin], [PB[5]])
                    S.add("act", lambda e, c0=c0, c1=c1, w=w: e.activation(out=KNo.ap[:, c0:c1], in_=PS[:, 5, 0:w], func=AF.Copy), R=[PB[5]], W=[KNo])
                    if gi == 0:
                        for k in range(4):
                            mm(PS[:, 6, 0:128], latin3[:, k, 0:128], wkv[:, k, 128:256], k == 0, k == 3, [wb, LATin], [PB[6]])
                        S.add("dve", lambda e: e.tensor_copy(vo3[:, 0, :], PS[:, 6, 0:128]), R=[PB[6]], W=[Vo])
                    else:
                        for bb in range(4):
                            for k in range(4):
                                mm(PS[:, 6, bb * 128:(bb + 1) * 128], latin3[:, k, bb * 128:(bb + 1) * 128], wkv[:, k, 128:256], k == 0, k == 3, [wb, LATin], [PB[6]])
                        vb0 = 1 + 4 * (gi - 1)
                        S.add("dve", lambda e, vb0=vb0: e.tensor_copy(vo3[:, vb0:vb0 + 4, :].rearrange("p b d -> p (b d)"), PS[:, 6, :]), R=[PB[6]], W=[Vo])
                for g in range(4):
                    c0, c1 = 512 * g, 512 * (g + 1)
                    pb_ = gctr[0] % 2
                    gctr[0] += 1
                    LATin = LTI[pb_]
                    latin3 = LATin.ap
                    S.add("sp", lambda e, c0=c0, c1=c1, latin3=latin3: e.dma_start(out=latin3[:, 0:4, :], in_=latr3[:, 0:4, c0:c1]), R=allp, W=[LATin], dma=True)
                    for k in range(4):
                        mm(PS[:, 5, :], wkv[:, k, 0:128], latin3[:, k, :], k == 0, k == 3, [wb, LATin], [PB[5]])
                    S.add("act", lambda e, c0=c0, c1=c1: e.activation(out=KNp.ap[:, c0:c1], in_=PS[:, 5, :], func=AF.Copy), R=[PB[5]], W=[KNp])
                    for bb in range(4):
                        for k in range(4):
                            mm(PS[:, 6, bb * 128:(bb + 1) * 128], latin3[:, k, bb * 128:(bb + 1) * 128], wkv[:, k, 128:256], k == 0, k == 3, [wb, LATin], [PB[6]])
                    S.add("dve", lambda e, g=g: e.tensor_copy(vp3[:, 4 * g:4 * g + 4, :].rearrange("p b d -> p (b d)"), PS[:, 6, :]), R=[PB[6]], W=[Vp])
                for gi, (c0, c1) in enumerate(QG):
                    w = c1 - c0
                    if gi not in DEBUG.get('groups', range(5)):
                        continue
                    kbl = []
                    if gi == 0:
                        kbl.append((KNo, KRo, vo3, Vo, 0, 128, 0, 0, False, 'meta'))
                    else:
                        g = gi - 1
                        for pb in range(16):
                            kbl.append((KNp, KRp, vp3, Vp, pb * 128, 128, pb, 0, False, 'prev'))
                        kbl.append((KNo, KRo, vo3, Vo, 0, 128, 0, 0, False, 'meta'))
                        for i in range(4 * g + 4):
                            kbl.append((KNo, KRo, vo3, Vo, NMETA + 128 * i, 128, 1 + i, max(0, 128 * (i - 4 * g)), i >= 4 * g, None))
                    if DEBUG.get('kfilter'):
                        kf = DEBUG['kfilter']
                        kbl = [kb for kb in kbl if (('p' in kf and kb[9] == 'prev') or ('m' in kf and kb[9] == 'meta') or ('d' in kf and kb[8]) or ('o' in kf and (not kb[9]) and kb[5] == 128 and not kb[8]))]
                    if DEBUG.get('fullcols'):
                        kbl = [kb[:7] + (0,) + kb[8:] for kb in kbl]
                    if DEBUG.get('nodiag'):
                        kbl = [kb[:8] + (False,) + kb[9:] for kb in kbl]
                    nb = len(kbl)
                    ob, db = (3, 4) if gi % 2 == 0 else (5, 6)

                    def qk(bi):
                        KNv, KRv, v3, Vv, k0, kw, vb, cs, diag, prevb = kbl[bi]
                        sb_ = bi % 3
                        mm(PS[0:kw, sb_, cs:w], KNv.ap[:, k0:k0 + kw], QN.ap[:, c0 + cs:c1], True, False, [KNv, QN], [PB[sb_]])
                        mm(PS[0:kw, sb_, cs:w], KRv.ap[0:64, k0:k0 + kw], QR.ap[0:64, c0 + cs:c1], False, True, [KRv, QR], [PB[sb_]])
                        pt = PTs[bi % 4]
                        o_ = pt.ap[0:kw, cs:w]
                        i_ = PS[0:kw, sb_, cs:w]
                        if prevb:
                            bi_ = PM[0:kw, 2:3] if prevb == 'prev' else PM[0:kw, 3:4]
                            S.add("act", lambda e, o_=o_, i_=i_, bi_=bi_: e.activation(out=o_, in_=i_, func=AF.Exp, bias=bi_, scale=SCALE), R=[PB[sb_], bSM], W=[pt])
                        else:
                            bi_ = CONST[0:kw, 2:3]
                            S.add("act", lambda e, o_=o_, i_=i_, bi_=bi_: e.activation(out=o_, in_=i_, func=AF.Exp, bias=bi_, scale=SCALE), R=[PB[sb_], bCONST], W=[pt])
                            if diag:
                                z_ = pt.ap[64:128, cs:cs + 64]
                                S.add("act", lambda e, z_=z_: e.memzero(z_), W=[pt])

                    def pv(bi):
                        KNv, KRv, v3, Vv, k0, kw, vb, cs, diag, prevb = kbl[bi]
                        pt = PTs[bi % 4]
                        first = bi == 0
                        last = bi == nb - 1
                        if True:
                            mm(PS[:, ob, cs:w], v3[0:kw, vb, :], pt.ap[0:kw, cs:w], first, last, [Vv, pt], [PB[ob]])
                            mm(PS[:, db, cs:w], ONES[0:kw, :], pt.ap[0:kw, cs:w], first, last, [bCONST, pt], [PB[db]])
                        else:
                            mm(PS[:, 3, cs:w], v3[0:64, vb, :], pt.ap[0:64, cs:w], first, False, [Vv, pt], [PB[3]])
                            mm(PS[:, 4, cs:w], ONES[0:64, :], pt.ap[0:64, cs:w], first, False, [bCONST, pt], [PB[4]])
                            mm(PS[:, 3, cs + 64:w], v3[64:128, vb, :], pt.ap[64:128, cs + 64:w], False, last, [Vv, pt], [PB[3]])
                            mm(PS[:, 4, cs + 64:w], ONES[64:128, :], pt.ap[64:128, cs + 64:w], False, last, [bCONST, pt], [PB[4]])

                    qk(0)
                    if nb > 1:
                        qk(1)
                    for bi in range(nb):
                        if bi + 2 < nb:
                            qk(bi + 2)
                        pv(bi)
                    S.add("dve", lambda e, w=w, db=db: e.reciprocal(RD.ap[:, 0:w], PS[:, db, 0:w]), R=[PB[db]], W=[RD])
                    S.add("dve", lambda e, w=w, ob=ob: e.tensor_tensor(OO.ap[:, 0:w], PS[:, ob, 0:w], RD.ap[:, 0:w], ALU.mult), R=[PB[ob], RD], W=[OO])
                    ws_ = NMETA if gi == 0 else w
                    ov = V(otd3[h * 128:(h + 1) * 128, c0:c0 + ws_], [buf("otd_h%d_g%d" % (h, gi))])
                    S.add("sp", lambda e, ov=ov, ws_=ws_: e.dma_start(out=ov.ap, in_=OO.ap[:, 0:ws_]), R=[OO], W=[ov], dma=True)
            if DEBUG.get('stop') == 'p2':
                return
            for p_ in PTs:
                S.add("dve", lambda e, p_=p_: e.memset(p_.ap[0:1, 0:1], 0.0), W=[p_, AB[20], AB[21]])
            S.add("dve", lambda e: e.memset(HNXf[0:1, 0:2], 0.0), W=[bHNX] + CQI + LTI)
            S.add("dve", lambda e: e.memset(ATf[0:1, 0:1], 0.0), W=[bAT] + COSB + SINB)
            allot = [buf("otd_h%d_g%d" % (h, gi)) for h in range(NH) for gi in range(5)]
            ot3 = pkn(OTD[:, :])
            for t in range(NTILES):
                sv = dtile(hsrc3, hsrc_name, t)
                S.add("sp", lambda e, sv=sv: e.dma_start(out=HT[:, :, :], in_=sv.ap), R=[sv], W=[bHT], dma=True)
                c0, c1 = t * TILE, (t + 1) * TILE
                S.add("sp", lambda e, c0=c0, c1=c1: e.dma_start(out=HNX[:, :, :], in_=ot3[:, :, c0:c1]), R=allot, W=[bHNX], dma=True)
                proj_residual(("wo", j), wd[("wo", j)])
                tail_tile(l, t, hdst3, hdst_name, is_final)

        def lru_layer(l, hsrc3, hsrc_name, hsrc2d, hdst3, hdst_name, is_final):
            j = l // 2
            hsd, hrd, ssd, srd = HSD[j], HRD[j], SSD[j], SRD[j]
            bhs, bhr = buf("hsd%d" % j), buf("hrd%d" % j)
            lastsrc = buf("%s_t%d" % (hsrc_name, NTILES - 1))
            hsd2 = hsd[:, :].rearrange("a (b c) -> (a b) c", c=64)
            hrd2 = hrd[0:128, :].rearrange("a (b c) -> (a b) c", c=64)
            S.add("sp", lambda e: e.dma_start(out=hsd2[:, :], in_=hsrc2d[:, NT - 64:NT]), R=[lastsrc], W=[bhs], dma=True)
            S.add("pool", lambda e: e.collective_compute("AllGather", ALU.bypass, replica_groups=RGROUPS,
                                                         ins=[hsd.ap().opt()], outs=[hrd.ap().opt()]), R=[bhs], W=[bhr], coll=True)
            S.add("sp", lambda e: e.dma_start(out=HH[:, :, :], in_=pkn(hrd2)), R=[bhr], W=[bHH], dma=True)
            rmsnorm(HH, bHH, 64, KC, 1.0 / D, GAINS[:, l, :], HHN, bHHN, (0,), RS[:, 1, :], bRS[1])
            if DEBUG.get('lstop') == 'halo':
                return
            GWv = abf(22, 3)
            S.add("pool", lambda e: e.dma_start(out=GWv.ap[:, 0:4096], in_=wd[("lgw", j)][0]), W=[GWv], dma=True)
            gw4 = GWv.ap[:, 0:4096].rearrange("p (a c m) -> p a c m", a=2, c=16)
            g03 = pkn(G0D[:, :])
            g13 = pkn(G1D[:, :])
            CWv = LRP[:, j, :, :]
            S1 = LS[:, j, 0, :]
            H1 = LS[:, j, 1, :]
            for t in range(DEBUG.get('ltiles', NTILES)):
                sv = dtile(hsrc3, hsrc_name, t)
                S.add("sp", lambda e, sv=sv: e.dma_start(out=HT[:, :, :], in_=sv.ap), R=[sv], W=[bHT], dma=True)
                rmsnorm(HT, bHT, TILE, KC, 1.0 / D, GAINS[:, l, :], HNX, bHNX, (4, 5), RS[:, 0, :], bRS[0])
                c0 = t * TILE
                def lru_views(c):
                        st = (c % 2) * 10
                        XB, Y, XC, RG, IG, AA, TH, H0, AC, OM = [af32(st + i) for i in range(10)]
                        XCB = abf(20 + (c % 2), 1)
                        return XB, Y, XC, RG, IG, AA, TH, H0, AC, OM, XCB

                def stageA(c, t=t, c0=c0):
                        XB, Y, XC, RG, IG, AA, TH, H0, AC, OM, XCB = lru_views(c)
                        sl = Wst.get(("lin", j, c), wd[("lin", j)][c], 4096)
                        wv = WS[:, sl, :].rearrange("p (a k m) -> p a k m", a=2, k=16)
                        for half, b0 in ((0, 0), (1, 2)):
                            for k in range(KC):
                                for si, (a, b) in enumerate(SEGS):
                                    mm(PS[:, b0 + si, 0:SEG], wv[:, half, k, :], HNX[:, k, a:b], k == 0, k == KC - 1, [WSB[sl], bHNX], [PB[b0 + si]])
                        if t == 0 and not DEBUG.get('nohalo'):
                            for k in range(KC):
                                mm(PS[:, 3, 384:448], wv[:, 0, k, :], HHN[:, k, 0:64], k == 0, k == KC - 1, [WSB[sl], bHHN], [PB[3]])
                            S.add("act", lambda e, c=c: e.activation(out=HXR[:, c, :], in_=PS[:, 3, 445:448], func=AF.Copy, scale=PM[:, 1:2]), R=[PB[3], bSM], W=[bHXR], short=True)
                        S.add("act", lambda e, Y=Y: e.activation(out=seg3(Y.ap[:, 0:TILE]), in_=PS[:, 2:4, 0:SEG], func=AF.Gelu_apprx_tanh), R=[PB[2], PB[3]], W=[Y])
                        S.add("act", lambda e, XB=XB: e.activation(out=seg3(XB.ap[:, 3:3 + TILE]), in_=PS[:, 0:2, 0:SEG], func=AF.Copy), R=[PB[0], PB[1]], W=[XB])

                        def conv(a, b, XB=XB, XC=XC, c=c, sh=False):
                            S.add("dve", lambda e: e.tensor_scalar(XC.ap[:, a:b], XB.ap[:, a + 3:b + 3], CWv[:, c, 3:4], CWv[:, c, 4:5], ALU.mult, ALU.add), R=[XB, bSM], W=[XC], short=sh)
                            for tap in range(3):
                                S.add("dve", lambda e, tap=tap: e.scalar_tensor_tensor(XC.ap[:, a:b], XB.ap[:, a + tap:b + tap], CWv[:, c, tap:tap + 1], XC.ap[:, a:b], ALU.mult, ALU.add), R=[XB, XC, bSM], W=[XC], short=sh)
                        if t == 0:
                            S.add("dve", lambda e, XB=XB: e.memset(XB.ap[:, 0:3], 0.0), W=[XB], short=True)
                            conv(0, NMETA, sh=True)
                            S.add("dve", lambda e, XB=XB, c=c: e.scalar_tensor_tensor(XB.ap[:, NMETA:NMETA + 3], XB.ap[:, NMETA:NMETA + 3], PM[:, 0:1], HXR[:, c, :], ALU.mult, ALU.add), R=[XB, bHXR, bSM], W=[XB], short=True)
                            conv(NMETA, TILE)
                        else:
                            S.add("dve", lambda e, XB=XB, c=c: e.tensor_copy(XB.ap[:, 0:3], XTAIL[:, c, :]), R=[bXTAIL], W=[XB], short=True)
                            conv(0, TILE)
                        S.add("dve", lambda e, XB=XB, c=c: e.tensor_copy(XTAIL[:, c, :], XB.ap[:, TILE:TILE + 3]), R=[XB], W=[bXTAIL], short=True)
                        S.add("act", lambda e, XC=XC, XCB=XCB: e.activation(out=XCB.ap[:, 0:TILE], in_=XC.ap[:, 0:TILE], func=AF.Copy), R=[XC], W=[XCB])

                def stageB(c, t=t, c0=c0):
                        XB, Y, XC, RG, IG, AA, TH, H0, AC, OM, XCB = lru_views(c)
                        for gate, b0 in ((0, 4), (1, 6)):
                            for si, (a, b) in enumerate(SEGS):
                                mm(PS[:, b0 + si, 0:SEG], gw4[:, gate, c, :], XCB.ap[:, a:b], True, True, [GWv, XCB], [PB[b0 + si]])
                        S.add("act", lambda e, RG=RG, c=c: e.activation(out=seg3(RG.ap[:, 0:TILE]), in_=PS[:, 4:6, 0:SEG], func=AF.Sigmoid, bias=CWv[:, c, 5:6], scale=1.0), R=[PB[4], PB[5], bSM], W=[RG])
                        S.add("act", lambda e, IG=IG, c=c: e.activation(out=seg3(IG.ap[:, 0:TILE]), in_=PS[:, 6:8, 0:SEG], func=AF.Sigmoid, bias=CWv[:, c, 6:7], scale=1.0), R=[PB[6], PB[7], bSM], W=[IG])
                        S.add("act", lambda e, RG=RG, AA=AA, c=c: e.activation(out=AA.ap[:, 0:TILE], in_=RG.ap[:, 0:TILE], func=AF.Exp, scale=S1[:, c:c + 1]), R=[RG, bLS], W=[AA])
                        S.add("act", lambda e, RG=RG, TH=TH, c=c: e.activation(out=TH.ap[:, 0:TILE], in_=RG.ap[:, 0:TILE], func=AF.Tanh, scale=H1[:, c:c + 1]), R=[RG, bLS], W=[TH])
                        S.add("dve", lambda e, AA=AA, TH=TH, OM=OM: e.scalar_tensor_tensor(OM.ap[:, 0:TILE], AA.ap[:, 0:TILE], 1.0, TH.ap[:, 0:TILE], ALU.add, ALU.mult), R=[AA, TH], W=[OM])
                        S.add("dve", lambda e, AA=AA, OM=OM: e.tensor_scalar(AA.ap[:, 0:TILE], OM.ap[:, 0:TILE], -1.0, 1.0, ALU.mult, ALU.add), R=[OM], W=[AA])
                        S.add("dve", lambda e, AA=AA, TH=TH, OM=OM: e.scalar_tensor_tensor(TH.ap[:, 0:TILE], AA.ap[:, 0:TILE], 1.0, OM.ap[:, 0:TILE], ALU.add, ALU.mult), R=[AA, OM], W=[TH])
                        S.add("act", lambda e, TH=TH: e.activation(out=TH.ap[:, 0:TILE], in_=TH.ap[:, 0:TILE], func=AF.Sqrt), R=[TH], W=[TH])
                        S.add("dve", lambda e, IG=IG, XC=XC: e.tensor_tensor(IG.ap[:, 0:TILE], IG.ap[:, 0:TILE], XC.ap[:, 0:TILE], ALU.mult), R=[IG, XC], W=[IG])
                        S.add("dve", lambda e, IG=IG, TH=TH: e.tensor_tensor(IG.ap[:, 0:TILE], IG.ap[:, 0:TILE], TH.ap[:, 0:TILE], ALU.mult), R=[IG, TH], W=[IG])
                        if t == 0:
                            S.add("dve", lambda e, AA=AA, IG=IG, H0=H0: e.tensor_tensor_scan(H0.ap[:, 0:NMETA], AA.ap[:, 0:NMETA], IG.ap[:, 0:NMETA], 0.0, ALU.mult, ALU.add), R=[AA, IG], W=[H0], short=True)
                            S.add("dve", lambda e, AC=AC: e.memset(AC.ap[:, 0:NMETA], 0.0), W=[AC], short=True)
                            S.add("dve", lambda e, H0=H0, c=c: e.tensor_copy(CAR[:, 2, c:c + 1], H0.ap[:, NMETA - 1:NMETA]), R=[H0], W=[bCAR], short=True)
                            S.add("dve", lambda e, AA=AA, IG=IG, H0=H0: e.tensor_tensor_scan(H0.ap[:, NMETA:TILE], AA.ap[:, NMETA:TILE], IG.ap[:, NMETA:TILE], 0.0, ALU.mult, ALU.add), R=[AA, IG], W=[H0])
                            S.add("dve", lambda e, AA=AA, AC=AC: e.tensor_tensor_scan(AC.ap[:, NMETA:TILE], AA.ap[:, NMETA:TILE], ZEROS[:, NMETA:TILE], 1.0, ALU.mult, ALU.add), R=[AA, bCONST], W=[AC])
                        else:
                            S.add("dve", lambda e, AA=AA, IG=IG, H0=H0, c=c: e.tensor_tensor_scan(H0.ap[:, 0:TILE], AA.ap[:, 0:TILE], IG.ap[:, 0:TILE], CAR[:, 0, c:c + 1], ALU.mult, ALU.add), R=[AA, IG, bCAR], W=[H0])
                            S.add("dve", lambda e, AA=AA, AC=AC, c=c: e.tensor_tensor_scan(AC.ap[:, 0:TILE], AA.ap[:, 0:TILE], ZEROS[:, 0:TILE], CAR[:, 1, c:c + 1], ALU.mult, ALU.add), R=[AA, bCONST, bCAR], W=[AC])
                        S.add("dve", lambda e, H0=H0, c=c: e.tensor_copy(CAR[:, 0, c:c + 1], H0.ap[:, TILE - 1:TILE]), R=[H0], W=[bCAR], short=True)
                        S.add("dve", lambda e, AC=AC, c=c: e.tensor_copy(CAR[:, 1, c:c + 1], AC.ap[:, TILE - 1:TILE]), R=[AC], W=[bCAR], short=True)
                        S.add("pool", lambda e, H0=H0, Y=Y: e.tensor_tensor(H0.ap[:, 0:TILE], H0.ap[:, 0:TILE], Y.ap[:, 0:TILE], ALU.mult), R=[H0, Y], W=[H0])
                        S.add("pool", lambda e, AC=AC, Y=Y: e.tensor_tensor(AC.ap[:, 0:TILE], AC.ap[:, 0:TILE], Y.ap[:, 0:TILE], ALU.mult), R=[AC, Y], W=[AC])
                        g0v = V(g03[:, c, c0:c0 + TILE], [buf("g0_%d_%d" % (t, c))])
                        g1v = V(g13[:, c, c0:c0 + TILE], [buf("g1_%d_%d" % (t, c))])
                        S.add("sp", lambda e, g0v=g0v, H0=H0: e.dma_start(out=g0v.ap, in_=H0.ap[:, 0:TILE]), R=[H0], W=[g0v], dma=True)
                        S.add("sp", lambda e, g1v=g1v, AC=AC: e.dma_start(out=g1v.ap, in_=AC.ap[:, 0:TILE]), R=[AC], W=[g1v], dma=True)

                ncx = DEBUG.get('nchunks', 16)
                stageA(0)
                for c in range(ncx):
                    if c + 1 < ncx:
                        stageA(c + 1)
                    stageB(c)

            if DEBUG.get('lstop') == 'p1':
                return
            S.add("dve", lambda e: e.tensor_tensor(CAR[:, 4, :], CAR[:, 1, :], CAR[:, 2, :], ALU.mult), R=[bCAR], W=[bCAR], short=True)
            S.add("dve", lambda e: e.tensor_tensor(CAR[:, 4, :], CAR[:, 4, :], CAR[:, 0, :], ALU.add), R=[bCAR], W=[bCAR], short=True)
            bss, bsr = buf("ssd%d" % j), buf("srd%d" % j)
            S.add("sp", lambda e: e.dma_start(out=ssd[:, :], in_=CAR[:, 4, :]), R=[bCAR], W=[bss], dma=True)
            S.add("pool", lambda e: e.collective_compute("AllGather", ALU.bypass, replica_groups=RGROUPS,
                                                         ins=[ssd.ap().opt()], outs=[srd.ap().opt()]), R=[bss], W=[bsr], coll=True)
            S.add("sp", lambda e: e.dma_start(out=CAR[:, 3, :], in_=srd[0:128, :]), R=[bsr], W=[bCAR], dma=True)
            S.add("dve", lambda e: e.tensor_scalar(CAR[:, 5, :], CAR[:, 2, :], PM[:, 0:1], 0.0, ALU.mult, ALU.add), R=[bCAR, bSM], W=[bCAR], short=True)
            S.add("dve", lambda e: e.scalar_tensor_tensor(CAR[:, 5, :], CAR[:, 3, :], PM[:, 1:2], CAR[:, 5, :], ALU.mult, ALU.add), R=[bCAR, bSM], W=[bCAR], short=True)
            if DEBUG.get('lstop') == 'x2':
                S.add("sp", lambda e: e.dma_start(out=hout[0:128, 0:96], in_=CAR[:, :, :].rearrange("p a b -> p (a b)")), R=[bCAR], W=[buf("hout_dbg")], dma=True)
                return
            for t in range(NTILES):
                sv = dtile(hsrc3, hsrc_name, t)
                S.add("sp", lambda e, sv=sv: e.dma_start(out=HT[:, :, :], in_=sv.ap), R=[sv], W=[bHT], dma=True)
                c0 = t * TILE
                for c in range(16):
                    A0, A1 = af32(2 * (c % 4)), af32(2 * (c % 4) + 1)
                    S.add("sp", lambda e, A0=A0, c=c, c0=c0: e.dma_start(out=A0.ap[:, 0:TILE], in_=g03[:, c, c0:c0 + TILE]), R=[buf("g0_%d_%d" % (t, c))], W=[A0], dma=True)
                    S.add("sp", lambda e, A1=A1, c=c, c0=c0: e.dma_start(out=A1.ap[:, 0:TILE], in_=g13[:, c, c0:c0 + TILE]), R=[buf("g1_%d_%d" % (t, c))], W=[A1], dma=True)
                    S.add("dve", lambda e, A0=A0, A1=A1, c=c: e.scalar_tensor_tensor(HNX[:, c, :], A1.ap[:, 0:TILE], CAR[:, 5, c:c + 1], A0.ap[:, 0:TILE], ALU.mult, ALU.add), R=[A0, A1, bCAR], W=[bHNX])
                proj_residual(("lwo", j), wd[("lwo", j)])
                tail_tile(l, t, hdst3, hdst_name, is_final)

        n = len(layer_ids)
        for i, l in enumerate(layer_ids):
            src2d = hin if i == 0 else RES[:, :]
            src_name = "hin" if i == 0 else "res"
            dst2d = hout if i == n - 1 else RES[:, :]
            dst_name = "hout" if i == n - 1 else "res"
            fin = final_norm and i == n - 1
            if l % 2 == 0:
                mla_layer(l, pkn(src2d), src_name, pkn(dst2d), dst_name, fin)
            else:
                lru_layer(l, pkn(src2d), src_name, src2d, pkn(dst2d), dst_name, fin)

    w0 = WStream(None)
    gen(Sched(), w0)
    S = Sched()
    w1 = WStream(w0.reqs)
    gen(S, w1)
    S.emit(nc, es)
    es.close()
    return nc


class WStream:
    def __init__(self, plan):
        self.plan = plan
        self.reqs = []
        self.i = 0
        self.issued = 0

    def bind(self, S, WS, WSB):
        self.S, self.WS, self.WSB = S, WS, WSB

    def get(self, key, src, n):
        i = self.i
        self.i += 1
        if self.plan is None:
            self.reqs.append((key, src, n))
            return i % NS
        plan = self.plan
        assert plan[i][0] == key, (plan[i][0], key)
        while self.issued < min(len(plan), i + 1 + LA):
            k, s_ap, nn = plan[self.issued]
            slot = self.issued % NS
            WS = self.WS
            self.S.add("pool", lambda e, slot=slot, s_ap=s_ap, nn=nn: e.dma_start(out=WS[:, slot, 0:nn], in_=s_ap), W=[self.WSB[slot]], dma=True)
            self.issued += 1
        return i % NS


def _blk2(W, half, nblk):
    a = W[:, :half].reshape(16, 128, nblk, 128).transpose(2, 1, 0, 3)
    b = W[:, half:].reshape(16, 128, nblk, 128).transpose(2, 1, 0, 3)
    return np.ascontiguousarray(np.stack([a, b], axis=2).reshape(nblk, 128, 4096))


def _blk_sq(W):
    return np.ascontiguousarray(W.reshape(16, 128, 16, 128).transpose(2, 1, 0, 3).reshape(16, 128, 2048))


def _prep_weights(inp, layer_ids):
    out = {}
    for l in layer_ids:
        j = l // 2
        out["wgu%d" % l] = _blk2(inp["ffn_w_gu"][l], DFF, NJ)
        wdn = inp["ffn_w_down"][l].reshape(NG, GJ, 128, 8, 2, 128).transpose(0, 3, 2, 4, 1, 5)
        out["wdn%d" % l] = np.ascontiguousarray(wdn.reshape(32, 128, 2816))
        if l % 2 == 0:
            w_in = inp["mla_w_in"][j]
            w3 = w_in.reshape(16, 128, 1088)
            blks = [w3[:, :, b * 128:(b + 1) * 128] for b in range(8)]
            rope = np.concatenate([w3[:, :, 1024:1088], w3[:, :, 1056:1088], w3[:, :, 1024:1056]], axis=2)
            blks.append(rope)
            out["win%d" % j] = np.ascontiguousarray(np.stack(blks, 0).transpose(0, 2, 1, 3).reshape(9, 128, 2048))
            wq = inp["mla_w_uq"][j].reshape(4, 128, NH, 192)
            wqh = np.concatenate([wq[..., 0:192], wq[..., 160:192], wq[..., 128:160]], axis=3)
            wkv = inp["mla_w_ukv"][j].reshape(4, 128, NH, 256)
            hd = np.concatenate([wqh.transpose(2, 1, 0, 3).reshape(NH, 128, 1024), wkv.transpose(2, 1, 0, 3).reshape(NH, 128, 1024)], axis=2)
            out["whd%d" % j] = np.ascontiguousarray(hd)
            out["wo%d" % j] = _blk_sq(inp["mla_w_o"][j])
        else:
            out["lin%d" % j] = _blk2(inp["lru_w_in"][j], 2048, 16)
            ga = inp["lru_w_gate_a"][j].transpose(1, 0, 2)
            gx = inp["lru_w_gate_x"][j].transpose(1, 0, 2)
            out["lgw%d" % j] = np.ascontiguousarray(np.stack([ga, gx], axis=1).reshape(1, 128, 4096))
            out["lwo%d" % j] = _blk_sq(inp["lru_w_o"][j])
    return out


def _prep_small(inp):
    def pk(v):
        return v.reshape(16, 128).T
    gains = np.stack([pk(inp["norm_mix"][l]) for l in range(4)] + [pk(inp["norm_ffn"][l]) for l in range(4)] + [pk(inp["norm_final"])], axis=1)
    mg = np.zeros((128, 2, 8), np.float32)
    for j in range(2):
        mg[:, j, 0:4] = inp["mla_q_norm"][j].reshape(4, 128).T
        mg[:, j, 4:8] = inp["mla_kv_norm"][j].reshape(4, 128).T
    lrp = np.zeros((128, 2, 16, 8), np.float32)
    for j in range(2):
        lrp[:, j, :, 0:4] = inp["lru_conv_w"][j].reshape(4, 16, 128).transpose(2, 1, 0)
        lrp[:, j, :, 4] = pk(inp["lru_conv_b"][j])
        lrp[:, j, :, 5] = inp["lru_b_gate_a"][j].T
        lrp[:, j, :, 6] = inp["lru_b_gate_x"][j].T
        lrp[:, j, :, 7] = pk(inp["lru_lambda"][j])
    return {"gains": np.ascontiguousarray(gains.reshape(128, 144), np.float32),
            "mg": np.ascontiguousarray(mg.reshape(128, 16)),
            "lrp": np.ascontiguousarray(lrp.reshape(128, 256))}


def _rope_tables(pos):
    inv_freq = (np.float32(10000.0) ** (-np.arange(0, 64, 2, dtype=np.float32) / np.float32(64))).astype(np.float32)
    ang = (pos.astype(np.float32)[None, :] * inv_freq[:, None]).astype(np.float32)
    c, s = np.cos(ang).astype(np.float32), np.sin(ang).astype(np.float32)
    return np.ascontiguousarray(np.concatenate([c, c], 0)), np.ascontiguousarray(np.concatenate([-s, s], 0))


_NC_CACHE = {}


def _get_nc(layer_ids, final, ncores=8):
    key = (tuple(layer_ids), final, ncores)
    if key not in _NC_CACHE:
        _NC_CACHE[key] = build(list(layer_ids), final, ncores)
    return _NC_CACHE[key]


def _core_consts(ncores):
    per = []
    for core in range(ncores):
        half = core % 2
        pos = np.concatenate([np.arange(NMETA), NMETA + half * NFR + np.arange(NFR)])
        cosd, sind = _rope_tables(pos)
        pm = np.zeros((128, 4), np.float32)
        pm[:, 0] = 1.0 if half == 0 else 0.0
        pm[:, 1] = 0.0 if half == 0 else 1.0
        pm[:, 2] = -30000.0 if half == 0 else 0.0
        pm[NMETA:, 3] = -30000.0
        per.append({"cosd": cosd, "sind": sind, "pm": pm})
    return per


def run_layers(inp, hT_list, layer_ids, final, ncores=8):
    nc = _get_nc(layer_ids, final, ncores)
    small = _prep_small(inp)
    wts = _prep_weights(inp, layer_ids)
    consts = _core_consts(ncores)
    in_maps = []
    for core in range(ncores):
        m = {"hin": hT_list[core]}
        m.update(consts[core])
        m.update(small)
        m.update(wts)
        in_maps.append(m)
    res = run_bass_kernel_spmd(nc, in_maps, core_ids=list(range(ncores)))
    return [r["hout"] for r in res.results]


def make_hT(x, meta_tokens, nb):
    hT = []
    metaT = np.ascontiguousarray(meta_tokens.T)
    for b in range(nb):
        xT = x[b].T
        for half in range(2):
            hT.append(np.ascontiguousarray(np.concatenate([metaT, xT[:, half * NFR:(half + 1) * NFR]], axis=1), dtype=np.float32))
    return hT


def kernel(**inp):
    inp = {k: np.asarray(v) for k, v in inp.items()}
    x = inp["x"]
    nb = x.shape[0]
    hT = make_hT(x, inp["meta_tokens"], nb)
    if FUSED:
        outs = run_layers(inp, hT, [0, 1, 2, 3], True, 2 * nb)
    else:
        outs = hT
        for l in range(4):
            outs = run_layers(inp, outs, [l], l == 3, 2 * nb)
    y = np.empty((nb, 2 * NFR, D), np.float32)
    for b in range(nb):
        for half in range(2):
            y[b, half * NFR:(half + 1) * NFR, :] = outs[2 * b + half][:, NMETA:].T
    return y
```

```python
import numpy as np
from contextlib import ExitStack
import concourse.bass as bass
import concourse.mybir as mybir
from concourse.bass_utils import run_bass_kernel_spmd

F32 = mybir.dt.float32
BF16 = mybir.dt.bfloat16
AF = mybir.ActivationFunctionType
ALU = mybir.AluOpType

D = 2048
KC = 16
NMETA = 16
NFR = 2048
NT = NMETA + NFR
TILE = 688
SEG = 344
NTILES = 3
SEGS = [(0, SEG), (SEG, TILE)]
DFF = 5632
NJ = 44
GJ = 11
NG = 4
NH = 16
ASLOT = 691
NASLOT = 26
NS = 4
LA = 3
SCALE = 192.0 ** -0.5
EPS = 1e-6
FUSED = True
STRICT = True
DEBUG = {}


class Buf:
    __slots__ = ("name", "writer", "readers")

    def __init__(self, name=""):
        self.name = name
        self.writer = None
        self.readers = []


class V:
    __slots__ = ("ap", "bufs")

    def __init__(self, ap, bufs):
        self.ap = ap
        self.bufs = bufs


class Op:
    __slots__ = ("eng", "fn", "deps", "dma", "coll", "signals", "token", "prev", "short")


def _flat(vs):
    out = []
    for v in vs:
        if isinstance(v, Buf):
            out.append(v)
        elif isinstance(v, V):
            out.extend(v.bufs)
        else:
            out.extend(_flat(v))
    return out


class Sched:
    ENGS = ("pe", "act", "dve", "pool", "sp")

    def __init__(self):
        self.q = {e: [] for e in self.ENGS}

    def add(self, eng, fn, R=(), W=(), dma=False, coll=False, short=False):
        op = Op()
        op.short = short
        op.eng = eng
        op.fn = fn
        op.dma = dma or coll
        op.coll = coll
        op.signals = False
        op.token = None
        op.prev = None
        reads = _flat(R)
        writes = _flat(W)
        deps = {}
        for b in reads:
            if b.writer is not None:
                deps[id(b.writer)] = b.writer
        for b in writes:
            if b.writer is not None:
                deps[id(b.writer)] = b.writer
            for r in b.readers:
                deps[id(r)] = r
        op.deps = [d for d in deps.values() if d is not op and (d.dma or op.dma or d.eng != eng or d.short or op.short or (STRICT and eng != 'pe'))]
        for b in reads:
            if not op.dma:
                b.readers = [r for r in b.readers if r.dma or r.eng != eng or r.short or (STRICT and eng != 'pe')]
            b.readers.append(op)
        for b in writes:
            b.writer = op
            b.readers = []
        self.q[eng].append(op)
        return op

    def emit(self, nc, es):
        CH = 4000
        RR = 8
        for e in self.ENGS:
            for op in self.q[e]:
                for d in op.deps:
                    d.signals = True
        sems = {}

        def getsem(key):
            if key not in sems:
                sems[key] = es.enter_context(nc.semaphore("s%d" % len(sems)))
            return sems[key]

        ncoll = 0
        for e in self.ENGS:
            ccnt = 0
            dcnt = 0
            for op in self.q[e]:
                if op.coll:
                    op.token = (getsem(("coll", ncoll)), 1)
                    ncoll += 1
                elif op.dma:
                    s = getsem(("d" + e, dcnt % RR))
                    op.token = (s, 16 * (dcnt // RR + 1))
                    if dcnt >= RR:
                        op.prev = (s, 16 * (dcnt // RR))
                    dcnt += 1
                elif op.signals:
                    op.token = (getsem((e, ccnt // CH)), ccnt % CH + 1)
                    ccnt += 1
        self.nsems = len(sems)
        q = self.q

        def run(e, eng):
            waited = {}
            for op in q[e]:
                need = [d.token for d in op.deps]
                if op.prev is not None:
                    need.append(op.prev)
                for (s, v) in need:
                    if waited.get(id(s), 0) < v:
                        eng.wait_ge(s, v)
                        waited[id(s)] = v
                ins = op.fn(eng)
                if op.token is not None:
                    if op.coll:
                        ins.then_inc(op.token[0])
                    elif op.dma:
                        ins.then_inc(op.token[0], 16)
                    else:
                        ins.then_inc(op.token[0], 1)
            for op in q[e]:
                if op.dma and waited.get(id(op.token[0]), 0) < op.token[1]:
                    eng.wait_ge(op.token[0], op.token[1])
                    waited[id(op.token[0])] = op.token[1]

        with nc.Block() as block:
            @block.tensor
            def _(t):
                run("pe", t)

            @block.scalar
            def _(s):
                run("act", s)

            @block.vector
            def _(v):
                run("dve", v)

            @block.gpsimd
            def _(g):
                run("pool", g)

            @block.sync
            def _(sy):
                run("sp", sy)


def build(layer_ids, final_norm, ncores=8):
    RGROUPS = [[2 * i, 2 * i + 1] for i in range(ncores // 2)]
    nc = bass.Bass("TRN2", target_bir_lowering=False)
    es = ExitStack()

    def din(name, shape, dt=F32):
        return nc.dram_tensor(name, list(shape), dt, kind="ExternalInput").ap()

    hin = din("hin", [D, NT])
    cosd = din("cosd", [64, NT])
    sind = din("sind", [64, NT])
    pmd = din("pm", [128, 4])
    gaind = din("gains", [128, 9 * 16])
    mgd = din("mg", [128, 16])
    lrpd = din("lrp", [128, 2 * 16 * 8])
    hout = nc.dram_tensor("hout", [D, NT], F32, kind="ExternalOutput").ap()
    wd = {}
    for l in layer_ids:
        j = l // 2
        wd[("wgu", l)] = din("wgu%d" % l, [NJ, 128, 4096])
        wd[("wdn", l)] = din("wdn%d" % l, [32, 128, 2816])
        if l % 2 == 0:
            wd[("win", j)] = din("win%d" % j, [9, 128, 2048])
            wd[("whd", j)] = din("whd%d" % j, [NH, 128, 2048])
            wd[("wo", j)] = din("wo%d" % j, [16, 128, 2048])
        else:
            wd[("lin", j)] = din("lin%d" % j, [16, 128, 4096])
            wd[("lgw", j)] = din("lgw%d" % j, [1, 128, 4096])
            wd[("lwo", j)] = din("lwo%d" % j, [16, 128, 2048])

    RES = nc.dram_tensor("res", [D, NT], F32)
    CQD = nc.dram_tensor("cqd", [512, NT], BF16)
    LATO = nc.dram_tensor("lato", [640, NT], BF16)
    LATS = [[nc.dram_tensor("lats%d_%d" % (i, k), [128, NFR], BF16) for k in range(5)] for i in range(2)]
    LATR = [[nc.dram_tensor("latr%d_%d" % (i, k), [256, NFR], BF16) for k in range(5)] for i in range(2)]
    LATP = [nc.dram_tensor("latp%d" % i, [640, NFR], BF16) for i in range(2)]
    OTD = nc.dram_tensor("otd", [D, NT], BF16)
    G0D = nc.dram_tensor("g0d", [D, NT], F32)
    G1D = nc.dram_tensor("g1d", [D, NT], F32)
    HSD = [nc.dram_tensor("hsd%d" % i, [128, 1024], F32) for i in range(2)]
    HRD = [nc.dram_tensor("hrd%d" % i, [256, 1024], F32) for i in range(2)]
    SSD = [nc.dram_tensor("ssd%d" % i, [128, 16], F32) for i in range(2)]
    SRD = [nc.dram_tensor("srd%d" % i, [256, 16], F32) for i in range(2)]

    def sb(name, shape, dt):
        return es.enter_context(nc.sbuf_tensor(name, list(shape), dt))

    WS = sb("ws", [128, NS, 4096], BF16)
    HT = sb("ht", [128, KC, TILE], F32)
    HNX = sb("hnx", [128, KC, TILE], BF16)
    AT = sb("at", [128, GJ, TILE], BF16)
    SQ = sb("sq", [128, 2, TILE], BF16)
    RS = sb("rs", [128, 2, TILE], F32)
    SIL = sb("sil", [128, 2, TILE], F32)
    ONES = sb("ones", [128, 128], BF16)
    ZEROS = sb("zeros", [128, TILE], F32)
    CONST = sb("const", [128, 4], F32)
    GAINS = sb("gainsb", [128, 9, 16], F32)
    PM = sb("pmb", [128, 4], F32)
    MG = sb("mgb", [128, 2, 8], F32)
    LRP = sb("lrpb", [128, 2, 16, 8], F32)
    LS = sb("lsb", [128, 2, 3, 16], F32)
    XTAIL = sb("xtail", [128, 16, 3], F32)
    CAR = sb("car", [128, 6, 16], F32)
    HXR = sb("hxr", [128, 16, 3], F32)
    HH = sb("hh", [128, 16, 64], F32)
    HHN = sb("hhn", [128, 16, 64], BF16)
    ARENA = sb("arena", [128, NASLOT * ASLOT], F32)
    PS = es.enter_context(nc.psum_tensor("ps", [128, 8, 512], F32))

    def gen(S, Wst):
        B = {}

        def buf(name):
            if name not in B:
                B[name] = Buf(name)
            return B[name]

        PB = [buf("pb%d" % i) for i in range(8)]
        WSB = [buf("ws%d" % i) for i in range(NS)]
        AB = [buf("ar%d" % i) for i in range(NASLOT)]
        bHT, bHNX, bAT = buf("HT"), buf("HNX"), buf("AT")
        bSQ = [buf("sq0"), buf("sq1")]
        bRS = [buf("rs0"), buf("rs1")]
        bSIL = [buf("sil0"), buf("sil1")]
        bCONST = buf("const")
        bSM = buf("small")
        bXTAIL, bCAR, bHXR, bHH, bHHN = buf("xtail"), buf("car"), buf("hxr"), buf("hh"), buf("hhn")
        Wst.bind(S, WS, WSB)

        def af32(slot, n=1):
            return V(ARENA[:, slot * ASLOT:(slot + n) * ASLOT], AB[slot:slot + n])

        def abf(slot, n):
            return V(ARENA[:, slot * ASLOT:(slot + n) * ASLOT].bitcast(BF16), AB[slot:slot + n])

        def dtile(t3d, name, t, c0=None, c1=None):
            if c0 is None:
                c0, c1 = t * TILE, (t + 1) * TILE
            return V(t3d[:, :, c0:c1], [buf("%s_t%d" % (name, t))])

        def pkn(ap):
            return ap.rearrange("(k p) n -> p k n", p=128)

        S.add("dve", lambda e: e.memset(ONES[:, :], 1.0), W=[bCONST])
        S.add("dve", lambda e: e.memset(ZEROS[:, :], 0.0), W=[bCONST])
        S.add("dve", lambda e: e.memset(CONST[:, 0:1], EPS), W=[bCONST])
        S.add("dve", lambda e: e.memset(CONST[:, 1:2], 1.0), W=[bCONST])
        S.add("dve", lambda e: e.memset(CONST[:, 2:3], 0.0), W=[bCONST])
        S.add("sp", lambda e: e.dma_start(out=GAINS[:, :, :].rearrange("p a b -> p (a b)"), in_=gaind[:, :]), W=[bSM], dma=True)
        S.add("sp", lambda e: e.dma_start(out=PM[:, :], in_=pmd[:, :]), W=[bSM], dma=True)
        S.add("sp", lambda e: e.dma_start(out=MG[:, :, :].rearrange("p a b -> p (a b)"), in_=mgd[:, :]), W=[bSM], dma=True)
        S.add("sp", lambda e: e.dma_start(out=LRP[:, :, :, :].rearrange("p a b c -> p (a b c)"), in_=lrpd[:, :]), W=[bSM], dma=True)
        bLS = buf("ls")
        for jj in range(2):
            S.add("act", lambda e, jj=jj: e.activation(out=LS[:, jj, 2, :], in_=LRP[:, jj, :, 7], func=AF.Exp, scale=-1.0), R=[bSM], W=[bLS], short=True)
            S.add("act", lambda e, jj=jj: e.activation(out=LS[:, jj, 2, :], in_=LS[:, jj, 2, :], func=AF.Ln, bias=CONST[:, 1:2], scale=1.0), R=[bCONST], W=[bLS], short=True)
            S.add("dve", lambda e, jj=jj: e.tensor_scalar(LS[:, jj, 0, :], LS[:, jj, 2, :], -8.0, 0.0, ALU.mult, ALU.add), R=[bLS], W=[bLS], short=True)
            S.add("dve", lambda e, jj=jj: e.tensor_scalar(LS[:, jj, 1, :], LS[:, jj, 2, :], 4.0, 0.0, ALU.mult, ALU.add), R=[bLS], W=[bLS], short=True)

        def mm(out_ap, lhsT, rhs, start, stop, R, W):
            S.add("pe", lambda e: e.matmul(out_ap, lhsT, rhs, start=start, stop=stop), R=R, W=W)

        def segsof(w):
            if w <= 512:
                return [(0, w)]
            h = w // 2
            return [(0, h), (h, w)]

        def rmsnorm(src, srcb, w, nk, inv_n, gain_ap, dst, dstb, banks, rsv, rsb, inplace_f32=False):
            sh = w < 128
            sg = segsof(w)
            for k in range(nk):
                sl = k % 2
                S.add("act", lambda e, k=k, sl=sl: e.activation(out=SQ[:, sl, 0:w], in_=src[:, k, 0:w], func=AF.Square), R=[srcb], W=[bSQ[sl]], short=sh)
                for si, (a, b) in enumerate(sg):
                    mm(PS[:, banks[si], 0:b - a], ONES[:, :], SQ[:, sl, a:b], k == 0, k == nk - 1, [bCONST, bSQ[sl]], [PB[banks[si]]])
            for si, (a, b) in enumerate(sg):
                S.add("act", lambda e, si=si, a=a, b=b: e.activation(out=rsv[:, a:b], in_=PS[:, banks[si], 0:b - a], func=AF.Sqrt, bias=CONST[:, 0:1], scale=inv_n), R=[PB[banks[si]], bCONST], W=[rsb], short=sh)
            S.add("dve", lambda e: e.reciprocal(rsv[:, 0:w], rsv[:, 0:w]), R=[rsb], W=[rsb], short=sh)
            for k in range(nk):
                S.add("dve", lambda e, k=k: e.scalar_tensor_tensor(dst[:, k, 0:w], src[:, k, 0:w], gain_ap[:, k:k + 1], rsv[:, 0:w], ALU.mult, ALU.mult), R=[srcb, rsb, bSM], W=[dstb], short=sh)

        def seg3(ap2d):
            return ap2d.rearrange("p (s c) -> p s c", s=2)

        def ffn(l):
            for g in range(NG):
                for jj in range(GJ):
                    j = g * GJ + jj
                    sl = Wst.get(("wgu", l, j), wd[("wgu", l)][j], 4096)
                    wv = WS[:, sl, :].rearrange("p (a k m) -> p a k m", a=2, k=16)
                    for half, b0 in ((0, 0), (1, 2)):
                        for k in range(KC):
                            for si, (a, b) in enumerate(SEGS):
                                mm(PS[:, b0 + si, 0:SEG], wv[:, half, k, :], HNX[:, k, a:b], k == 0, k == KC - 1, [WSB[sl], bHNX], [PB[b0 + si]])
                    ss = jj % 2
                    S.add("act", lambda e, ss=ss: e.activation(out=seg3(SIL[:, ss, :]), in_=PS[:, 0:2, 0:SEG], func=AF.Silu), R=[PB[0], PB[1]], W=[bSIL[ss]])
                    S.add("dve", lambda e, ss=ss, jj=jj: e.tensor_tensor(seg3(AT[:, jj, :]), seg3(SIL[:, ss, :]), PS[:, 2:4, 0:SEG], ALU.mult), R=[bSIL[ss], PB[2], PB[3]], W=[bAT])
                for ocp in range(8):
                    sl = Wst.get(("wdn", l, g, ocp), wd[("wdn", l)][g * 8 + ocp], 2816)
                    wv = WS[:, sl, 0:2816].rearrange("p (a j m) -> p a j m", a=2, j=GJ)
                    for o2 in range(2):
                        oc = ocp * 2 + o2
                        b0 = 4 + 2 * (oc % 2)
                        for jj in range(GJ):
                            for si, (a, b) in enumerate(SEGS):
                                mm(PS[:, b0 + si, 0:SEG], wv[:, o2, jj, :], AT[:, jj, a:b], jj == 0, jj == GJ - 1, [WSB[sl], bAT], [PB[b0 + si]])
                        S.add("dve", lambda e, oc=oc, b0=b0: e.tensor_tensor(seg3(HT[:, oc, :]), seg3(HT[:, oc, :]), PS[:, b0:b0 + 2, 0:SEG], ALU.add), R=[bHT, PB[b0], PB[b0 + 1]], W=[bHT])

        def proj_residual(key, blocks):
            for oc in range(16):
                sl = Wst.get((key, oc), blocks[oc], 2048)
                wv = WS[:, sl, 0:2048].rearrange("p (k m) -> p k m", k=16)
                b0 = 4 + 2 * (oc % 2)
                for k in range(KC):
                    for si, (a, b) in enumerate(SEGS):
                        mm(PS[:, b0 + si, 0:SEG], wv[:, k, :], HNX[:, k, a:b], k == 0, k == KC - 1, [WSB[sl], bHNX], [PB[b0 + si]])
                S.add("dve", lambda e, oc=oc, b0=b0: e.tensor_tensor(seg3(HT[:, oc, :]), seg3(HT[:, oc, :]), PS[:, b0:b0 + 2, 0:SEG], ALU.add), R=[bHT, PB[b0], PB[b0 + 1]], W=[bHT])

        def tail_tile(l, t, hdst3, hdst_name, is_final):
            rmsnorm(HT, bHT, TILE, KC, 1.0 / D, GAINS[:, 4 + l, :], HNX, bHNX, (0, 1), RS[:, 0, :], bRS[0])
            ffn(l)
            if is_final:
                rmsnorm(HT, bHT, TILE, KC, 1.0 / D, GAINS[:, 8, :], HT, bHT, (0, 1), RS[:, 0, :], bRS[0])
            dv = dtile(hdst3, hdst_name, t)
            S.add("sp", lambda e: e.dma_start(out=dv.ap, in_=HT[:, :, :]), R=[bHT], W=[dv], dma=True)

        def mla_layer(l, hsrc3, hsrc_name, hdst3, hdst_name, is_final):
            j = l // 2
            lats, latr, latp = LATS[j], LATR[j], LATP[j]
            cq3 = pkn(CQD[:, :])
            lato3 = pkn(LATO[:, :])
            for t in range(NTILES):
                sv = dtile(hsrc3, hsrc_name, t)
                S.add("sp", lambda e, sv=sv: e.dma_start(out=HT[:, :, :], in_=sv.ap), R=[sv], W=[bHT], dma=True)
                c0, c1 = t * TILE, (t + 1) * TILE
                S.add("sp", lambda e, c0=c0, c1=c1: e.dma_start(out=SIL[0:64, 0, :], in_=cosd[:, c0:c1]), W=[bSIL[0]], dma=True)
                S.add("sp", lambda e, c0=c0, c1=c1: e.dma_start(out=SIL[0:64, 1, :], in_=sind[:, c0:c1]), W=[bSIL[1]], dma=True)
                rmsnorm(HT, bHT, TILE, KC, 1.0 / D, GAINS[:, l, :], HNX, bHNX, (0, 1), RS[:, 0, :], bRS[0])
                CQ = [af32(i) for i in range(4)]
                CKV = [af32(4 + i) for i in range(4)]
                CQN = abf(8, 2)
                LATT = abf(10, 3)
                TM1, TM2 = af32(13), af32(14)
                RSQ = af32(15)
                cqn3 = CQN.ap[:, 0:4 * TILE].rearrange("p (k n) -> p k n", k=4)
                latt3 = LATT.ap[:, 0:5 * TILE].rearrange("p (k n) -> p k n", k=5)
                for blk in range(8):
                    sl = Wst.get(("win", j, blk), wd[("win", j)][blk], 2048)
                    wv = WS[:, sl, 0:2048].rearrange("p (k m) -> p k m", k=16)
                    b0 = 4 + 2 * (blk % 2)
                    for k in range(KC):
                        for si, (a, b) in enumerate(SEGS):
                            mm(PS[:, b0 + si, 0:SEG], wv[:, k, :], HNX[:, k, a:b], k == 0, k == KC - 1, [WSB[sl], bHNX], [PB[b0 + si]])
                    dstv = CQ[blk] if blk < 4 else CKV[blk - 4]
                    S.add("act", lambda e, dstv=dstv, b0=b0: e.activation(out=seg3(dstv.ap[:, 0:TILE]), in_=PS[:, b0:b0 + 2, 0:SEG], func=AF.Copy), R=[PB[b0], PB[b0 + 1]], W=[dstv])
                sl = Wst.get(("win", j, 8), wd[("win", j)][8], 2048)
                wv = WS[:, sl, 0:2048].rearrange("p (k m) -> p k m", k=16)
                for which, b0 in ((0, 0), (1, 2)):
                    for k in range(KC):
                        for si, (a, b) in enumerate(SEGS):
                            mm(PS[0:64, b0 + si, 0:SEG], wv[:, k, which * 64:(which + 1) * 64], HNX[:, k, a:b], k == 0, k == KC - 1, [WSB[sl], bHNX], [PB[b0 + si]])
                S.add("dve", lambda e, TM1=TM1: e.tensor_tensor(seg3(TM1.ap[0:64, 0:TILE]), PS[0:64, 0:2, 0:SEG], seg3(SIL[0:64, 0, :]), ALU.mult), R=[PB[0], PB[1], bSIL[0]], W=[TM1])
                S.add("dve", lambda e, TM2=TM2: e.tensor_tensor(seg3(TM2.ap[0:64, 0:TILE]), PS[0:64, 2:4, 0:SEG], seg3(SIL[0:64, 1, :]), ALU.mult), R=[PB[2], PB[3], bSIL[1]], W=[TM2])
                S.add("dve", lambda e, TM1=TM1, TM2=TM2, latt3=latt3: e.tensor_tensor(latt3[0:64, 4, :], TM1.ap[0:64, 0:TILE], TM2.ap[0:64, 0:TILE], ALU.add), R=[TM1, TM2], W=[LATT])
                for which, (srcs, dst3, goff) in enumerate(((CQ, cqn3, 0), (CKV, latt3, 4))):
                    for k in range(4):
                        sl2 = k % 2
                        S.add("act", lambda e, k=k, sl2=sl2, srcs=srcs: e.activation(out=SQ[:, sl2, :], in_=srcs[k].ap[:, 0:TILE], func=AF.Square), R=[srcs[k]], W=[bSQ[sl2]])
                        for si, (a, b) in enumerate(SEGS):
                            mm(PS[:, si, 0:SEG], ONES[:, :], SQ[:, sl2, a:b], k == 0, k == 3, [bCONST, bSQ[sl2]], [PB[si]])
                    S.add("act", lambda e, RSQ=RSQ: e.activation(out=seg3(RSQ.ap[:, 0:TILE]), in_=PS[:, 0:2, 0:SEG], func=AF.Sqrt, bias=CONST[:, 0:1], scale=1.0 / 512), R=[PB[0], PB[1], bCONST], W=[RSQ])
                    S.add("dve", lambda e, RSQ=RSQ: e.reciprocal(RSQ.ap[:, 0:TILE], RSQ.ap[:, 0:TILE]), R=[RSQ], W=[RSQ])
                    dstV = CQN if which == 0 else LATT
                    for k in range(4):
                        S.add("dve", lambda e, k=k, srcs=srcs, dst3=dst3, goff=goff, RSQ=RSQ: e.scalar_tensor_tensor(dst3[:, k, :], srcs[k].ap[:, 0:TILE], MG[:, j, goff + k:goff + k + 1], RSQ.ap[:, 0:TILE], ALU.mult, ALU.mult), R=[srcs[k], RSQ, bSM], W=[dstV])
                cqv = dtile(cq3, "cqd", t)
                S.add("sp", lambda e, cqv=cqv, cqn3=cqn3: e.dma_start(out=cqv.ap, in_=cqn3), R=[CQN], W=[cqv], dma=True)
                lov = dtile(lato3, "lato", t)
                S.add("sp", lambda e, lov=lov, latt3=latt3: e.dma_start(out=lov.ap, in_=latt3), R=[LATT], W=[lov], dma=True)
                f0 = max(c0, NMETA)
                for k in range(5):
                    lsv = V(lats[k][:, f0 - NMETA:c1 - NMETA], [buf("lats%d_%d_t%d" % (j, k, t))])
                    S.add("sp", lambda e, lsv=lsv, latt3=latt3, f0=f0, c0=c0, k=k: e.dma_start(out=lsv.ap, in_=latt3[:, k, f0 - c0:TILE]), R=[LATT], W=[lsv], dma=True)
            if DEBUG.get('stop') == 'p1':
                return
            blr = buf("latp%d" % j)
            for k in range(5):
                brk = buf("latr%d_%d" % (j, k))
                S.add("pool", lambda e, k=k: e.collective_compute("AllGather", ALU.bypass, replica_groups=RGROUPS,
                                                             ins=[lats[k].ap().opt()], outs=[latr[k].ap().opt()]),
                      R=[buf("lats%d_%d_t%d" % (j, k, t)) for t in range(NTILES)], W=[brk], coll=True)
                bpk = buf("latp%d_%d" % (j, k))
                S.add("sp", lambda e, k=k: e.dma_start(out=latp[k * 128:(k + 1) * 128, :], in_=latr[k][0:128, :]), R=[brk], W=[bpk], dma=True)
            allp = [buf("latp%d_%d" % (j, k)) for k in range(5)]
            latr3 = pkn(latp[:, :])
            if DEBUG.get('stop') == 'xch':
                return
            CQNin = abf(0, 2)
            LATin = abf(2, 2)
            QN, QR = abf(4, 2), abf(6, 2)
            KNo, KNp = abf(8, 2), abf(10, 2)
            Vo, Vp = abf(12, 2), abf(14, 2)
            KRo, KRp = abf(16, 2), abf(18, 2)
            PTs = [V(ARENA[:, 20 * ASLOT:22 * ASLOT].bitcast(BF16)[:, i * 512:(i + 1) * 512], [buf("pt%d" % i), AB[20], AB[21]]) for i in range(4)]
            for p_ in PTs:
                p_.bufs = [p_.bufs[0]]
            OO = abf(22, 1)
            RD = af32(23)
            HNXf = HNX[:, :, :].rearrange("p k n -> p (k n)")
            ATf = AT[:, :, :].rearrange("p j n -> p (j n)").bitcast(F32)
            CQI = [V(HNXf[:, b_ * 2048:(b_ + 1) * 2048].rearrange("p (k n) -> p k n", k=4), [buf("cqin%d" % b_)]) for b_ in range(2)]
            LTI = [V(HNXf[:, 4096 + b_ * 2048:4096 + (b_ + 1) * 2048].rearrange("p (k n) -> p k n", k=4), [buf("latin%d" % b_)]) for b_ in range(2)]
            COSB = [V(ATf[0:64, b_ * 512:(b_ + 1) * 512], [buf("cosb%d" % b_)]) for b_ in range(2)]
            SINB = [V(ATf[0:64, 1024 + b_ * 512:1024 + (b_ + 1) * 512], [buf("sinb%d" % b_)]) for b_ in range(2)]
            S.add("dve", lambda e: e.memset(HNXf[0:1, 0:2], 0.0), W=[bHNX] + CQI + LTI)
            S.add("dve", lambda e: e.memset(ATf[0:1, 0:1], 0.0), W=[bAT] + COSB + SINB)
            gctr = [0]
            vo3 = Vo.ap[:, 0:17 * 128].rearrange("p (b d) -> p b d", b=17)
            vp3 = Vp.ap[:, 0:16 * 128].rearrange("p (b d) -> p b d", b=16)
            allat = [buf("lato_t%d" % t) for t in range(NTILES)]
            allcq = [buf("cqd_t%d" % t) for t in range(NTILES)]
            S.add("sp", lambda e: e.dma_start(out=KRo.ap[0:64, 0:NT], in_=LATO[512:576, :]), R=allat, W=[KRo, AB[20], AB[21]], dma=True)
            S.add("sp", lambda e: e.dma_start(out=KRp.ap[0:64, 0:NFR], in_=latp[512:576, :]), R=allp, W=[KRp], dma=True)
            QG = [(0, 128)] + [(NMETA + 512 * g, NMETA + 512 * (g + 1)) for g in range(4)]
            otd3 = OTD[:, :]
            for h in range(DEBUG.get('nh', NH)):
                sl = Wst.get(("whd", j, h), wd[("whd", j)][h], 2048)
                wq = WS[:, sl, 0:1024].rearrange("p (k m) -> p k m", k=4)
                wkv = WS[:, sl, 1024:2048].rearrange("p (k m) -> p k m", k=4)
                wb = WSB[sl]
                for gi, (c0, c1) in enumerate(QG):
                    w = c1 - c0
                    pb_ = gctr[0] % 2
                    gctr[0] += 1
                    CQNin, LATin, COSv, SINv = CQI[pb_], LTI[pb_], COSB[pb_], SINB[pb_]
                    cqin3, latin3 = CQNin.ap, LATin.ap
                    S.add("sp", lambda e, c0=c0, c1=c1, w=w, cqin3=cqin3: e.dma_start(out=cqin3[:, :, 0:w], in_=cq3[:, :, c0:c1]), R=allcq, W=[CQNin], dma=True)
                    S.add("sp", lambda e, c0=c0, c1=c1, w=w, COSv=COSv: e.dma_start(out=COSv.ap[:, 0:w], in_=cosd[:, c0:c1]), W=[COSv], dma=True)
                    S.add("sp", lambda e, c0=c0, c1=c1, w=w, SINv=SINv: e.dma_start(out=SINv.ap[:, 0:w], in_=sind[:, c0:c1]), W=[SINv], dma=True)
                    wl = 128 if gi == 0 else w
                    S.add("sp", lambda e, c0=c0, wl=wl, latin3=latin3: e.dma_start(out=latin3[:, 0:4, 0:wl], in_=lato3[:, 0:4, c0:c0 + wl]), R=allat, W=[LATin], dma=True)
                    for k in range(4):
                        mm(PS[:, 5, 0:w], wq[:, k, 0:128], cqin3[:, k, 0:w], k == 0, k == 3, [wb, CQNin], [PB[5]])
                    S.add("act", lambda e, c0=c0, c1=c1, w=w: e.activation(out=QN.ap[:, c0:c1], in_=PS[:, 5, 0:w], func=AF.Copy), R=[PB[5]], W=[QN])
                    for k in range(4):
                        mm(PS[0:64, 6, 0:w], wq[:, k, 128:192], cqin3[:, k, 0:w], k == 0, k == 3, [wb, CQNin], [PB[6]])
                    for k in range(4):
                        mm(PS[0:64, 7, 0:w], wq[:, k, 192:256], cqin3[:, k, 0:w], k == 0, k == 3, [wb, CQNin], [PB[7]])
                    S.add("dve", lambda e, w=w, COSv=COSv: e.tensor_tensor(SIL[0:64, 1, 0:w], PS[0:64, 6, 0:w], COSv.ap[:, 0:w], ALU.mult), R=[PB[6], COSv], W=[bSIL[1]])
                    S.add("dve", lambda e, w=w, SINv=SINv: e.tensor_tensor(RS[0:64, 0, 0:w], PS[0:64, 7, 0:w], SINv.ap[:, 0:w], ALU.mult), R=[PB[7], SINv], W=[bRS[0]])
                    S.add("dve", lambda e, c0=c0, c1=c1, w=w: e.tensor_tensor(QR.ap[0:64, c0:c1], SIL[0:64, 1, 0:w], RS[0:64, 0, 0:w], ALU.add), R=[bSIL[1], bRS[0]], W=[QR])
                    for k in range(4):
                        mm(PS[:, 5, 0:w], wkv[:, k, 0:128], latin3[:, k, 0:w], k == 0, k == 3, [wb, LATin], [PB[5]])
                    S.add("act", lambda e, c0=c0, c1=c1, w=w: e.activation(out=KNo.ap[:, c0:c1], in_=PS[:, 5, 0:w], func=AF.Copy), R=[PB[5]], W=[KNo])
                    if gi == 0:
                        for k in range(4):
                            mm(PS[:, 6, 0:128], latin3[:, k, 0:128], wkv[:, k, 128:256], k == 0, k == 3, [wb, LATin], [PB[6]])
                        S.add("dve", lambda e: e.tensor_copy(vo3[:, 0, :], PS[:, 6, 0:128]), R=[PB[6]], W=[Vo])
                    else:
                        for bb in range(4):
                            for k in range(4):
                                mm(PS[:, 6, bb * 128:(bb + 1) * 128], latin3[:, k, bb * 128:(bb + 1) * 128], wkv[:, k, 128:256], k == 0, k == 3, [wb, LATin], [PB[6]])
                        vb0 = 1 + 4 * (gi - 1)
                        S.add("dve", lambda e, vb0=vb0: e.tensor_copy(vo3[:, vb0:vb0 + 4, :].rearrange("p b d -> p (b d)"), PS[:, 6, :]), R=[PB[6]], W=[Vo])
                for g in range(4):
                    c0, c1 = 512 * g, 512 * (g + 1)
                    pb_ = gctr[0] % 2
                    gctr[0] += 1
                    LATin = LTI[pb_]
                    latin3 = LATin.ap
                    S.add("sp", lambda e, c0=c0, c1=c1, latin3=latin3: e.dma_start(out=latin3[:, 0:4, :], in_=latr3[:, 0:4, c0:c1]), R=allp, W=[LATin], dma=True)
                    for k in range(4):
                        mm(PS[:, 5, :], wkv[:, k, 0:128], latin3[:, k, :], k == 0, k == 3, [wb, LATin], [PB[5]])
                    S.add("act", lambda e, c0=c0, c1=c1: e.activation(out=KNp.ap[:, c0:c1], in_=PS[:, 5, :], func=AF.Copy), R=[PB[5]], W=[KNp])
                    for bb in range(4):
                        for k in range(4):
                            mm(PS[:, 6, bb * 128:(bb + 1) * 128], latin3[:, k, bb * 128:(bb + 1) * 128], wkv[:, k, 128:256], k == 0, k == 3, [wb, LATin], [PB[6]])
                    S.add("dve", lambda e, g=g: e.tensor_copy(vp3[:, 4 * g:4 * g + 4, :].rearrange("p b d -> p (b d)"), PS[:, 6, :]), R=[PB[6]], W=[Vp])
                for gi, (c0, c1) in enumerate(QG):
                    w = c1 - c0
                    if gi not in DEBUG.get('groups', range(5)):
                        continue
                    kbl = []
                    if gi == 0:
                        kbl.append((KNo, KRo, vo3, Vo, 0, 128, 0, 0, False, 'meta'))
                    else:
                        g = gi - 1
                        for pb in range(16):
                            kbl.append((KNp, KRp, vp3, Vp, pb * 128, 128, pb, 0, False, 'prev'))
                        kbl.append((KNo, KRo, vo3, Vo, 0, 128, 0, 0, False, 'meta'))
                        for i in range(4 * g + 4):
                            kbl.append((KNo, KRo, vo3, Vo, NMETA + 128 * i, 128, 1 + i, max(0, 128 * (i - 4 * g)), i >= 4 * g, None))
                    if DEBUG.get('kfilter'):
                        kf = DEBUG['kfilter']
                        kbl = [kb for kb in kbl if (('p' in kf and kb[9] == 'prev') or ('m' in kf and kb[9] == 'meta') or ('d' in kf and kb[8]) or ('o' in kf and (not kb[9]) and kb[5] == 128 and not kb[8]))]
                    if DEBUG.get('fullcols'):
                        kbl = [kb[:7] + (0,) + kb[8:] for kb in kbl]
                    if DEBUG.get('nodiag'):
                        kbl = [kb[:8] + (False,) + kb[9:] for kb in kbl]
                    nb = len(kbl)
                    ob, db = (3, 4) if gi % 2 == 0 else (5, 6)

                    def qk(bi):
                        KNv, KRv, v3, Vv, k0, kw, vb, cs, diag, prevb = kbl[bi]
                        sb_ = bi % 3
                        mm(PS[0:kw, sb_, cs:w], KNv.ap[:, k0:k0 + kw], QN.ap[:, c0 + cs:c1], True, False, [KNv, QN], [PB[sb_]])
                        mm(PS[0:kw, sb_, cs:w], KRv.ap[0:64, k0:k0 + kw], QR.ap[0:64, c0 + cs:c1], False, True, [KRv, QR], [PB[sb_]])
                        pt = PTs[bi % 4]
                        o_ = pt.ap[0:kw, cs:w]
                        i_ = PS[0:kw, sb_, cs:w]
                        if prevb:
                            bi_ = PM[0:kw, 2:3] if prevb == 'prev' else PM[0:kw, 3:4]
                            S.add("act", lambda e, o_=o_, i_=i_, bi_=bi_: e.activation(out=o_, in_=i_, func=AF.Exp, bias=bi_, scale=SCALE), R=[PB[sb_], bSM], W=[pt])
                        else:
                            bi_ = CONST[0:kw, 2:3]
                            S.add("act", lambda e, o_=o_, i_=i_, bi_=bi_: e.activation(out=o_, in_=i_, func=AF.Exp, bias=bi_, scale=SCALE), R=[PB[sb_], bCONST], W=[pt])
                            if diag:
                                z_ = pt.ap[64:128, cs:cs + 64]
                                S.add("act", lambda e, z_=z_: e.memzero(z_), W=[pt])

                    def pv(bi):
                        KNv, KRv, v3, Vv, k0, kw, vb, cs, diag, prevb = kbl[bi]
                        pt = PTs[bi % 4]
                        first = bi == 0
                        last = bi == nb - 1
                        if True:
                            mm(PS[:, ob, cs:w], v3[0:kw, vb, :], pt.ap[0:kw, cs:w], first, last, [Vv, pt], [PB[ob]])
                            mm(PS[:, db, cs:w], ONES[0:kw, :], pt.ap[0:kw, cs:w], first, last, [bCONST, pt], [PB[db]])
                        else:
                            mm(PS[:, 3, cs:w], v3[0:64, vb, :], pt.ap[0:64, cs:w], first, False, [Vv, pt], [PB[3]])
                            mm(PS[:, 4, cs:w], ONES[0:64, :], pt.ap[0:64, cs:w], first, False, [bCONST, pt], [PB[4]])
                            mm(PS[:, 3, cs + 64:w], v3[64:128, vb, :], pt.ap[64:128, cs + 64:w], False, last, [Vv, pt], [PB[3]])
                            mm(PS[:, 4, cs + 64:w], ONES[64:128, :], pt.ap[64:128, cs + 64:w], False, last, [bCONST, pt], [PB[4]])

                    qk(0)
                    if nb > 1:
                        qk(1)
                    for bi in range(nb):
                        if bi + 2 < nb:
                            qk(bi + 2)
                        pv(bi)
                    S.add("dve", lambda e, w=w, db=db: e.reciprocal(RD.ap[:, 0:w], PS[:, db, 0:w]), R=[PB[db]], W=[RD])
                    S.add("dve", lambda e, w=w, ob=ob: e.tensor_tensor(OO.ap[:, 0:w], PS[:, ob, 0:w], RD.ap[:, 0:w], ALU.mult), R=[PB[ob], RD], W=[OO])
                    ws_ = NMETA if gi == 0 else w
                    ov = V(otd3[h * 128:(h + 1) * 128, c0:c0 + ws_], [buf("otd_h%d_g%d" % (h, gi))])
                    S.add("sp", lambda e, ov=ov, ws_=ws_: e.dma_start(out=ov.ap, in_=OO.ap[:, 0:ws_]), R=[OO], W=[ov], dma=True)
            if DEBUG.get('stop') == 'p2':
                return
            for p_ in PTs:
                S.add("dve", lambda e, p_=p_: e.memset(p_.ap[0:1, 0:1], 0.0), W=[p_, AB[20], AB[21]])
            S.add("dve", lambda e: e.memset(HNXf[0:1, 0:2], 0.0), W=[bHNX] + CQI + LTI)
            S.add("dve", lambda e: e.memset(ATf[0:1, 0:1], 0.0), W=[bAT] + COSB + SINB)
            allot = [buf("otd_h%d_g%d" % (h, gi)) for h in range(NH) for gi in range(5)]
            ot3 = pkn(OTD[:, :])
            for t in range(NTILES):
                sv = dtile(hsrc3, hsrc_name, t)
                S.add("sp", lambda e, sv=sv: e.dma_start(out=HT[:, :, :], in_=sv.ap), R=[sv], W=[bHT], dma=True)
                c0, c1 = t * TILE, (t + 1) * TILE
                S.add("sp", lambda e, c0=c0, c1=c1: e.dma_start(out=HNX[:, :, :], in_=ot3[:, :, c0:c1]), R=allot, W=[bHNX], dma=True)
                proj_residual(("wo", j), wd[("wo", j)])
                tail_tile(l, t, hdst3, hdst_name, is_final)

        def lru_layer(l, hsrc3, hsrc_name, hsrc2d, hdst3, hdst_name, is_final):
            j = l // 2
            hsd, hrd, ssd, srd = HSD[j], HRD[j], SSD[j], SRD[j]
            bhs, bhr = buf("hsd%d" % j), buf("hrd%d" % j)
            lastsrc = buf("%s_t%d" % (hsrc_name, NTILES - 1))
            hsd2 = hsd[:, :].rearrange("a (b c) -> (a b) c", c=64)
            hrd2 = hrd[0:128, :].rearrange("a (b c) -> (a b) c", c=64)
            S.add("sp", lambda e: e.dma_start(out=hsd2[:, :], in_=hsrc2d[:, NT - 64:NT]), R=[lastsrc], W=[bhs], dma=True)
            S.add("pool", lambda e: e.collective_compute("AllGather", ALU.bypass, replica_groups=RGROUPS,
                                                         ins=[hsd.ap().opt()], outs=[hrd.ap().opt()]), R=[bhs], W=[bhr], coll=True)
            S.add("sp", lambda e: e.dma_start(out=HH[:, :, :], in_=pkn(hrd2)), R=[bhr], W=[bHH], dma=True)
            rmsnorm(HH, bHH, 64, KC, 1.0 / D, GAINS[:, l, :], HHN, bHHN, (0,), RS[:, 1, :], bRS[1])
            if DEBUG.get('lstop') == 'halo':
                return
            GWv = abf(22, 3)
            S.add("pool", lambda e: e.dma_start(out=GWv.ap[:, 0:4096], in_=wd[("lgw", j)][0]), W=[GWv], dma=True)
            gw4 = GWv.ap[:, 0:4096].rearrange("p (a c m) -> p a c m", a=2, c=16)
            g03 = pkn(G0D[:, :])
            g13 = pkn(G1D[:, :])
            CWv = LRP[:, j, :, :]
            S1 = LS[:, j, 0, :]
            H1 = LS[:, j, 1, :]
            for t in range(DEBUG.get('ltiles', NTILES)):
                sv = dtile(hsrc3, hsrc_name, t)
                S.add("sp", lambda e, sv=sv: e.dma_start(out=HT[:, :, :], in_=sv.ap), R=[sv], W=[bHT], dma=True)
                rmsnorm(HT, bHT, TILE, KC, 1.0 / D, GAINS[:, l, :], HNX, bHNX, (4, 5), RS[:, 0, :], bRS[0])
                c0 = t * TILE
                def lru_views(c):
                        st = (c % 2) * 10
                        XB, Y, XC, RG, IG, AA, TH, H0, AC, OM = [af32(st + i) for i in range(10)]
                        XCB = abf(20 + (c % 2), 1)
                        return XB, Y, XC, RG, IG, AA, TH, H0, AC, OM, XCB

                def stageA(c, t=t, c0=c0):
                        XB, Y, XC, RG, IG, AA, TH, H0, AC, OM, XCB = lru_views(c)
                        sl = Wst.get(("lin", j, c), wd[("lin", j)][c], 4096)
                        wv = WS[:, sl, :].rearrange("p (a k m) -> p a k m", a=2, k=16)
                        for half, b0 in ((0, 0), (1, 2)):
                            for k in range(KC):
                                for si, (a, b) in enumerate(SEGS):
                                    mm(PS[:, b0 + si, 0:SEG], wv[:, half, k, :], HNX[:, k, a:b], k == 0, k == KC - 1, [WSB[sl], bHNX], [PB[b0 + si]])
                        if t == 0 and not DEBUG.get('nohalo'):
                            for k in range(KC):
                                mm(PS[:, 3, 384:448], wv[:, 0, k, :], HHN[:, k, 0:64], k == 0, k == KC - 1, [WSB[sl], bHHN], [PB[3]])
                            S.add("act", lambda e, c=c: e.activation(out=HXR[:, c, :], in_=PS[:, 3, 445:448], func=AF.Copy, scale=PM[:, 1:2]), R=[PB[3], bSM], W=[bHXR], short=True)
                        S.add("act", lambda e, Y=Y: e.activation(out=seg3(Y.ap[:, 0:TILE]), in_=PS[:, 2:4, 0:SEG], func=AF.Gelu_apprx_tanh), R=[PB[2], PB[3]], W=[Y])
                        S.add("act", lambda e, XB=XB: e.activation(out=seg3(XB.ap[:, 3:3 + TILE]), in_=PS[:, 0:2, 0:SEG], func=AF.Copy), R=[PB[0], PB[1]], W=[XB])

                        def conv(a, b, XB=XB, XC=XC, c=c, sh=False):
                            S.add("dve", lambda e: e.tensor_scalar(XC.ap[:, a:b], XB.ap[:, a + 3:b + 3], CWv[:, c, 3:4], CWv[:, c, 4:5], ALU.mult, ALU.add), R=[XB, bSM], W=[XC], short=sh)
                            for tap in range(3):
                                S.add("dve", lambda e, tap=tap: e.scalar_tensor_tensor(XC.ap[:, a:b], XB.ap[:, a + tap:b + tap], CWv[:, c, tap:tap + 1], XC.ap[:, a:b], ALU.mult, ALU.add), R=[XB, XC, bSM], W=[XC], short=sh)
                        if t == 0:
                            S.add("dve", lambda e, XB=XB: e.memset(XB.ap[:, 0:3], 0.0), W=[XB], short=True)
                            conv(0, NMETA, sh=True)
                            S.add("dve", lambda e, XB=XB, c=c: e.scalar_tensor_tensor(XB.ap[:, NMETA:NMETA + 3], XB.ap[:, NMETA:NMETA + 3], PM[:, 0:1], HXR[:, c, :], ALU.mult, ALU.add), R=[XB, bHXR, bSM], W=[XB], short=True)
                            conv(NMETA, TILE)
                        else:
                            S.add("dve", lambda e, XB=XB, c=c: e.tensor_copy(XB.ap[:, 0:3], XTAIL[:, c, :]), R=[bXTAIL], W=[XB], short=True)
                            conv(0, TILE)
                        S.add("dve", lambda e, XB=XB, c=c: e.tensor_copy(XTAIL[:, c, :], XB.ap[:, TILE:TILE + 3]), R=[XB], W=[bXTAIL], short=True)
                        S.add("act", lambda e, XC=XC, XCB=XCB: e.activation(out=XCB.ap[:, 0:TILE], in_=XC.ap[:, 0:TILE], func=AF.Copy), R=[XC], W=[XCB])

                def stageB(c, t=t, c0=c0):
                        XB, Y, XC, RG, IG, AA, TH, H0, AC, OM, XCB = lru_views(c)
                        for gate, b0 in ((0, 4), (1, 6)):
                            for si, (a, b) in enumerate(SEGS):
                                mm(PS[:, b0 + si, 0:SEG], gw4[:, gate, c, :], XCB.ap[:, a:b], True, True, [GWv, XCB], [PB[b0 + si]])
                        S.add("act", lambda e, RG=RG, c=c: e.activation(out=seg3(RG.ap[:, 0:TILE]), in_=PS[:, 4:6, 0:SEG], func=AF.Sigmoid, bias=CWv[:, c, 5:6], scale=1.0), R=[PB[4], PB[5], bSM], W=[RG])
                        S.add("act", lambda e, IG=IG, c=c: e.activation(out=seg3(IG.ap[:, 0:TILE]), in_=PS[:, 6:8, 0:SEG], func=AF.Sigmoid, bias=CWv[:, c, 6:7], scale=1.0), R=[PB[6], PB[7], bSM], W=[IG])
                        S.add("act", lambda e, RG=RG, AA=AA, c=c: e.activation(out=AA.ap[:, 0:TILE], in_=RG.ap[:, 0:TILE], func=AF.Exp, scale=S1[:, c:c + 1]), R=[RG, bLS], W=[AA])
                        S.add("act", lambda e, RG=RG, TH=TH, c=c: e.activation(out=TH.ap[:, 0:TILE], in_=RG.ap[:, 0:TILE], func=AF.Tanh, scale=H1[:, c:c + 1]), R=[RG, bLS], W=[TH])
                        S.add("dve", lambda e, AA=AA, TH=TH, OM=OM: e.scalar_tensor_tensor(OM.ap[:, 0:TILE], AA.ap[:, 0:TILE], 1.0, TH.ap[:, 0:TILE], ALU.add, ALU.mult), R=[AA, TH], W=[OM])
                        S.add("dve", lambda e, AA=AA, OM=OM: e.tensor_scalar(AA.ap[:, 0:TILE], OM.ap[:, 0:TILE], -1.0, 1.0, ALU.mult, ALU.add), R=[OM], W=[AA])
                        S.add("dve", lambda e, AA=AA, TH=TH, OM=OM: e.scalar_tensor_tensor(TH.ap[:, 0:TILE], AA.ap[:, 0:TILE], 1.0, OM.ap[:, 0:TILE], ALU.add, ALU.mult), R=[AA, OM], W=[TH])
                        S.add("act", lambda e, TH=TH: e.activation(out=TH.ap[:, 0:TILE], in_=TH.ap[:, 0:TILE], func=AF.Sqrt), R=[TH], W=[TH])
                        S.add("dve", lambda e, IG=IG, XC=XC: e.tensor_tensor(IG.ap[:, 0:TILE], IG.ap[:, 0:TILE], XC.ap[:, 0:TILE], ALU.mult), R=[IG, XC], W=[IG])
                        S.add("dve", lambda e, IG=IG, TH=TH: e.tensor_tensor(IG.ap[:, 0:TILE], IG.ap[:, 0:TILE], TH.ap[:, 0:TILE], ALU.mult), R=[IG, TH], W=[IG])
                        if t == 0:
                            S.add("dve", lambda e, AA=AA, IG=IG, H0=H0: e.tensor_tensor_scan(H0.ap[:, 0:NMETA], AA.ap[:, 0:NMETA], IG.ap[:, 0:NMETA], 0.0, ALU.mult, ALU.add), R=[AA, IG], W=[H0], short=True)
                            S.add("dve", lambda e, AC=AC: e.memset(AC.ap[:, 0:NMETA], 0.0), W=[AC], short=True)
                            S.add("dve", lambda e, H0=H0, c=c: e.tensor_copy(CAR[:, 2, c:c + 1], H0.ap[:, NMETA - 1:NMETA]), R=[H0], W=[bCAR], short=True)
                            S.add("dve", lambda e, AA=AA, IG=IG, H0=H0: e.tensor_tensor_scan(H0.ap[:, NMETA:TILE], AA.ap[:, NMETA:TILE], IG.ap[:, NMETA:TILE], 0.0, ALU.mult, ALU.add), R=[AA, IG], W=[H0])
                            S.add("dve", lambda e, AA=AA, AC=AC: e.tensor_tensor_scan(AC.ap[:, NMETA:TILE], AA.ap[:, NMETA:TILE], ZEROS[:, NMETA:TILE], 1.0, ALU.mult, ALU.add), R=[AA, bCONST], W=[AC])
                        else:
                            S.add("dve", lambda e, AA=AA, IG=IG, H0=H0, c=c: e.tensor_tensor_scan(H0.ap[:, 0:TILE], AA.ap[:, 0:TILE], IG.ap[:, 0:TILE], CAR[:, 0, c:c + 1], ALU.mult, ALU.add), R=[AA, IG, bCAR], W=[H0])
                            S.add("dve", lambda e, AA=AA, AC=AC, c=c: e.tensor_tensor_scan(AC.ap[:, 0:TILE], AA.ap[:, 0:TILE], ZEROS[:, 0:TILE], CAR[:, 1, c:c + 1], ALU.mult, ALU.add), R=[AA, bCONST, bCAR], W=[AC])
                        S.add("dve", lambda e, H0=H0, c=c: e.tensor_copy(CAR[:, 0, c:c + 1], H0.ap[:, TILE - 1:TILE]), R=[H0], W=[bCAR], short=True)
                        S.add("dve", lambda e, AC=AC, c=c: e.tensor_copy(CAR[:, 1, c:c + 1], AC.ap[:, TILE - 1:TILE]), R=[AC], W=[bCAR], short=True)
                        S.add("pool", lambda e, H0=H0, Y=Y: e.tensor_tensor(H0.ap[:, 0:TILE], H0.ap[:, 0:TILE], Y.ap[:, 0:TILE], ALU.mult), R=[H0, Y], W=[H0])
                        S.add("pool", lambda e, AC=AC, Y=Y: e.tensor_tensor(AC.ap[:, 0:TILE], AC.ap[:, 0:TILE], Y.ap[:, 0:TILE], ALU.mult), R=[AC, Y], W=[AC])
                        g0v = V(g03[:, c, c0:c0 + TILE], [buf("g0_%d_%d" % (t, c))])
                        g1v = V(g13[:, c, c0:c0 + TILE], [buf("g1_%d_%d" % (t, c))])
                        S.add("sp", lambda e, g0v=g0v, H0=H0: e.dma_start(out=g0v.ap, in_=H0.ap[:, 0:TILE]), R=[H0], W=[g0v], dma=True)
                        S.add("sp", lambda e, g1v=g1v, AC=AC: e.dma_start(out=g1v.ap, in_=AC.ap[:, 0:TILE]), R=[AC], W=[g1v], dma=True)

                ncx = DEBUG.get('nchunks', 16)
                stageA(0)
                for c in range(ncx):
                    if c + 1 < ncx:
                        stageA(c + 1)
                    stageB(c)

            if DEBUG.get('lstop') == 'p1':
                return
            S.add("dve", lambda e: e.tensor_tensor(CAR[:, 4, :], CAR[:, 1, :], CAR[:, 2, :], ALU.mult), R=[bCAR], W=[bCAR], short=True)
            S.add("dve", lambda e: e.tensor_tensor(CAR[:, 4, :], CAR[:, 4, :], CAR[:, 0, :], ALU.add), R=[bCAR], W=[bCAR], short=True)
            bss, bsr = buf("ssd%d" % j), buf("srd%d" % j)
            S.add("sp", lambda e: e.dma_start(out=ssd[:, :], in_=CAR[:, 4, :]), R=[bCAR], W=[bss], dma=True)
            S.add("pool", lambda e: e.collective_compute("AllGather", ALU.bypass, replica_groups=RGROUPS,
                                                         ins=[ssd.ap().opt()], outs=[srd.ap().opt()]), R=[bss], W=[bsr], coll=True)
            S.add("sp", lambda e: e.dma_start(out=CAR[:, 3, :], in_=srd[0:128, :]), R=[bsr], W=[bCAR], dma=True)
            S.add("dve", lambda e: e.tensor_scalar(CAR[:, 5, :], CAR[:, 2, :], PM[:, 0:1], 0.0, ALU.mult, ALU.add), R=[bCAR, bSM], W=[bCAR], short=True)
            S.add("dve", lambda e: e.scalar_tensor_tensor(CAR[:, 5, :], CAR[:, 3, :], PM[:, 1:2], CAR[:, 5, :], ALU.mult, ALU.add), R=[bCAR, bSM], W=[bCAR], short=True)
            if DEBUG.get('lstop') == 'x2':
                S.add("sp", lambda e: e.dma_start(out=hout[0:128, 0:96], in_=CAR[:, :, :].rearrange("p a b -> p (a b)")), R=[bCAR], W=[buf("hout_dbg")], dma=True)
                return
            for t in range(NTILES):
                sv = dtile(hsrc3, hsrc_name, t)
                S.add("sp", lambda e, sv=sv: e.dma_start(out=HT[:, :, :], in_=sv.ap), R=[sv], W=[bHT], dma=True)
                c0 = t * TILE
                for c in range(16):
                    A0, A1 = af32(2 * (c % 4)), af32(2 * (c % 4) + 1)
                    S.add("sp", lambda e, A0=A0, c=c, c0=c0: e.dma_start(out=A0.ap[:, 0:TILE], in_=g03[:, c, c0:c0 + TILE]), R=[buf("g0_%d_%d" % (t, c))], W=[A0], dma=True)
                    S.add("sp", lambda e, A1=A1, c=c, c0=c0: e.dma_start(out=A1.ap[:, 0:TILE], in_=g13[:, c, c0:c0 + TILE]), R=[buf("g1_%d_%d" % (t, c))], W=[A1], dma=True)
                    S.add("dve", lambda e, A0=A0, A1=A1, c=c: e.scalar_tensor_tensor(HNX[:, c, :], A1.ap[:, 0:TILE], CAR[:, 5, c:c + 1], A0.ap[:, 0:TILE], ALU.mult, ALU.add), R=[A0, A1, bCAR], W=[bHNX])
                proj_residual(("lwo", j), wd[("lwo", j)])
                tail_tile(l, t, hdst3, hdst_name, is_final)

        n = len(layer_ids)
        for i, l in enumerate(layer_ids):
            src2d = hin if i == 0 else RES[:, :]
            src_name = "hin" if i == 0 else "res"
            dst2d = hout if i == n - 1 else RES[:, :]
            dst_name = "hout" if i == n - 1 else "res"
            fin = final_norm and i == n - 1
            if l % 2 == 0:
                mla_layer(l, pkn(src2d), src_name, pkn(dst2d), dst_name, fin)
            else:
                lru_layer(l, pkn(src2d), src_name, src2d, pkn(dst2d), dst_name, fin)

    w0 = WStream(None)
    gen(Sched(), w0)
    S = Sched()
    w1 = WStream(w0.reqs)
    gen(S, w1)
    S.emit(nc, es)
    es.close()
    return nc


class WStream:
    def __init__(self, plan):
        self.plan = plan
        self.reqs = []
        self.i = 0
        self.issued = 0

    def bind(self, S, WS, WSB):
        self.S, self.WS, self.WSB = S, WS, WSB

    def get(self, key, src, n):
        i = self.i
        self.i += 1
        if self.plan is None:
            self.reqs.append((key, src, n))
            return i % NS
        plan = self.plan
        assert plan[i][0] == key, (plan[i][0], key)
        while self.issued < min(len(plan), i + 1 + LA):
            k, s_ap, nn = plan[self.issued]
            slot = self.issued % NS
            WS = self.WS
            self.S.add("pool", lambda e, slot=slot, s_ap=s_ap, nn=nn: e.dma_start(out=WS[:, slot, 0:nn], in_=s_ap), W=[self.WSB[slot]], dma=True)
            self.issued += 1
        return i % NS


def _blk2(W, half, nblk):
    a = W[:, :half].reshape(16, 128, nblk, 128).transpose(2, 1, 0, 3)
    b = W[:, half:].reshape(16, 128, nblk, 128).transpose(2, 1, 0, 3)
    return np.ascontiguousarray(np.stack([a, b], axis=2).reshape(nblk, 128, 4096))


def _blk_sq(W):
    return np.ascontiguousarray(W.reshape(16, 128, 16, 128).transpose(2, 1, 0, 3).reshape(16, 128, 2048))


def _prep_weights(inp, layer_ids):
    out = {}
    for l in layer_ids:
        j = l // 2
        out["wgu%d" % l] = _blk2(inp["ffn_w_gu"][l], DFF, NJ)
        wdn = inp["ffn_w_down"][l].reshape(NG, GJ, 128, 8, 2, 128).transpose(0, 3, 2, 4, 1, 5)
        out["wdn%d" % l] = np.ascontiguousarray(wdn.reshape(32, 128, 2816))
        if l % 2 == 0:
            w_in = inp["mla_w_in"][j]
            w3 = w_in.reshape(16, 128, 1088)
            blks = [w3[:, :, b * 128:(b + 1) * 128] for b in range(8)]
            rope = np.concatenate([w3[:, :, 1024:1088], w3[:, :, 1056:1088], w3[:, :, 1024:1056]], axis=2)
            blks.append(rope)
            out["win%d" % j] = np.ascontiguousarray(np.stack(blks, 0).transpose(0, 2, 1, 3).reshape(9, 128, 2048))
            wq = inp["mla_w_uq"][j].reshape(4, 128, NH, 192)
            wqh = np.concatenate([wq[..., 0:192], wq[..., 160:192], wq[..., 128:160]], axis=3)
            wkv = inp["mla_w_ukv"][j].reshape(4, 128, NH, 256)
            hd = np.concatenate([wqh.transpose(2, 1, 0, 3).reshape(NH, 128, 1024), wkv.transpose(2, 1, 0, 3).reshape(NH, 128, 1024)], axis=2)
            out["whd%d" % j] = np.ascontiguousarray(hd)
            out["wo%d" % j] = _blk_sq(inp["mla_w_o"][j])
        else:
            out["lin%d" % j] = _blk2(inp["lru_w_in"][j], 2048, 16)
            ga = inp["lru_w_gate_a"][j].transpose(1, 0, 2)
            gx = inp["lru_w_gate_x"][j].transpose(1, 0, 2)
            out["lgw%d" % j] = np.ascontiguousarray(np.stack([ga, gx], axis=1).reshape(1, 128, 4096))
            out["lwo%d" % j] = _blk_sq(inp["lru_w_o"][j])
    return out


def _prep_small(inp):
    def pk(v):
        return v.reshape(16, 128).T
    gains = np.stack([pk(inp["norm_mix"][l]) for l in range(4)] + [pk(inp["norm_ffn"][l]) for l in range(4)] + [pk(inp["norm_final"])], axis=1)
    mg = np.zeros((128, 2, 8), np.float32)
    for j in range(2):
        mg[:, j, 0:4] = inp["mla_q_norm"][j].reshape(4, 128).T
        mg[:, j, 4:8] = inp["mla_kv_norm"][j].reshape(4, 128).T
    lrp = np.zeros((128, 2, 16, 8), np.float32)
    for j in range(2):
        lrp[:, j, :, 0:4] = inp["lru_conv_w"][j].reshape(4, 16, 128).transpose(2, 1, 0)
        lrp[:, j, :, 4] = pk(inp["lru_conv_b"][j])
        lrp[:, j, :, 5] = inp["lru_b_gate_a"][j].T
        lrp[:, j, :, 6] = inp["lru_b_gate_x"][j].T
        lrp[:, j, :, 7] = pk(inp["lru_lambda"][j])
    return {"gains": np.ascontiguousarray(gains.reshape(128, 144), np.float32),
            "mg": np.ascontiguousarray(mg.reshape(128, 16)),
            "lrp": np.ascontiguousarray(lrp.reshape(128, 256))}


def _rope_tables(pos):
    inv_freq = (np.float32(10000.0) ** (-np.arange(0, 64, 2, dtype=np.float32) / np.float32(64))).astype(np.float32)
    ang = (pos.astype(np.float32)[None, :] * inv_freq[:, None]).astype(np.float32)
    c, s = np.cos(ang).astype(np.float32), np.sin(ang).astype(np.float32)
    return np.ascontiguousarray(np.concatenate([c, c], 0)), np.ascontiguousarray(np.concatenate([-s, s], 0))


_NC_CACHE = {}


def _get_nc(layer_ids, final, ncores=8):
    key = (tuple(layer_ids), final, ncores)
    if key not in _NC_CACHE:
        _NC_CACHE[key] = build(list(layer_ids), final, ncores)
    return _NC_CACHE[key]


def _core_consts(ncores):
    per = []
    for core in range(ncores):
        half = core % 2
        pos = np.concatenate([np.arange(NMETA), NMETA + half * NFR + np.arange(NFR)])
        cosd, sind = _rope_tables(pos)
        pm = np.zeros((128, 4), np.float32)
        pm[:, 0] = 1.0 if half == 0 else 0.0
        pm[:, 1] = 0.0 if half == 0 else 1.0
        pm[:, 2] = -30000.0 if half == 0 else 0.0
        pm[NMETA:, 3] = -30000.0
        per.append({"cosd": cosd, "sind": sind, "pm": pm})
    return per


def run_layers(inp, hT_list, layer_ids, final, ncores=8):
    nc = _get_nc(layer_ids, final, ncores)
    small = _prep_small(inp)
    wts = _prep_weights(inp, layer_ids)
    consts = _core_consts(ncores)
    in_maps = []
    for core in range(ncores):
        m = {"hin": hT_list[core]}
        m.update(consts[core])
        m.update(small)
        m.update(wts)
        in_maps.append(m)
    res = run_bass_kernel_spmd(nc, in_maps, core_ids=list(range(ncores)))
    return [r["hout"] for r in res.results]


def make_hT(x, meta_tokens, nb):
    hT = []
    metaT = np.ascontiguousarray(meta_tokens.T)
    for b in range(nb):
        xT = x[b].T
        for half in range(2):
            hT.append(np.ascontiguousarray(np.concatenate([metaT, xT[:, half * NFR:(half + 1) * NFR]], axis=1), dtype=np.float32))
    return hT


def kernel(**inp):
    inp = {k: np.asarray(v) for k, v in inp.items()}
    x = inp["x"]
    nb = x.shape[0]
    hT = make_hT(x, inp["meta_tokens"], nb)
    if FUSED:
        outs = run_layers(inp, hT, [0, 1, 2, 3], True, 2 * nb)
    else:
        outs = hT
        for l in range(4):
            outs = run_layers(inp, outs, [l], l == 3, 2 * nb)
    y = np.empty((nb, 2 * NFR, D), np.float32)
    for b in range(nb):
        for half in range(2):
            y[b, half * NFR:(half + 1) * NFR, :] = outs[2 * b + half][:, NMETA:].T
    return y
```
